# Optimizing a Trainium2 kernel written in Bass

```python
import math
import jax, jax.numpy as jnp
from jax import lax
import numpy as np

D_MODEL = 1024
BATCH = 16
SEQ = 2048
DEPTH = 2

F_GROUPS = 4
F_GROUP_DIM = 64
F_WIDTH = F_GROUPS * F_GROUP_DIM
N_HEADS = 6
Q_LORA = 256
KV_LORA = 256
QK_NOPE = 128
QK_ROPE = 64
V_DIM = 128
QK_DIM = QK_NOPE + QK_ROPE
A_WIDTH = N_HEADS * V_DIM
MIX_WIDTH = F_WIDTH + A_WIDTH
IN_WIDTH = F_WIDTH + Q_LORA + KV_LORA + QK_ROPE
ROPE_BASE = 10000.0
Q_BLOCK = 128
MAX_POS_OFFSET = 1024
D_FF = 4 * D_MODEL
EPS = 1e-6

kernel_name = "hybrid_fnet_mla_encoder"


def rms_norm(x, g):
    xf = x.astype(jnp.float32)
    y = xf * lax.rsqrt(jnp.mean(xf * xf, axis=-1, keepdims=True) + EPS)
    return (y * g.astype(jnp.float32)).astype(x.dtype)


def rotary_tables(positions):
    half = QK_ROPE // 2
    inv_freq = ROPE_BASE ** (-jnp.arange(half, dtype=jnp.float32) / half)
    ang = positions.astype(jnp.float32)[..., None] * inv_freq
    return jnp.cos(ang)[:, :, None, :], jnp.sin(ang)[:, :, None, :]


def apply_rotary(x, cos, sin):
    xf = x.astype(jnp.float32)
    x1, x2 = jnp.split(xf, 2, axis=-1)
    out = jnp.concatenate([x1 * cos - x2 * sin, x2 * cos + x1 * sin], axis=-1)
    return out.astype(x.dtype)


def fourier_mixer(u, w_fourier):
    b, s, _ = u.shape
    ug = u.reshape(b, s, F_GROUPS, F_GROUP_DIM).astype(jnp.float32)
    fr = jnp.real(jnp.fft.fft2(ug, axes=(1, 3), norm='ortho'))
    y = jnp.einsum('bsgc,gcd->bsgd', fr, w_fourier.astype(jnp.float32))
    return y.reshape(b, s, F_WIDTH).astype(u.dtype)


def dense_attention(q, k, v):
    b, s, h, dk = q.shape
    dv = v.shape[-1]
    nblk = s // Q_BLOCK
    scale = 1.0 / math.sqrt(dk)
    kf = jnp.transpose(k, (0, 2, 1, 3)).astype(jnp.float32)
    vf = jnp.transpose(v, (0, 2, 1, 3)).astype(jnp.float32)
    qb = jnp.transpose(q, (0, 2, 1, 3)).reshape(b, h, nblk, Q_BLOCK, dk)
    qb = jnp.transpose(qb, (2, 0, 1, 3, 4))

    def one_block(qblk):
        scores = jnp.einsum('bhqd,bhkd->bhqk', qblk.astype(jnp.float32), kf) * scale
        probs = jax.nn.softmax(scores, axis=-1)
        return jnp.einsum('bhqk,bhkd->bhqd', probs, vf)

    o = lax.map(one_block, qb)
    o = jnp.transpose(o, (1, 0, 3, 2, 4)).reshape(b, s, h, dv)
    return o.astype(q.dtype)


def mla_mixer(c_q, c_kv, k_pe, cos, sin, q_a_g, w_q_up, kv_a_g, w_kv_up, q_norm_g, k_norm_g):
    b, s, _ = c_q.shape
    q = (rms_norm(c_q, q_a_g) @ w_q_up).reshape(b, s, N_HEADS, QK_DIM)
    kv = (rms_norm(c_kv, kv_a_g) @ w_kv_up).reshape(b, s, N_HEADS, QK_NOPE + V_DIM)
    k_nope, v = kv[..., :QK_NOPE], kv[..., QK_NOPE:]
    k_pe_h = jnp.broadcast_to(k_pe[:, :, None, :], (b, s, N_HEADS, QK_ROPE))
    k = jnp.concatenate([k_nope, k_pe_h], axis=-1)
    q = rms_norm(q, q_norm_g)
    k = rms_norm(k, k_norm_g)
    q = jnp.concatenate([q[..., :QK_NOPE], apply_rotary(q[..., QK_NOPE:], cos, sin)], axis=-1)
    k = jnp.concatenate([k[..., :QK_NOPE], apply_rotary(k[..., QK_NOPE:], cos, sin)], axis=-1)
    o = dense_attention(q, k, v)
    return o.reshape(b, s, A_WIDTH)


def setup_inputs(seed: int = 0) -> dict:
    key = jax.random.key(seed)
    ks = jax.random.split(key, 18)
    f32 = jnp.float32

    def w(k, shape, fan_in):
        return jax.random.normal(k, shape, f32) * (fan_in ** -0.5)

    def gain(k, shape):
        return 1.0 + 0.05 * jax.random.normal(k, shape, f32)

    x = jax.random.normal(ks[0], (BATCH, SEQ, D_MODEL), f32)
    offsets = jax.random.randint(ks[1], (BATCH, 1), 0, MAX_POS_OFFSET, dtype=jnp.int32)
    positions = (jnp.arange(SEQ, dtype=jnp.int32)[None, :] + offsets).astype(jnp.int32)
    return {
        'x': x,
        'positions': positions,
        'attn_norm_g': gain(ks[2], (DEPTH, D_MODEL)),
        'w_in': w(ks[3], (DEPTH, D_MODEL, IN_WIDTH), D_MODEL),
        'w_fourier': w(ks[4], (DEPTH, F_GROUPS, F_GROUP_DIM, F_GROUP_DIM), F_GROUP_DIM),
        'q_a_g': gain(ks[5], (DEPTH, Q_LORA)),
        'w_q_up': w(ks[6], (DEPTH, Q_LORA, N_HEADS * QK_DIM), Q_LORA),
        'kv_a_g': gain(ks[7], (DEPTH, KV_LORA)),
        'w_kv_up': w(ks[8], (DEPTH, KV_LORA, N_HEADS * (QK_NOPE + V_DIM)), KV_LORA),
        'q_norm_g': gain(ks[9], (DEPTH, QK_DIM)),
        'k_norm_g': gain(ks[10], (DEPTH, QK_DIM)),
        'fourier_out_g': gain(ks[11], (DEPTH, F_WIDTH)),
        'attn_out_g': gain(ks[12], (DEPTH, A_WIDTH)),
        'w_out': w(ks[13], (DEPTH, MIX_WIDTH, D_MODEL), MIX_WIDTH),
        'mlp_norm_g': gain(ks[14], (DEPTH, D_MODEL)),
        'w_mlp_in': w(ks[15], (DEPTH, D_MODEL, D_FF), D_MODEL),
        'w_mlp_out': w(ks[16], (DEPTH, D_FF, D_MODEL), D_FF),
    }


def reference(x, positions, attn_norm_g, w_in, w_fourier, q_a_g, w_q_up, kv_a_g, w_kv_up,
              q_norm_g, k_norm_g, fourier_out_g, attn_out_g, w_out, mlp_norm_g,
              w_mlp_in, w_mlp_out):
    cos, sin = rotary_tables(positions)
    splits = [F_WIDTH, F_WIDTH + Q_LORA, F_WIDTH + Q_LORA + KV_LORA]
    for l in range(DEPTH):
        h = rms_norm(x, attn_norm_g[l])
        z = h @ w_in[l]
        z_f, c_q, c_kv, k_pe = jnp.split(z, splits, axis=-1)
        y_f = rms_norm(fourier_mixer(z_f, w_fourier[l]), fourier_out_g[l])
        y_a = rms_norm(mla_mixer(c_q, c_kv, k_pe, cos, sin, q_a_g[l], w_q_up[l], kv_a_g[l],
                                 w_kv_up[l], q_norm_g[l], k_norm_g[l]), attn_out_g[l])
        x = x + jnp.concatenate([y_f, y_a], axis=-1) @ w_out[l]
        hm = rms_norm(x, mlp_norm_g[l]) @ w_mlp_in[l]
        x = x + jnp.square(jax.nn.relu(hm)) @ w_mlp_out[l]
    return x
```

```python
import math
from bisect import bisect_left
from contextlib import ExitStack

import numpy as np
import ml_dtypes

import concourse.bass as bass
import concourse.mybir as mybir
from concourse.bass_utils import run_bass_kernel_spmd

F32 = mybir.dt.float32
BF16 = mybir.dt.bfloat16
I32 = mybir.dt.int32
ALU = mybir.AluOpType
AF = mybir.ActivationFunctionType

NCORES = 8
BATCH, SEQ, D = 16, 2048, 1024
DEPTH = 2
NSEQ = BATCH // NCORES
NH = 6
DFF = 4096
EPS = 1e-6
TB = 512
NTB = SEQ // TB
GRAN = 256
PIPE = True
P1PIPE = True
P4PIPE = True
HOIST = True
NBP = True
SKIPB = False
SPLIT_DMA_SEMS = True
VACT = False
NOV = False
STOP = None


class _Stop(Exception):
    pass
NBP_POOL = (6, 7)
_DS = {F32: 4, BF16: 2, I32: 4}


class Prog:
    ENG = ("pe", "act", "dve", "pool", "sp")

    def __init__(self, nc, ndma=20):
        self.nc = nc
        self.stream = {e: [] for e in self.ENG}
        self.npos = {e: 0 for e in self.ENG}
        self.ops = {e: [] for e in self.ENG}
        self.ndma = ndma
        self.dmacnt = [0] * ndma
        self.dmarr = {"sp": 0, "pool": 0}
        ctrs = list(self.ENG) + [("d", i) for i in range(ndma)]
        self.ctrs = ctrs
        self.seen = {e: {c: 0 for c in ctrs} for e in self.ENG}
        self.lastw = {}
        self.readers = {}

    @staticmethod
    def keys(ap):
        t = ap.tensor
        tn = type(t).__name__
        if tn.startswith("DRam"):
            return ()
        shape = list(t.shape)
        pstride = 1
        for s in shape[1:]:
            pstride *= int(s)
        off = int(ap.offset)
        p0 = off // pstride
        e0 = off % pstride
        dims = [(int(s), int(c)) for s, c in ap.ap]
        pc = dims[0][1]
        ext = 1
        for s, c in dims[1:]:
            ext += (c - 1) * abs(s)
        ds = _DS[ap.dtype]
        if tn.startswith("SB"):
            base = int(t.manual_sbuf_range[0])
            sp = 0
        else:
            base = 0
            sp = 1
        lo = base + e0 * ds
        hi = lo + ext * ds
        q0 = p0 // 32
        q1 = (p0 + pc - 1) // 32
        gr = 2048 if sp else GRAN
        return [(sp, g, q) for g in range(lo // gr, (hi - 1) // gr + 1)
                for q in range(q0, q1 + 1)]

    def _deps(self, rkeys, wkeys, eng=None):
        need = {}
        lastw, readers = self.lastw, self.readers
        for k in rkeys:
            w = lastw.get(k)
            if w is not None and need.get(w[0], 0) < w[1]:
                need[w[0]] = w[1]
            if k[0] == 1:
                r = readers.get(k)
                if r:
                    for c, p in r.items():
                        if c != eng and need.get(c, 0) < p:
                            need[c] = p
        for k in wkeys:
            w = lastw.get(k)
            if w is not None and need.get(w[0], 0) < w[1]:
                need[w[0]] = w[1]
            r = readers.get(k)
            if r:
                for c, p in r.items():
                    if need.get(c, 0) < p:
                        need[c] = p
        return need

    def _waits(self, eng, need):
        seen = self.seen[eng]
        for ctr, pos in need.items():
            if ctr == eng and eng == "pe":
                continue
            if seen[ctr] < pos:
                if not isinstance(ctr, tuple):
                    self.ops[ctr][pos - 1][2] = True
                self.stream[eng].append(["wait", ctr, pos])
                seen[ctr] = pos

    def _record(self, ctr, pos, rkeys, wkeys):
        lastw, readers = self.lastw, self.readers
        for k in rkeys:
            r = readers.get(k)
            if r is None:
                readers[k] = {ctr: pos}
            else:
                r[ctr] = pos
        for k in wkeys:
            lastw[k] = (ctr, pos)
            if k in readers:
                readers[k] = {}

    def op(self, eng, fn, reads, writes):
        rk = [k for a in reads for k in self.keys(a)]
        wk = [k for a in writes for k in self.keys(a)]
        self._waits(eng, self._deps(rk, wk, eng))
        ent = ["op", fn, False]
        self.stream[eng].append(ent)
        self.ops[eng].append(ent)
        self.npos[eng] += 1
        self._record(eng, self.npos[eng], rk, wk)

    def dma(self, queue, out, in_):
        rk = self.keys(in_)
        wk = self.keys(out)
        if SPLIT_DMA_SEMS:
            half = self.ndma // 2
            k = self.dmarr[queue]
            self.dmarr[queue] = (k + 1) % half
            j = k if queue == "sp" else half + k
        else:
            j = self.dmarr["sp"]
            self.dmarr["sp"] = (j + 1) % self.ndma
        n = self.dmacnt[j] + 1
        need = self._deps(rk, wk)
        if n > 1:
            need[("d", j)] = max(need.get(("d", j), 0), n - 1)
        self._waits(queue, need)
        self.stream[queue].append(["dma", (out, in_), j])
        self.dmacnt[j] = n
        self._record(("d", j), n, rk, wk)

    def emit(self):
        nc = self.nc
        for j in range(self.ndma):
            if self.dmacnt[j]:
                self.stream["sp"].append(["wait", ("d", j), self.dmacnt[j]])
        rank = {}
        for e in self.ENG:
            r, rk = 0, []
            for ent in self.ops[e]:
                if ent[2]:
                    r += 1
                rk.append(r)
            rank[e] = rk
        with ExitStack() as st:
            sem = {}
            for c in self.ctrs:
                nm = c if isinstance(c, str) else "dq%d" % c[1]
                sem[c] = st.enter_context(nc.semaphore("s_" + nm))
            block = st.enter_context(nc.Block())

            def mk(e):
                def body(eng):
                    for ent in self.stream[e]:
                        if ent[0] == "wait":
                            c, p = ent[1], ent[2]
                            v = 16 * p if isinstance(c, tuple) else rank[c][p - 1]
                            eng.wait_ge(sem[c], v)
                        elif ent[0] == "op":
                            ins = ent[1](eng)
                            if ent[2]:
                                ins.then_inc(sem[e], 1)
                        else:
                            o, i = ent[1]
                            eng.dma_start(out=o, in_=i).then_inc(sem[("d", ent[2])], 16)
                return body

            block.tensor(mk("pe"))
            block.scalar(mk("act"))
            block.vector(mk("dve"))
            block.gpsimd(mk("pool"))
            block.sync(mk("sp"))


def build_nc(layers=(0, 1), nseq=NSEQ, dbg=None):
    nc = bass.Bass("TRN2", target_bir_lowering=False)
    P = Prog(nc)

    def din(name, shape, dt):
        return nc.dram_tensor(name, list(shape), dt, kind="ExternalInput").ap()

    xT_d = din("xT", [NSEQ, 128, 8, SEQ], F32)
    pos_d = din("pos", [NSEQ, 64, SEQ], I32)
    gains_d = din("gains", [128, 64], F32)
    cst_d = din("cst", [128, 8], F32)
    rm_d = din("rm", [64, 64], BF16)
    dft64_d = din("dft64", [128, 256], BF16)
    c0_d = din("c0", [128, 16, 512], BF16)
    ns0_d = din("ns0", [128, 16, 512], BF16)
    win_d = din("w_in", [DEPTH, 128, 8, 832], F32)
    wf_d = din("w_fourier", [DEPTH, 4, 64, 64], F32)
    wq_d = din("w_q_up", [DEPTH, 128, 2, 1152], F32)
    wkv_d = din("w_kv_up", [DEPTH, 128, 2, 1536], F32)
    wo_d = din("w_out", [DEPTH, 128, 8, 1024], F32)
    w1_d = din("w_mlp_in", [DEPTH, 128, 8, DFF], F32)
    w2_d = din("w_mlp_out", [DEPTH, 128, 32, 1024], F32)
    yT_d = nc.dram_tensor("yT", [NSEQ, 128, 8, SEQ], F32, kind="ExternalOutput").ap()
    dbg_d = {}
    if dbg:
        for nm, shp in dbg.items():
            dbg_d[nm] = nc.dram_tensor("dbg_" + nm, list(shp), F32, kind="ExternalOutput").ap()

    cur = [16512]

    def sb(name, shape, dt, at=None):
        n = _DS[dt]
        for s in shape[1:]:
            n *= s
        if at is None:
            at = cur[0]
            cur[0] = (at + n + 31) // 32 * 32
        return nc.alloc_sbuf_tensor_at(name, list(shape), dt, offset=at)

    xT = sb("xT", [128, 8, SEQ], F32)
    cos2 = sb("cos2", [128, SEQ], BF16)
    sin2 = sb("sin2", [128, SEQ], BF16)
    ones = sb("ones", [128, 128], BF16)
    G = sb("gains", [128, 64], F32)
    CST = sb("cst", [128, 8], F32)
    RM = sb("rm", [128, 128], BF16)
    WFBD = sb("wfbd", [128, 2, 128], BF16)
    RHS1 = sb("rhs1", [128, 2, 256], BF16)
    DFT64 = sb("dft64", [128, 256], BF16)
    A0 = cur[0]
    R0, R1, R2, R3 = A0, A0 + 32768, A0 + 65536, A0 + 98304
    assert R3 + 32768 + 3072 <= 229344, R3

    WIN = sb("win", [128, 8, 832], BF16, at=R2 + 8192)
    XN = sb("xn", [128, 2, 8, TB], BF16, at=R0 + 13312)
    ZF = sb("zf", [128, 2, SEQ], BF16, at=R1)
    CQN = sb("cqn", [128, 2, SEQ], BF16, at=R1 + 8192)
    CKVN = sb("ckvn", [128, 2, SEQ], BF16, at=R1 + 16384)
    KPE = sb("kpe", [128, SEQ], F32, at=R1 + 24576)
    YFN = sb("yfn", [128, 2, SEQ], BF16, at=R2)
    SQ_A = sb("sqA", [128, 2, TB], BF16, at=R2 + 24576)
    RST_A = sb("rstA", [128, 2, TB], F32, at=R2 + 24576 + 2048)
    TMP_A = sb("tmpA", [128, 2, TB], BF16, at=R2 + 24576 + 2048 + 4096)
    SQ_B = sb("sqB", [128, 2, TB], BF16, at=R1 + 6144)
    RST_B = sb("rstB", [128, 2, TB], F32, at=R3 + 18944)
    TMP_B = sb("tmpB", [128, 3, TB], BF16, at=R3 + 18944 + 4096)
    TMPF_B = sb("tmpfB", [128, TB], F32, at=R3 + 18944 + 4096 + 3072)
    QRAW = sb("qraw", [128, 2, TB], F32, at=R3 + 18944 + 4096 + 3072 + 2048)
    SQX = sb("sqx", [128, 8, TB], BF16, at=R2)
    RAW4 = sb("raw4", [128, 4, TB], F32, at=R0)
    SQC = sb("sqc", [128, 4, TB], BF16, at=R0 + 8192)
    RSTC1 = sb("rstc1", [128, TB], F32, at=R2 + 24576 + 2048 + 4096)
    SQY = sb("sqy", [128, NH, TB], BF16, at=R1)
    SQM = sb("sqm", [128, 8, TB], BF16, at=R1 + 24576)
    W1X = sb("w1x", [128, 8, 512], BF16, at=R3)
    W2X = sb("w2x", [128, 4, 1024], BF16, at=R3 + 8192)
    PQ0 = sb("pq0", [128, 16, 512], BF16, at=R0)
    PQH = [sb("pqh%d" % i, [128, 16, 256], BF16, at=R0 + 16384 + 8192 * i) for i in range(2)]
    T1 = sb("t1", [128, 16, 512], BF16, at=R2 + 8192)
    C0 = sb("c0", [128, 16, 512], BF16, at=R3)
    NS0 = sb("ns0", [128, 16, 512], BF16, at=R3 + 16384)
    QN = [sb("qn%d" % i, [128, SEQ], BF16, at=R0 + 16384 * i) for i in range(2)]
    QR = [sb("qr%d" % i, [128, SEQ], BF16, at=R0 + 16384 * i + 4096) for i in range(2)]
    KN = [sb("kn%d" % i, [128, SEQ], BF16, at=R0 + 16384 * i + 8192) for i in range(2)]
    VH = [sb("vh%d" % i, [128, 16, 128], BF16, at=R0 + 16384 * i + 12288) for i in range(2)]
    KR = [sb("kr%d" % i, [128, SEQ], BF16, at=R1 + 24576 + 4096 * i) for i in range(2)]
    PT = sb("pt", [128, 3, 2 * TB], BF16, at=R1)
    YA = sb("ya", [128, NH, SEQ], BF16, at=R2 + 8192)
    WQ = sb("wq", [128, 2, 1152], BF16, at=R3)
    WKV = sb("wkv", [128, 2, 1536], BF16, at=R3 + 4608)
    KPR = sb("kpr", [128, SEQ], BF16, at=R3 + 10752)
    SQPE = sb("sqpe", [128, SEQ], BF16, at=R3 + 14848)
    TK = sb("tk", [128, SEQ], BF16, at=R2 + 8192 + 5 * 4096)
    QRAW2 = sb("qraw2", [128, TB], F32, at=R2 + 8192 + 5 * 4096)
    SQ2 = sb("sq2", [128, TB], BF16, at=R2 + 8192 + 5 * 4096 + 2048)
    WO = sb("wo", [128, 8, 1024], BF16, at=R1 + 8192)
    XNA = sb("xna", [128, 8, SEQ], BF16, at=R0)
    W1B = sb("w1b", [128, 2, 8, 512], BF16, at=R1)
    W2B = sb("w2b", [128, 2, 4, 1024], BF16, at=R1 + 16384)
    HB = sb("hb", [128, 2, 4, TB], BF16, at=R2)
    POSI = sb("posi", [128, SEQ], I32, at=R0)
    ANG = sb("ang", [128, SEQ], F32, at=R0 + 8192)
    KQI = sb("kqi", [128, SEQ], I32, at=R0 + 16384)
    KQF = sb("kqf", [128, SEQ], F32, at=R0 + 24576)
    RR = sb("rr", [128, SEQ], F32, at=R1)
    MM_ = sb("mmk", [128, SEQ], F32, at=R1 + 8192)

    WQR = sb("wqr", [128, 2, NH, 128], BF16, at=R3 + 32768)
    PS = nc.alloc_psum_tensor("ps", [128, 4096], F32)

    def bank(i, p=128, n=TB):
        return PS[0:p, i * 512:i * 512 + n]

    rr = [0]

    def nb():
        i = rr[0]
        rr[0] = (i + 1) % 8
        return i

    def mm(out, lhsT, rhs, start, stop):
        P.op("pe", lambda e: e.matmul(out, lhsT, rhs, start=start, stop=stop),
             [lhsT, rhs], [out])

    def act(out, in_, func, scale=1.0, bias=None):
        rd = [in_]
        if bias is not None and not isinstance(bias, float):
            rd.append(bias)
        if not isinstance(scale, float):
            rd.append(scale)
        if bias is None:
            P.op("act", lambda e: e.activation(out, in_, func, scale=scale), rd, [out])
        else:
            P.op("act", lambda e: e.activation(out, in_, func, bias=bias, scale=scale), rd, [out])

    def tt(out, in0, in1, op, eng="dve"):
        P.op(eng, lambda e: e.tensor_tensor(out, in0, in1, op), [in0, in1], [out])

    def ts(out, in0, s1, s2, op0, op1=None, eng="dve"):
        rd = [in0] + [s for s in (s1, s2) if s is not None and not isinstance(s, float)]
        if op1 is None:
            P.op(eng, lambda e: e.tensor_scalar(out, in0, s1, None, op0), rd, [out])
        else:
            P.op(eng, lambda e: e.tensor_scalar(out, in0, s1, s2, op0, op1), rd, [out])

    def stt(out, in0, scalar, in1, op0, op1):
        rd = [in0, in1] + ([] if isinstance(scalar, float) else [scalar])
        P.op("dve", lambda e: e.scalar_tensor_tensor(out, in0, scalar, in1, op0, op1), rd, [out])

    def cp(out, in_, eng="dve"):
        P.op(eng, lambda e: e.tensor_copy(out, in_), [in_], [out])

    def recip(out, in_):
        P.op("dve", lambda e: e.reciprocal(out, in_), [in_], [out])

    def memset(ap, v, eng="dve"):
        P.op(eng, lambda e: e.memset(ap, v), [], [ap])

    def dump(nm, ap_sb, dst):
        if nm in dbg_d:
            P.dma("sp", dst, ap_sb)


    def rstd_from(ssb, dd, rst):
        act(rst, ssb, AF.Ln, scale=1.0 / dd, bias=CST[:, 5:6])
        act(rst, rst, AF.Exp, scale=-0.5)

    memset(ones[:, :], 1.0)
    P.dma("sp", G[:, :], gains_d)
    P.dma("sp", CST[:, :], cst_d)
    memset(RM[:, :], 0.0)
    P.dma("sp", RM[0:64, 0:64], rm_d)
    memset(cos2[:, :], 0.0)
    memset(sin2[:, :], 0.0)
    memset(WQR[:, :, :, :], 0.0)
    P.dma("sp", DFT64[:, :], dft64_d)

    def g_col(l, c, p=128):
        return G[0:p, l * 32 + c:l * 32 + c + 1]

    pairs = [(s_, l_) for s_ in range(nseq) for l_ in layers]
    P.dma("pool", WIN[:, :, :], win_d[layers[0]])
    for s in range(nseq):
        for c in range(8):
            P.dma("sp", xT[:, c, :], xT_d[s, :, c, :])
        P.dma("sp", POSI[0:64, :], pos_d[s])
        cp(ANG[0:64, :], POSI[0:64, :])
        ts(ANG[0:64, :], ANG[0:64, :], CST[0:64, 0:1], None, ALU.mult)
        ts(KQI[0:64, :], ANG[0:64, :], 1.0 / (2 * math.pi), None, ALU.mult)
        cp(KQF[0:64, :], KQI[0:64, :])
        C1 = 6.28125
        C2 = 2 * math.pi - C1
        stt(RR[0:64, :], KQF[0:64, :], -C1, ANG[0:64, :], ALU.mult, ALU.add)
        stt(RR[0:64, :], KQF[0:64, :], -C2, RR[0:64, :], ALU.mult, ALU.add)
        ts(MM_[0:64, :], RR[0:64, :], math.pi, -2 * math.pi, ALU.is_gt, ALU.mult)
        tt(RR[0:64, :], RR[0:64, :], MM_[0:64, :], ALU.add)
        ts(MM_[0:64, :], RR[0:64, :], -math.pi, 2 * math.pi, ALU.is_lt, ALU.mult)
        tt(RR[0:64, :], RR[0:64, :], MM_[0:64, :], ALU.add)
        PI_S = 3.1415925
        ts(RR[0:64, :], RR[0:64, :], -PI_S, PI_S, ALU.max, ALU.min)
        act(sin2[0:64, :], RR[0:64, :], AF.Sin)
        act(MM_[0:64, :], RR[0:64, :], AF.Abs)
        act(cos2[0:64, :], MM_[0:64, :], AF.Sin, scale=-1.0, bias=CST[0:64, 6:7])
        if s == 0 and "cos" in dbg_d:
            cp(ANG[0:64, :], cos2[0:64, :])
            P.dma("sp", dbg_d["cos"], ANG[0:64, :])
            cp(KQF[0:64, :], sin2[0:64, :])
            P.dma("sp", dbg_d["sin"], KQF[0:64, :])

        for l in layers:
          try:
            memset(KPE[64:128, :], 0.0, eng="pool")
            memset(WFBD[:, :, :], 0.0, eng="pool")
            for g in range(4):
                pr, hf = g // 2, g % 2
                P.dma("pool", WFBD[64 * hf:64 * hf + 64, pr, 64 * hf:64 * hf + 64], wf_d[l, g])
            for pr in range(2):
                b = nb()
                mm(bank(b, 128, 128), DFT64[:, 0:128], WFBD[:, pr, :], True, True)
                mm(PS[:, b * 512 + 128:b * 512 + 256], DFT64[:, 128:256], WFBD[:, pr, :], True, True)
                act(RHS1[:, pr, :], PS[:, b * 512:b * 512 + 256], AF.Copy)

            def xn_sq(tb):
                tsl = slice(tb * TB, (tb + 1) * TB)
                for c in range(8):
                    act(SQX[:, c, :], xT[:, c, tsl], AF.Square)

            def xn_fin(tb):
                tsl = slice(tb * TB, (tb + 1) * TB)
                ssb = nb()
                for c in range(8):
                    mm(bank(ssb), ones[:, :], SQX[:, c, :], c == 0, c == 7)
                rstd_from(bank(ssb), 1024.0, RST_A[:, 0, :])
                for c in range(8):
                    stt(XN[:, tb % 2, c, :], xT[:, c, tsl], g_col(l, c), RST_A[:, 0, :], ALU.mult, ALU.mult)

            def chain_fin(tb):
                tsl = slice(tb * TB, (tb + 1) * TB)
                for (dst, j0, gc, rst) in ((CQN, 0, 16, RST_A[:, 1, :]), (CKVN, 2, 18, RSTC1[:, :])):
                    sb2 = nb()
                    for j in range(2):
                        mm(bank(sb2), ones[:, :], SQC[:, j0 + j, :], j == 0, j == 1)
                    rstd_from(bank(sb2), 256.0, rst)
                    for j in range(2):
                        stt(dst[:, j, tsl], RAW4[:, j0 + j, :], g_col(l, gc + j), rst,
                            ALU.mult, ALU.mult)

            def p1_group(tb, gi):
                tsl = slice(tb * TB, (tb + 1) * TB)
                xb = tb % 2
                b = nb()
                m = 128 if gi < 6 else 64
                for kc in range(8):
                    mm(bank(b, m), WIN[:, kc, gi * 128:gi * 128 + m], XN[:, xb, kc, :], kc == 0, kc == 7)
                if gi < 2:
                    act(ZF[:, gi, tsl], bank(b), AF.Copy)
                elif gi < 6:
                    cp(RAW4[:, gi - 2, :], bank(b))
                    act(SQC[:, gi - 2, :], RAW4[:, gi - 2, :], AF.Square)
                else:
                    act(KPE[0:64, tsl], bank(b, 64), AF.Copy)

            xn_sq(0)
            xn_fin(0)
            for tb in range(NTB):
                if tb + 1 < NTB:
                    xn_sq(tb + 1)
                p1_group(tb, 0)
                p1_group(tb, 1)
                if tb > 0:
                    chain_fin(tb - 1)
                p1_group(tb, 2)
                p1_group(tb, 3)
                if tb + 1 < NTB:
                    xn_fin(tb + 1)
                p1_group(tb, 4)
                p1_group(tb, 5)
                p1_group(tb, 6)
            chain_fin(NTB - 1)
            if s == 0 and l == layers[0] and "cqn" in dbg_d:
                for j in range(2):
                    cp(ANG[:, :], CQN[:, j, :])
                    P.dma("sp", dbg_d["cqn"][j], ANG[:, :])
                    cp(ANG[:, :], ZF[:, j, :])
                    P.dma("sp", dbg_d["zf"][j], ANG[:, :])

            P.dma("sp", C0[:, :, :], c0_d)
            P.dma("sp", NS0[:, :, :], ns0_d)
            for c in range(16):
                b = nb()
                for pr in range(2):
                    mm(PS[:, b * 512 + 256 * pr:b * 512 + 256 * pr + 256],
                       ZF[:, pr, c * 128:(c + 1) * 128], RHS1[:, pr, :], True, True)
                if c % 2 == 0:
                    act(PQ0[:, c, :], bank(b), AF.Copy)
                else:
                    cp(PQ0[:, c, :], bank(b))

            sgn2, c1c, s1c, ns1c = CST[:, 1:2], CST[:, 2:3], CST[:, 3:4], CST[:, 4:5]
            ts(T1[:, :, :], PQ0[:, :, :], c1c, None, ALU.mult)

            def pq_src(j, fc):
                lo = 256 * fc
                if j == 0:
                    return PQ0, lo
                dst = PQH[fc]
                if j == 2:
                    ts(dst[:, :, :], PQ0[:, :, lo:lo + 256], sgn2, None, ALU.mult)
                else:
                    sa, sb_ = (ns1c, s1c) if j == 1 else (s1c, ns1c)
                    stt(dst[:, :, 0:128], PQ0[:, :, lo + 128:lo + 256], sa, T1[:, :, lo:lo + 128],
                        ALU.mult, ALU.add)
                    stt(dst[:, :, 128:256], PQ0[:, :, lo:lo + 128], sb_, T1[:, :, lo + 128:lo + 256],
                        ALU.mult, ALU.add)
                return dst, 0

            jorder = (0, 2, 1, 3)
            srcs = {}
            for fc in range(2):
                srcs[(jorder[0], fc)] = pq_src(jorder[0], fc)
            for ji, j in enumerate(jorder):
                jn = jorder[ji + 1] if ji + 1 < 4 else None
                tsl = slice(j * TB, (j + 1) * TB)
                fb = []
                for fc in range(2):
                    src, lo = srcs[(j, fc)]
                    b = nb()
                    fb.append(b)
                    for c in range(16):
                        mm(bank(b), src[:, c, lo:lo + 128], C0[:, c, :], c == 0, False)
                        mm(bank(b), src[:, c, lo + 128:lo + 256], NS0[:, c, :], False, c == 15)
                    if jn is not None:
                        srcs[(jn, fc)] = pq_src(jn, fc)
                sb2 = nb()
                for fc in range(2):
                    act(SQ_A[:, fc, :], bank(fb[fc]), AF.Square)
                    mm(bank(sb2), ones[:, :], SQ_A[:, fc, :], fc == 0, fc == 1)
                rstd_from(bank(sb2), 256.0, RST_A[:, 0, :])
                for fc in range(2):
                    stt(YFN[:, fc, tsl], bank(fb[fc]), g_col(l, 24 + fc), RST_A[:, 0, :], ALU.mult, ALU.mult)

            P.dma("pool", WQ[:, :, :], wq_d[l])
            P.dma("pool", WKV[:, :, :], wkv_d[l])
            P.dma("pool", WQR[:, :, :, 0:64],
                  wq_d[l].rearrange("p k (h e) -> p k h e", h=NH)[:, :, :, 128:192])
            ts(TK[:, :], KPE[:, :], g_col(l, 23), None, ALU.mult)
            for tb in range(NTB):
                tsl = slice(tb * TB, (tb + 1) * TB)
                act(SQPE[:, tsl], KPE[:, tsl], AF.Square)
                b = nb()
                mm(bank(b), RM[:, :], TK[:, tsl], True, True)
                tt(TMP_B[:, 0, :], TK[:, tsl], cos2[:, tsl], ALU.mult)
                tt(TMP_B[:, 1, :], bank(b), sin2[:, tsl], ALU.mult)
                tt(KPR[:, tsl], TMP_B[:, 0, :], TMP_B[:, 1, :], ALU.add)

            scale = 1.0 / math.sqrt(192.0)
            pb = [0]

            def nbp():
                if not NBP:
                    return nb()
                pb[0] = (pb[0] + 1) % len(NBP_POOL)
                return NBP_POOL[pb[0]]

            def proj_gen(h):
                hp = h % 2
                for tb in range(NTB):
                    tsl = slice(tb * TB, (tb + 1) * TB)
                    qa, qb = nbp(), nbp()
                    for kc in range(2):
                        mm(bank(qa), WQ[:, kc, h * 192:h * 192 + 128], CQN[:, kc, tsl], kc == 0, kc == 1)
                    for kc in range(2):
                        mm(bank(qb), WQR[:, kc, h, :], CQN[:, kc, tsl], kc == 0, kc == 1)
                    act(SQ_B[:, 0, :], bank(qa), AF.Square)
                    act(SQ_B[:, 1, :], bank(qb), AF.Square)
                    cp(QRAW[:, 0, :], bank(qa))
                    cp(QRAW[:, 1, :], bank(qb))
                    yield
                    ka = nbp()
                    for kc in range(2):
                        mm(bank(ka), WKV[:, kc, h * 256:h * 256 + 128], CKVN[:, kc, tsl], kc == 0, kc == 1)
                    act(SQ2[:, :], bank(ka), AF.Square)
                    cp(QRAW2[:, :], bank(ka))
                    yield
                    ssb = nbp()
                    mm(bank(ssb), ones[:, :], SQ_B[:, 0, :], True, False)
                    mm(bank(ssb), ones[:, :], SQ_B[:, 1, :], False, True)
                    rstd_from(bank(ssb), 192.0, RST_B[:, 0, :])
                    stt(TMP_B[:, 0, :], QRAW[:, 1, :], g_col(l, 21), RST_B[:, 0, :], ALU.mult, ALU.mult)
                    stt(QN[hp][:, tsl], QRAW[:, 0, :], g_col(l, 20), RST_B[:, 0, :], ALU.mult, ALU.mult)
                    yield
                    ssk = nbp()
                    mm(bank(ssk), ones[:, :], SQ2[:, :], True, False)
                    mm(bank(ssk), ones[:, :], SQPE[:, tsl], False, True)
                    rstd_from(bank(ssk), 192.0, RST_B[:, 1, :])
                    stt(KN[hp][:, tsl], QRAW2[:, :], g_col(l, 22), RST_B[:, 1, :], ALU.mult, ALU.mult)
                    tt(KR[hp][:, tsl], KPR[:, tsl], RST_B[:, 1, :], ALU.mult)
                    yield
                    swb = nbp()
                    mm(bank(swb), RM[:, :], TMP_B[:, 0, :], True, True)
                    tt(TMP_B[:, 1, :], TMP_B[:, 0, :], cos2[:, tsl], ALU.mult)
                    tt(TMP_B[:, 2, :], bank(swb), sin2[:, tsl], ALU.mult)
                    tt(QR[hp][:, tsl], TMP_B[:, 1, :], TMP_B[:, 2, :], ALU.add)
                    yield
                    vb = nbp()
                    for ci in range(4):
                        c = tb * 4 + ci
                        for kc in range(2):
                            mm(PS[:, vb * 512 + ci * 128:vb * 512 + ci * 128 + 128],
                               CKVN[:, kc, c * 128:(c + 1) * 128],
                               WKV[:, kc, h * 256 + 128:h * 256 + 256], kc == 0, kc == 1)
                    cp(VH[hp][:, tb * 4:tb * 4 + 4, :], bank(vb))
                    yield

            if STOP == "p2":
                raise _Stop()
            if STOP and STOP.startswith("st"):
                for i_, _ in enumerate(proj_gen(0)):
                    if i_ + 1 >= int(STOP[2:]):
                        break
                raise _Stop()
            for _ in proj_gen(0):
                pass
            if STOP == "proj0":
                raise _Stop()
            for h in range(NH):
                hp = h % 2
                gen = proj_gen(h + 1) if h + 1 < NH else None
                if gen is not None and not PIPE:
                    for _ in gen:
                        pass
                    gen = None
                for qb_ in range(NTB):
                    qsl = slice(qb_ * TB, (qb_ + 1) * TB)
                    OB, DB = 4, 5

                    def qk(g):
                        for ci in range(2):
                            c = 2 * g + ci
                            bk = (g % 2) * 2 + ci
                            mm(bank(bk), KN[hp][:, c * 128:(c + 1) * 128], QN[hp][:, qsl], True, False)
                            mm(bank(bk), KR[hp][:, c * 128:(c + 1) * 128], QR[hp][:, qsl], False, True)

                    def ex(g):
                        bk = (g % 2) * 2
                        ptb = g % 3
                        act(PT[:, ptb, :], PS[:, bk * 512:bk * 512 + 1024], AF.Exp, scale=scale)

                    def pv(g):
                        ptb = g % 3
                        for ci in range(2):
                            c = 2 * g + ci
                            mm(bank(OB), VH[hp][:, c, :], PT[:, ptb, ci * TB:(ci + 1) * TB], c == 0, c == 15)
                            mm(bank(DB), ones[:, :], PT[:, ptb, ci * TB:(ci + 1) * TB], c == 0, c == 15)

                    qk(0)
                    for g in range(8):
                        if g + 1 < 8:
                            qk(g + 1)
                        ex(g)
                        if gen is not None:
                            next(gen, None)
                        pv(g)
                    act(TMPF_B[:, :], bank(DB), AF.Ln)
                    act(TMPF_B[:, :], TMPF_B[:, :], AF.Exp, scale=-1.0)
                    tt(YA[:, h, qsl], bank(OB), TMPF_B[:, :], ALU.mult)
                if gen is not None:
                    for _ in gen:
                        pass
                if STOP == "attn0":
                    raise _Stop()
            if STOP == "p3":
                raise _Stop()

            P.dma("pool", WO[:, :, :], wo_d[l])
            def ya_sq(tb):
                tsl = slice(tb * TB, (tb + 1) * TB)
                for h in range(NH):
                    act(SQY[:, h, :], YA[:, h, tsl], AF.Square)

            def ya_fin(tb):
                tsl = slice(tb * TB, (tb + 1) * TB)
                ssb = nb()
                for h in range(NH):
                    mm(bank(ssb), ones[:, :], SQY[:, h, :], h == 0, h == NH - 1)
                rstd_from(bank(ssb), 768.0, RST_B[:, 0, :])
                for h in range(NH):
                    stt(YA[:, h, tsl], YA[:, h, tsl], g_col(l, 26 + h), RST_B[:, 0, :], ALU.mult, ALU.mult)

            def mlp_sq(tb):
                tsl = slice(tb * TB, (tb + 1) * TB)
                for c in range(8):
                    act(SQM[:, c, :], xT[:, c, tsl], AF.Square)

            def mlp_fin(tb):
                tsl = slice(tb * TB, (tb + 1) * TB)
                ssb = nb()
                for c in range(8):
                    mm(bank(ssb), ones[:, :], SQM[:, c, :], c == 0, c == 7)
                rstd_from(bank(ssb), 1024.0, RST_B[:, 1, :])
                for c in range(8):
                    stt(XNA[:, c, tsl], xT[:, c, tsl], g_col(l, 8 + c), RST_B[:, 1, :], ALU.mult, ALU.mult)

            def wout_group(tb, o):
                tsl = slice(tb * TB, (tb + 1) * TB)
                b = nb()
                for kc in range(8):
                    rhs = YFN[:, kc, tsl] if kc < 2 else YA[:, kc - 2, tsl]
                    mm(bank(b), WO[:, kc, o * 128:(o + 1) * 128], rhs, kc == 0, kc == 7)
                tt(xT[:, o, tsl], bank(b), xT[:, o, tsl], ALU.add)

            def load_w(e):
                if e == 0:
                    P.dma("pool", W1X[:, :, :], w1_d[l, :, :, 0:512])
                    P.dma("pool", W2X[:, :, :], w2_d[l, :, 0:4, :])
                else:
                    P.dma("pool", W1B[:, e % 2, :, :], w1_d[l, :, :, e * 512:(e + 1) * 512])
                    P.dma("pool", W2B[:, e % 2, :, :], w2_d[l, :, e * 4:(e + 1) * 4, :])

            load_w(0)

            ya_sq(0)
            ya_fin(0)
            for tb in range(NTB):
                if tb + 1 < NTB:
                    ya_sq(tb + 1)
                for o in range(4):
                    wout_group(tb, o)
                if tb > 0:
                    mlp_fin(tb - 1)
                if tb + 1 < NTB:
                    ya_fin(tb + 1)
                for o in range(4, 8):
                    wout_group(tb, o)
                mlp_sq(tb)
            mlp_fin(NTB - 1)

            ip = pairs.index((s, l))
            if ip + 1 < len(pairs):
                P.dma("pool", WIN[:, :, :], win_d[pairs[ip + 1][1]])

            def up(e, tb, hb):
                tsl = slice(tb * TB, (tb + 1) * TB)
                for jj in range(4):
                    b = nb()
                    for kc in range(8):
                        w1 = W1X[:, kc, jj * 128:(jj + 1) * 128] if e == 0 else W1B[:, e % 2, kc, jj * 128:(jj + 1) * 128]
                        mm(bank(b), w1, XNA[:, kc, tsl], kc == 0, kc == 7)
                    act(SQ_A[:, jj % 2, :], bank(b), AF.Square)
                    stt(HB[:, hb, jj, :], bank(b), 0.0, SQ_A[:, jj % 2, :], ALU.is_gt, ALU.mult)

            def down(e, tb, hb):
                tsl = slice(tb * TB, (tb + 1) * TB)
                for o in range(8):
                    b = nb()
                    for jj in range(4):
                        w2 = W2X[:, jj, o * 128:(o + 1) * 128] if e == 0 else W2B[:, e % 2, jj, o * 128:(o + 1) * 128]
                        mm(bank(b), w2, HB[:, hb, jj, :], jj == 0, jj == 3)
                    tt(xT[:, o, tsl], bank(b), xT[:, o, tsl], ALU.add)

            steps = [(e, tb) for e in range(8) for tb in range(NTB)]
            up(0, 0, 0)
            for i, (e, tb) in enumerate(steps):
                if tb == 0 and e + 1 < 8:
                    load_w(e + 1)
                if i + 1 < len(steps):
                    e2, tb2 = steps[i + 1]
                    up(e2, tb2, (i + 1) % 2)
                down(e, tb, i % 2)

          except _Stop:
            pass
        for c in range(8):
            P.dma("sp", yT_d[s, :, c, :], xT[:, c, :])

    P.emit()
    return nc


def _pmajor(w, kchunks):
    K, N = w.shape
    return np.ascontiguousarray(w.reshape(kchunks, 128, N).transpose(1, 0, 2))


def _consts():
    bf = ml_dtypes.bfloat16
    half = 32
    inv_freq = (10000.0 ** (-np.arange(half, dtype=np.float32) / half)).astype(np.float32)
    cst = np.zeros((128, 8), np.float32)
    p = np.arange(128)
    cst[:64, 0] = inv_freq[p[:64] % 32]
    cst[:, 1] = np.where(p % 2 == 0, 1.0, -1.0)
    cst[:, 2] = np.array([1.0, 0.0, -1.0, 0.0])[p % 4]
    cst[:, 3] = np.array([0.0, 1.0, 0.0, -1.0])[p % 4]
    cst[:, 4] = -cst[:, 3]
    cst[:, 5] = EPS
    cst[:, 6] = 1.5707963
    rm = np.zeros((64, 64), np.float32)
    for m in range(32):
        rm[m + 32, m] = -1.0
        rm[m, m + 32] = 1.0
    k = np.arange(64)
    a64 = 2 * np.pi * np.outer(k, k) / 64.0
    cc = np.cos(a64) / 8.0
    sc = np.sin(a64) / 8.0
    dft64 = np.zeros((128, 256), np.float64)
    dft64[0:64, 0:64] = cc
    dft64[64:128, 64:128] = cc
    dft64[0:64, 128:192] = sc
    dft64[64:128, 192:256] = sc
    sidx = np.arange(SEQ, dtype=np.int64)
    r = np.arange(512, dtype=np.int64)
    ph = (np.outer(sidx, r) % SEQ).astype(np.float64) * (2 * np.pi / SEQ)
    c0 = np.cos(ph) / math.sqrt(SEQ)
    ns0 = -np.sin(ph) / math.sqrt(SEQ)
    c0 = np.ascontiguousarray(c0.reshape(16, 128, 512).transpose(1, 0, 2))
    ns0 = np.ascontiguousarray(ns0.reshape(16, 128, 512).transpose(1, 0, 2))
    return dict(cst=cst, rm=rm.astype(bf), dft64=dft64.astype(np.float32).astype(bf),
                c0=c0.astype(np.float32).astype(bf), ns0=ns0.astype(np.float32).astype(bf))


def _gains(inp):
    g = np.zeros((128, 64), np.float32)

    def cols(v):
        return np.asarray(v, np.float32).reshape(-1, 128).T

    for l in range(DEPTH):
        o = l * 32
        g[:, o + 0:o + 8] = cols(inp["attn_norm_g"][l])
        g[:, o + 8:o + 16] = cols(inp["mlp_norm_g"][l])
        g[:, o + 16:o + 18] = cols(inp["q_a_g"][l])
        g[:, o + 18:o + 20] = cols(inp["kv_a_g"][l])
        g[:, o + 20] = inp["q_norm_g"][l][:128]
        g[:64, o + 21] = inp["q_norm_g"][l][128:]
        g[:, o + 22] = inp["k_norm_g"][l][:128]
        g[:64, o + 23] = inp["k_norm_g"][l][128:]
        g[:, o + 24:o + 26] = cols(inp["fourier_out_g"][l])
        g[:, o + 26:o + 32] = cols(inp["attn_out_g"][l])
    return g


def _shared_inputs(inp):
    f = lambda a: np.asarray(a, np.float32)
    sh = dict(_consts())
    sh["gains"] = _gains({k: np.asarray(v) for k, v in inp.items()})
    sh["w_in"] = np.stack([_pmajor(f(inp["w_in"][l]), 8) for l in range(DEPTH)])
    sh["w_fourier"] = np.ascontiguousarray(f(inp["w_fourier"]))
    sh["w_q_up"] = np.stack([_pmajor(f(inp["w_q_up"][l]), 2) for l in range(DEPTH)])
    sh["w_kv_up"] = np.stack([_pmajor(f(inp["w_kv_up"][l]), 2) for l in range(DEPTH)])
    sh["w_out"] = np.stack([_pmajor(f(inp["w_out"][l]), 8) for l in range(DEPTH)])
    sh["w_mlp_in"] = np.stack([_pmajor(f(inp["w_mlp_in"][l]), 8) for l in range(DEPTH)])
    sh["w_mlp_out"] = np.stack([_pmajor(f(inp["w_mlp_out"][l]), 32) for l in range(DEPTH)])
    return sh


def _x_to_dev(x):
    xt = np.asarray(x, np.float32).transpose(0, 2, 1).reshape(BATCH, 8, 128, SEQ).transpose(0, 2, 1, 3)
    return [np.ascontiguousarray(xt[i * NSEQ:(i + 1) * NSEQ]) for i in range(NCORES)]


def _y_from_dev(ys):
    yt = np.concatenate(ys, axis=0)
    return np.ascontiguousarray(yt.transpose(0, 2, 1, 3).reshape(BATCH, D, SEQ).transpose(0, 2, 1))


_NC_CACHE = {}


def kernel(**inputs):
    sh = _shared_inputs(inputs)
    pos = np.asarray(inputs["positions"], np.int32)
    posr = np.ascontiguousarray(np.broadcast_to(pos[:, None, :], (BATCH, 64, SEQ)))
    xs = _x_to_dev(inputs["x"])
    if "full" not in _NC_CACHE:
        _NC_CACHE["full"] = build_nc(layers=(0, 1))
    nc = _NC_CACHE["full"]
    in_maps = []
    for i in range(NCORES):
        m = dict(sh)
        m["xT"] = xs[i]
        m["pos"] = np.ascontiguousarray(posr[i * NSEQ:(i + 1) * NSEQ])
        in_maps.append(m)
    res = run_bass_kernel_spmd(nc, in_maps, core_ids=list(range(NCORES)))
    return _y_from_dev([np.asarray(r["yT"]) for r in res.results])
```

```python
import math
from bisect import bisect_left
from contextlib import ExitStack

import numpy as np
import ml_dtypes

import concourse.bass as bass
import concourse.mybir as mybir
from concourse.bass_utils import run_bass_kernel_spmd

F32 = mybir.dt.float32
BF16 = mybir.dt.bfloat16
I32 = mybir.dt.int32
ALU = mybir.AluOpType
AF = mybir.ActivationFunctionType

NCORES = 8
BATCH, SEQ, D = 16, 2048, 1024
DEPTH = 2
NSEQ = BATCH // NCORES
NH = 6
DFF = 4096
EPS = 1e-6
TB = 512
NTB = SEQ // TB
GRAN = 256
PIPE = True
P1PIPE = True
P4PIPE = True
HOIST = True
NBP = True
SKIPB = False
SPLIT_DMA_SEMS = True
VACT = False
NOV = False
STOP = None


class _Stop(Exception):
    pass
NBP_POOL = (6, 7)
_DS = {F32: 4, BF16: 2, I32: 4}


class Prog:
    ENG = ("pe", "act", "dve", "pool", "sp")

    def __init__(self, nc, ndma=20):
        self.nc = nc
        self.stream = {e: [] for e in self.ENG}
        self.npos = {e: 0 for e in self.ENG}
        self.ops = {e: [] for e in self.ENG}
        self.ndma = ndma
        self.dmacnt = [0] * ndma
        self.dmarr = {"sp": 0, "pool": 0}
        ctrs = list(self.ENG) + [("d", i) for i in range(ndma)]
        self.ctrs = ctrs
        self.seen = {e: {c: 0 for c in ctrs} for e in self.ENG}
        self.lastw = {}
        self.readers = {}

    @staticmethod
    def keys(ap):
        t = ap.tensor
        tn = type(t).__name__
        if tn.startswith("DRam"):
            return ()
        shape = list(t.shape)
        pstride = 1
        for s in shape[1:]:
            pstride *= int(s)
        off = int(ap.offset)
        p0 = off // pstride
        e0 = off % pstride
        dims = [(int(s), int(c)) for s, c in ap.ap]
        pc = dims[0][1]
        ext = 1
        for s, c in dims[1:]:
            ext += (c - 1) * abs(s)
        ds = _DS[ap.dtype]
        if tn.startswith("SB"):
            base = int(t.manual_sbuf_range[0])
            sp = 0
        else:
            base = 0
            sp = 1
        lo = base + e0 * ds
        hi = lo + ext * ds
        q0 = p0 // 32
        q1 = (p0 + pc - 1) // 32
        gr = 2048 if sp else GRAN
        return [(sp, g, q) for g in range(lo // gr, (hi - 1) // gr + 1)
                for q in range(q0, q1 + 1)]

    def _deps(self, rkeys, wkeys, eng=None):
        need = {}
        lastw, readers = self.lastw, self.readers
        for k in rkeys:
            w = lastw.get(k)
            if w is not None and need.get(w[0], 0) < w[1]:
                need[w[0]] = w[1]
            if k[0] == 1:
                r = readers.get(k)
                if r:
                    for c, p in r.items():
                        if c != eng and need.get(c, 0) < p:
                            need[c] = p
        for k in wkeys:
            w = lastw.get(k)
            if w is not None and need.get(w[0], 0) < w[1]:
                need[w[0]] = w[1]
            r = readers.get(k)
            if r:
                for c, p in r.items():
                    if need.get(c, 0) < p:
                        need[c] = p
        return need

    def _waits(self, eng, need):
        seen = self.seen[eng]
        for ctr, pos in need.items():
            if ctr == eng and eng == "pe":
                continue
            if seen[ctr] < pos:
                if not isinstance(ctr, tuple):
                    self.ops[ctr][pos - 1][2] = True
                self.stream[eng].append(["wait", ctr, pos])
                seen[ctr] = pos

    def _record(self, ctr, pos, rkeys, wkeys):
        lastw, readers = self.lastw, self.readers
        for k in rkeys:
            r = readers.get(k)
            if r is None:
                readers[k] = {ctr: pos}
            else:
                r[ctr] = pos
        for k in wkeys:
            lastw[k] = (ctr, pos)
            if k in readers:
                readers[k] = {}

    def op(self, eng, fn, reads, writes):
        rk = [k for a in reads for k in self.keys(a)]
        wk = [k for a in writes for k in self.keys(a)]
        self._waits(eng, self._deps(rk, wk, eng))
        ent = ["op", fn, False]
        self.stream[eng].append(ent)
        self.ops[eng].append(ent)
        self.npos[eng] += 1
        self._record(eng, self.npos[eng], rk, wk)

    def dma(self, queue, out, in_):
        rk = self.keys(in_)
        wk = self.keys(out)
        if SPLIT_DMA_SEMS:
            half = self.ndma // 2
            k = self.dmarr[queue]
            self.dmarr[queue] = (k + 1) % half
            j = k if queue == "sp" else half + k
        else:
            j = self.dmarr["sp"]
            self.dmarr["sp"] = (j + 1) % self.ndma
        n = self.dmacnt[j] + 1
        need = self._deps(rk, wk)
        if n > 1:
            need[("d", j)] = max(need.get(("d", j), 0), n - 1)
        self._waits(queue, need)
        self.stream[queue].append(["dma", (out, in_), j])
        self.dmacnt[j] = n
        self._record(("d", j), n, rk, wk)

    def emit(self):
        nc = self.nc
        for j in range(self.ndma):
            if self.dmacnt[j]:
                self.stream["sp"].append(["wait", ("d", j), self.dmacnt[j]])
        rank = {}
        for e in self.ENG:
            r, rk = 0, []
            for ent in self.ops[e]:
                if ent[2]:
                    r += 1
                rk.append(r)
            rank[e] = rk
        with ExitStack() as st:
            sem = {}
            for c in self.ctrs:
                nm = c if isinstance(c, str) else "dq%d" % c[1]
                sem[c] = st.enter_context(nc.semaphore("s_" + nm))
            block = st.enter_context(nc.Block())

            def mk(e):
                def body(eng):
                    for ent in self.stream[e]:
                        if ent[0] == "wait":
                            c, p = ent[1], ent[2]
                            v = 16 * p if isinstance(c, tuple) else rank[c][p - 1]
                            eng.wait_ge(sem[c], v)
                        elif ent[0] == "op":
                            ins = ent[1](eng)
                            if ent[2]:
                                ins.then_inc(sem[e], 1)
                        else:
                            o, i = ent[1]
                            eng.dma_start(out=o, in_=i).then_inc(sem[("d", ent[2])], 16)
                return body

            block.tensor(mk("pe"))
            block.scalar(mk("act"))
            block.vector(mk("dve"))
            block.gpsimd(mk("pool"))
            block.sync(mk("sp"))


def build_nc(layers=(0, 1), nseq=NSEQ, dbg=None):
    nc = bass.Bass("TRN2", target_bir_lowering=False)
    P = Prog(nc)

    def din(name, shape, dt):
        return nc.dram_tensor(name, list(shape), dt, kind="ExternalInput").ap()

    xT_d = din("xT", [NSEQ, 128, 8, SEQ], F32)
    pos_d = din("pos", [NSEQ, 64, SEQ], I32)
    gains_d = din("gains", [128, 64], F32)
    cst_d = din("cst", [128, 8], F32)
    rm_d = din("rm", [64, 64], BF16)
    dft64_d = din("dft64", [128, 256], BF16)
    c0_d = din("c0", [128, 16, 512], BF16)
    ns0_d = din("ns0", [128, 16, 512], BF16)
    win_d = din("w_in", [DEPTH, 128, 8, 832], F32)
    wf_d = din("w_fourier", [DEPTH, 4, 64, 64], F32)
    wq_d = din("w_q_up", [DEPTH, 128, 2, 1152], F32)
    wkv_d = din("w_kv_up", [DEPTH, 128, 2, 1536], F32)
    wo_d = din("w_out", [DEPTH, 128, 8, 1024], F32)
    w1_d = din("w_mlp_in", [DEPTH, 128, 8, DFF], F32)
    w2_d = din("w_mlp_out", [DEPTH, 128, 32, 1024], F32)
    yT_d = nc.dram_tensor("yT", [NSEQ, 128, 8, SEQ], F32, kind="ExternalOutput").ap()
    dbg_d = {}
    if dbg:
        for nm, shp in dbg.items():
            dbg_d[nm] = nc.dram_tensor("dbg_" + nm, list(shp), F32, kind="ExternalOutput").ap()

    cur = [16512]

    def sb(name, shape, dt, at=None):
        n = _DS[dt]
        for s in shape[1:]:
            n *= s
        if at is None:
            at = cur[0]
            cur[0] = (at + n + 31) // 32 * 32
        return nc.alloc_sbuf_tensor_at(name, list(shape), dt, offset=at)

    xT = sb("xT", [128, 8, SEQ], F32)
    cos2 = sb("cos2", [128, SEQ], BF16)
    sin2 = sb("sin2", [128, SEQ], BF16)
    ones = sb("ones", [128, 128], BF16)
    G = sb("gains", [128, 64], F32)
    CST = sb("cst", [128, 8], F32)
    RM = sb("rm", [128, 128], BF16)
    WFBD = sb("wfbd", [128, 2, 128], BF16)
    RHS1 = sb("rhs1", [128, 2, 256], BF16)
    DFT64 = sb("dft64", [128, 256], BF16)
    A0 = cur[0]
    R0, R1, R2, R3 = A0, A0 + 32768, A0 + 65536, A0 + 98304
    assert R3 + 32768 + 3072 <= 229344, R3

    WIN = sb("win", [128, 8, 832], BF16, at=R2 + 8192)
    XN = sb("xn", [128, 2, 8, TB], BF16, at=R0 + 13312)
    ZF = sb("zf", [128, 2, SEQ], BF16, at=R1)
    CQN = sb("cqn", [128, 2, SEQ], BF16, at=R1 + 8192)
    CKVN = sb("ckvn", [128, 2, SEQ], BF16, at=R1 + 16384)
    KPE = sb("kpe", [128, SEQ], F32, at=R1 + 24576)
    YFN = sb("yfn", [128, 2, SEQ], BF16, at=R2)
    SQ_A = sb("sqA", [128, 2, TB], BF16, at=R2 + 24576)
    RST_A = sb("rstA", [128, 2, TB], F32, at=R2 + 24576 + 2048)
    TMP_A = sb("tmpA", [128, 2, TB], BF16, at=R2 + 24576 + 2048 + 4096)
    SQ_B = sb("sqB", [128, 2, TB], BF16, at=R1 + 6144)
    RST_B = sb("rstB", [128, 2, TB], F32, at=R3 + 18944)
    TMP_B = sb("tmpB", [128, 3, TB], BF16, at=R3 + 18944 + 4096)
    TMPF_B = sb("tmpfB", [128, TB], F32, at=R3 + 18944 + 4096 + 3072)
    QRAW = sb("qraw", [128, 2, TB], F32, at=R3 + 18944 + 4096 + 3072 + 2048)
    SQX = sb("sqx", [128, 8, TB], BF16, at=R2)
    RAW4 = sb("raw4", [128, 4, TB], F32, at=R0)
    SQC = sb("sqc", [128, 4, TB], BF16, at=R0 + 8192)
    RSTC1 = sb("rstc1", [128, TB], F32, at=R2 + 24576 + 2048 + 4096)
    SQY = sb("sqy", [128, NH, TB], BF16, at=R1)
    SQM = sb("sqm", [128, 8, TB], BF16, at=R1 + 24576)
    W1X = sb("w1x", [128, 8, 512], BF16, at=R3)
    W2X = sb("w2x", [128, 4, 1024], BF16, at=R3 + 8192)
    PQ0 = sb("pq0", [128, 16, 512], BF16, at=R0)
    PQT = sb("pqt", [128, 16, 512], BF16, at=R0 + 16384)
    T1 = sb("t1", [128, 16, 512], BF16, at=R2 + 8192)
    C0 = sb("c0", [128, 16, 512], BF16, at=R3)
    NS0 = sb("ns0", [128, 16, 512], BF16, at=R3 + 16384)
    QN = [sb("qn%d" % i, [128, SEQ], BF16, at=R0 + 16384 * i) for i in range(2)]
    QR = [sb("qr%d" % i, [128, SEQ], BF16, at=R0 + 16384 * i + 4096) for i in range(2)]
    KN = [sb("kn%d" % i, [128, SEQ], BF16, at=R0 + 16384 * i + 8192) for i in range(2)]
    VH = [sb("vh%d" % i, [128, 16, 128], BF16, at=R0 + 16384 * i + 12288) for i in range(2)]
    KR = [sb("kr%d" % i, [128, SEQ], BF16, at=R1 + 24576 + 4096 * i) for i in range(2)]
    PT = sb("pt", [128, 3, 2 * TB], BF16, at=R1)
    YA = sb("ya", [128, NH, SEQ], BF16, at=R2 + 8192)
    WQ = sb("wq", [128, 2, 1152], BF16, at=R3)
    WKV = sb("wkv", [128, 2, 1536], BF16, at=R3 + 4608)
    KPR = sb("kpr", [128, SEQ], BF16, at=R3 + 10752)
    SQPE = sb("sqpe", [128, SEQ], BF16, at=R3 + 14848)
    TK = sb("tk", [128, SEQ], BF16, at=R2 + 8192 + 5 * 4096)
    QRAW2 = sb("qraw2", [128, TB], F32, at=R2 + 8192 + 5 * 4096)
    SQ2 = sb("sq2", [128, TB], BF16, at=R2 + 8192 + 5 * 4096 + 2048)
    WO = sb("wo", [128, 8, 1024], BF16, at=R1 + 8192)
    XNA = sb("xna", [128, 8, SEQ], BF16, at=R0)
    W1B = sb("w1b", [128, 2, 8, 512], BF16, at=R1)
    W2B = sb("w2b", [128, 2, 4, 1024], BF16, at=R1 + 16384)
    HB = sb("hb", [128, 2, 4, TB], BF16, at=R2)
    POSI = sb("posi", [128, SEQ], I32, at=R0)
    ANG = sb("ang", [128, SEQ], F32, at=R0 + 8192)
    KQI = sb("kqi", [128, SEQ], I32, at=R0 + 16384)
    KQF = sb("kqf", [128, SEQ], F32, at=R0 + 24576)
    RR = sb("rr", [128, SEQ], F32, at=R1)
    MM_ = sb("mmk", [128, SEQ], F32, at=R1 + 8192)

    WQR = sb("wqr", [128, 2, NH, 128], BF16, at=R3 + 32768)
    PS = nc.alloc_psum_tensor("ps", [128, 4096], F32)

    def bank(i, p=128, n=TB):
        return PS[0:p, i * 512:i * 512 + n]

    rr = [0]

    def nb():
        i = rr[0]
        rr[0] = (i + 1) % 8
        return i

    def mm(out, lhsT, rhs, start, stop):
        P.op("pe", lambda e: e.matmul(out, lhsT, rhs, start=start, stop=stop),
             [lhsT, rhs], [out])

    def act(out, in_, func, scale=1.0, bias=None):
        rd = [in_]
        if bias is not None and not isinstance(bias, float):
            rd.append(bias)
        if not isinstance(scale, float):
            rd.append(scale)
        if bias is None:
            P.op("act", lambda e: e.activation(out, in_, func, scale=scale), rd, [out])
        else:
            P.op("act", lambda e: e.activation(out, in_, func, bias=bias, scale=scale), rd, [out])

    def tt(out, in0, in1, op, eng="dve"):
        P.op(eng, lambda e: e.tensor_tensor(out, in0, in1, op), [in0, in1], [out])

    def ts(out, in0, s1, s2, op0, op1=None, eng="dve"):
        rd = [in0] + [s for s in (s1, s2) if s is not None and not isinstance(s, float)]
        if op1 is None:
            P.op(eng, lambda e: e.tensor_scalar(out, in0, s1, None, op0), rd, [out])
        else:
            P.op(eng, lambda e: e.tensor_scalar(out, in0, s1, s2, op0, op1), rd, [out])

    def stt(out, in0, scalar, in1, op0, op1):
        rd = [in0, in1] + ([] if isinstance(scalar, float) else [scalar])
        P.op("dve", lambda e: e.scalar_tensor_tensor(out, in0, scalar, in1, op0, op1), rd, [out])

    def cp(out, in_, eng="dve"):
        P.op(eng, lambda e: e.tensor_copy(out, in_), [in_], [out])

    def recip(out, in_):
        P.op("dve", lambda e: e.reciprocal(out, in_), [in_], [out])

    def memset(ap, v, eng="dve"):
        P.op(eng, lambda e: e.memset(ap, v), [], [ap])

    def dump(nm, ap_sb, dst):
        if nm in dbg_d:
            P.dma("sp", dst, ap_sb)


    def rstd_from(ssb, dd, rst):
        act(rst, ssb, AF.Ln, scale=1.0 / dd, bias=CST[:, 5:6])
        act(rst, rst, AF.Exp, scale=-0.5)

    memset(ones[:, :], 1.0)
    P.dma("sp", G[:, :], gains_d)
    P.dma("sp", CST[:, :], cst_d)
    memset(RM[:, :], 0.0)
    P.dma("sp", RM[0:64, 0:64], rm_d)
    memset(cos2[:, :], 0.0)
    memset(sin2[:, :], 0.0)
    memset(WQR[:, :, :, :], 0.0)
    P.dma("sp", DFT64[:, :], dft64_d)

    def g_col(l, c, p=128):
        return G[0:p, l * 32 + c:l * 32 + c + 1]

    pairs = [(s_, l_) for s_ in range(nseq) for l_ in layers]
    P.dma("pool", WIN[:, :, :], win_d[layers[0]])
    for s in range(nseq):
        for c in range(8):
            P.dma("sp", xT[:, c, :], xT_d[s, :, c, :])
        P.dma("sp", POSI[0:64, :], pos_d[s])
        cp(ANG[0:64, :], POSI[0:64, :])
        ts(ANG[0:64, :], ANG[0:64, :], CST[0:64, 0:1], None, ALU.mult)
        ts(KQI[0:64, :], ANG[0:64, :], 1.0 / (2 * math.pi), None, ALU.mult)
        cp(KQF[0:64, :], KQI[0:64, :])
        C1 = 6.28125
        C2 = 2 * math.pi - C1
        stt(RR[0:64, :], KQF[0:64, :], -C1, ANG[0:64, :], ALU.mult, ALU.add)
        stt(RR[0:64, :], KQF[0:64, :], -C2, RR[0:64, :], ALU.mult, ALU.add)
        ts(MM_[0:64, :], RR[0:64, :], math.pi, -2 * math.pi, ALU.is_gt, ALU.mult)
        tt(RR[0:64, :], RR[0:64, :], MM_[0:64, :], ALU.add)
        ts(MM_[0:64, :], RR[0:64, :], -math.pi, 2 * math.pi, ALU.is_lt, ALU.mult)
        tt(RR[0:64, :], RR[0:64, :], MM_[0:64, :], ALU.add)
        PI_S = 3.1415925
        ts(RR[0:64, :], RR[0:64, :], -PI_S, PI_S, ALU.max, ALU.min)
        act(sin2[0:64, :], RR[0:64, :], AF.Sin)
        act(MM_[0:64, :], RR[0:64, :], AF.Abs)
        act(cos2[0:64, :], MM_[0:64, :], AF.Sin, scale=-1.0, bias=CST[0:64, 6:7])
        if s == 0 and "cos" in dbg_d:
            cp(ANG[0:64, :], cos2[0:64, :])
            P.dma("sp", dbg_d["cos"], ANG[0:64, :])
            cp(KQF[0:64, :], sin2[0:64, :])
            P.dma("sp", dbg_d["sin"], KQF[0:64, :])

        for l in layers:
          try:
            memset(KPE[64:128, :], 0.0, eng="pool")
            memset(WFBD[:, :, :], 0.0, eng="pool")
            for g in range(4):
                pr, hf = g // 2, g % 2
                P.dma("pool", WFBD[64 * hf:64 * hf + 64, pr, 64 * hf:64 * hf + 64], wf_d[l, g])
            for pr in range(2):
                b = nb()
                mm(bank(b, 128, 128), DFT64[:, 0:128], WFBD[:, pr, :], True, True)
                mm(PS[:, b * 512 + 128:b * 512 + 256], DFT64[:, 128:256], WFBD[:, pr, :], True, True)
                act(RHS1[:, pr, :], PS[:, b * 512:b * 512 + 256], AF.Copy)

            def xn_sq(tb):
                tsl = slice(tb * TB, (tb + 1) * TB)
                for c in range(8):
                    act(SQX[:, c, :], xT[:, c, tsl], AF.Square)

            def xn_red(tb):
                ssb = nb()
                for c in range(8):
                    mm(bank(ssb), ones[:, :], SQX[:, c, :], c == 0, c == 7)
                rstd_from(bank(ssb), 1024.0, RST_A[:, 0, :])

            def xn_apply(tb, c0, c1):
                tsl = slice(tb * TB, (tb + 1) * TB)
                for c in range(c0, c1):
                    stt(XN[:, tb % 2, c, :], xT[:, c, tsl], g_col(l, c), RST_A[:, 0, :], ALU.mult, ALU.mult)

            def xn_fin(tb):
                xn_red(tb)
                xn_apply(tb, 0, 8)

            def chain_fin(tb):
                tsl = slice(tb * TB, (tb + 1) * TB)
                for (dst, j0, gc, rst) in ((CQN, 0, 16, RST_A[:, 1, :]), (CKVN, 2, 18, RSTC1[:, :])):
                    sb2 = nb()
                    for j in range(2):
                        mm(bank(sb2), ones[:, :], SQC[:, j0 + j, :], j == 0, j == 1)
                    rstd_from(bank(sb2), 256.0, rst)
                    for j in range(2):
                        stt(dst[:, j, tsl], RAW4[:, j0 + j, :], g_col(l, gc + j), rst,
                            ALU.mult, ALU.mult)

            def p1_group(tb, gi):
                tsl = slice(tb * TB, (tb + 1) * TB)
                xb = tb % 2
                b = nb()
                m = 128 if gi < 6 else 64
                for kc in range(8):
                    mm(bank(b, m), WIN[:, kc, gi * 128:gi * 128 + m], XN[:, xb, kc, :], kc == 0, kc == 7)
                if gi < 2:
                    act(ZF[:, gi, tsl], bank(b), AF.Copy)
                elif gi < 6:
                    cp(RAW4[:, gi - 2, :], bank(b))
                    act(SQC[:, gi - 2, :], RAW4[:, gi - 2, :], AF.Square)
                else:
                    act(KPE[0:64, tsl], bank(b, 64), AF.Copy)

            xn_sq(0)
            xn_fin(0)
            for tb in range(NTB):
                if tb + 1 < NTB:
                    xn_sq(tb + 1)
                p1_group(tb, 0)
                p1_group(tb, 1)
                if tb > 0:
                    chain_fin(tb - 1)
                p1_group(tb, 2)
                p1_group(tb, 3)
                if tb + 1 < NTB:
                    xn_red(tb + 1)
                p1_group(tb, 4)
                if tb + 1 < NTB:
                    xn_apply(tb + 1, 0, 4)
                p1_group(tb, 5)
                if tb + 1 < NTB:
                    xn_apply(tb + 1, 4, 8)
                p1_group(tb, 6)
            chain_fin(NTB - 1)
            if s == 0 and l == layers[0] and "cqn" in dbg_d:
                for j in range(2):
                    cp(ANG[:, :], CQN[:, j, :])
                    P.dma("sp", dbg_d["cqn"][j], ANG[:, :])
                    cp(ANG[:, :], ZF[:, j, :])
                    P.dma("sp", dbg_d["zf"][j], ANG[:, :])

            P.dma("sp", C0[:, :, :], c0_d)
            P.dma("sp", NS0[:, :, :], ns0_d)
            for c in range(16):
                b = nb()
                for pr in range(2):
                    mm(PS[:, b * 512 + 256 * pr:b * 512 + 256 * pr + 256],
                       ZF[:, pr, c * 128:(c + 1) * 128], RHS1[:, pr, :], True, True)
                if c % 2 == 0:
                    act(PQ0[:, c, :], bank(b), AF.Copy)
                else:
                    cp(PQ0[:, c, :], bank(b))

            def pq_view(t, half):
                return t[:, :, :].rearrange("p c (r h e) -> p (c r) h e", r=2, h=2)[:, :, half, :]

            sgn2, c1c, s1c, ns1c = CST[:, 1:2], CST[:, 2:3], CST[:, 3:4], CST[:, 4:5]
            for j in (0, 2, 1, 3):
                if j == 0:
                    src = PQ0
                elif j == 2:
                    ts(PQT[:, :, :], PQ0[:, :, :], sgn2, None, ALU.mult)
                    src = PQT
                else:
                    if j == 1:
                        ts(T1[:, :, :], PQ0[:, :, :], c1c, None, ALU.mult)
                    sa, sb_ = (ns1c, s1c) if j == 1 else (s1c, ns1c)
                    stt(pq_view(PQT, 0), pq_view(PQ0, 1), sa, pq_view(T1, 0), ALU.mult, ALU.add)
                    stt(pq_view(PQT, 1), pq_view(PQ0, 0), sb_, pq_view(T1, 1), ALU.mult, ALU.add)
                    src = PQT
                tsl = slice(j * TB, (j + 1) * TB)
                fb = []
                for fc in range(2):
                    b = nb()
                    fb.append(b)
                    for c in range(16):
                        mm(bank(b), src[:, c, 256 * fc:256 * fc + 128], C0[:, c, :], c == 0, False)
                        mm(bank(b), src[:, c, 256 * fc + 128:256 * fc + 256], NS0[:, c, :], False, c == 15)
                sb2 = nb()
                for fc in range(2):
                    act(SQ_A[:, fc, :], bank(fb[fc]), AF.Square)
                    mm(bank(sb2), ones[:, :], SQ_A[:, fc, :], fc == 0, fc == 1)
                rstd_from(bank(sb2), 256.0, RST_A[:, 0, :])
                for fc in range(2):
                    stt(YFN[:, fc, tsl], bank(fb[fc]), g_col(l, 24 + fc), RST_A[:, 0, :], ALU.mult, ALU.mult)

            P.dma("pool", WQ[:, :, :], wq_d[l])
            P.dma("pool", WKV[:, :, :], wkv_d[l])
            P.dma("pool", WQR[:, :, :, 0:64],
                  wq_d[l].rearrange("p k (h e) -> p k h e", h=NH)[:, :, :, 128:192])
            ts(TK[:, :], KPE[:, :], g_col(l, 23), None, ALU.mult)
            for tb in range(NTB):
                tsl = slice(tb * TB, (tb + 1) * TB)
                act(SQPE[:, tsl], KPE[:, tsl], AF.Square)
                b = nb()
                mm(bank(b), RM[:, :], TK[:, tsl], True, True)
                tt(TMP_B[:, 0, :], TK[:, tsl], cos2[:, tsl], ALU.mult)
                tt(TMP_B[:, 1, :], bank(b), sin2[:, tsl], ALU.mult)
                tt(KPR[:, tsl], TMP_B[:, 0, :], TMP_B[:, 1, :], ALU.add)

            scale = 1.0 / math.sqrt(192.0)
            pb = [0]

            def nbp():
                if not NBP:
                    return nb()
                pb[0] = (pb[0] + 1) % len(NBP_POOL)
                return NBP_POOL[pb[0]]

            def proj_gen(h):
                hp = h % 2
                for tb in range(NTB):
                    tsl = slice(tb * TB, (tb + 1) * TB)
                    qa, qb = nbp(), nbp()
                    for kc in range(2):
                        mm(bank(qa), WQ[:, kc, h * 192:h * 192 + 128], CQN[:, kc, tsl], kc == 0, kc == 1)
                    for kc in range(2):
                        mm(bank(qb), WQR[:, kc, h, :], CQN[:, kc, tsl], kc == 0, kc == 1)
                    act(SQ_B[:, 0, :], bank(qa), AF.Square)
                    act(SQ_B[:, 1, :], bank(qb), AF.Square)
                    cp(QRAW[:, 0, :], bank(qa))
                    cp(QRAW[:, 1, :], bank(qb))
                    yield
                    ka = nbp()
                    for kc in range(2):
                        mm(bank(ka), WKV[:, kc, h * 256:h * 256 + 128], CKVN[:, kc, tsl], kc == 0, kc == 1)
                    act(SQ2[:, :], bank(ka), AF.Square)
                    cp(QRAW2[:, :], bank(ka))
                    yield
                    ssb = nbp()
                    mm(bank(ssb), ones[:, :], SQ_B[:, 0, :], True, False)
                    mm(bank(ssb), ones[:, :], SQ_B[:, 1, :], False, True)
                    rstd_from(bank(ssb), 192.0, RST_B[:, 0, :])
                    stt(TMP_B[:, 0, :], QRAW[:, 1, :], g_col(l, 21), RST_B[:, 0, :], ALU.mult, ALU.mult)
                    stt(QN[hp][:, tsl], QRAW[:, 0, :], g_col(l, 20), RST_B[:, 0, :], ALU.mult, ALU.mult)
                    yield
                    ssk = nbp()
                    mm(bank(ssk), ones[:, :], SQ2[:, :], True, False)
                    mm(bank(ssk), ones[:, :], SQPE[:, tsl], False, True)
                    rstd_from(bank(ssk), 192.0, RST_B[:, 1, :])
                    stt(KN[hp][:, tsl], QRAW2[:, :], g_col(l, 22), RST_B[:, 1, :], ALU.mult, ALU.mult)
                    tt(KR[hp][:, tsl], KPR[:, tsl], RST_B[:, 1, :], ALU.mult)
                    yield
                    swb = nbp()
                    mm(bank(swb), RM[:, :], TMP_B[:, 0, :], True, True)
                    tt(TMP_B[:, 1, :], TMP_B[:, 0, :], cos2[:, tsl], ALU.mult)
                    tt(TMP_B[:, 2, :], bank(swb), sin2[:, tsl], ALU.mult)
                    tt(QR[hp][:, tsl], TMP_B[:, 1, :], TMP_B[:, 2, :], ALU.add)
                    yield
                    vb = nbp()
                    for ci in range(4):
                        c = tb * 4 + ci
                        for kc in range(2):
                            mm(PS[:, vb * 512 + ci * 128:vb * 512 + ci * 128 + 128],
                               CKVN[:, kc, c * 128:(c + 1) * 128],
                               WKV[:, kc, h * 256 + 128:h * 256 + 256], kc == 0, kc == 1)
                    cp(VH[hp][:, tb * 4:tb * 4 + 4, :], bank(vb))
                    yield

            if STOP == "p2":
                raise _Stop()
            if STOP and STOP.startswith("st"):
                for i_, _ in enumerate(proj_gen(0)):
                    if i_ + 1 >= int(STOP[2:]):
                        break
                raise _Stop()
            for _ in proj_gen(0):
                pass
            if STOP == "proj0":
                raise _Stop()
            for h in range(NH):
                hp = h % 2
                gen = proj_gen(h + 1) if h + 1 < NH else None
                if gen is not None and not PIPE:
                    for _ in gen:
                        pass
                    gen = None
                for qb_ in range(NTB):
                    qsl = slice(qb_ * TB, (qb_ + 1) * TB)
                    OB, DB = 4, 5

                    def qk(g):
                        for ci in range(2):
                            c = 2 * g + ci
                            bk = (g % 2) * 2 + ci
                            mm(bank(bk), KN[hp][:, c * 128:(c + 1) * 128], QN[hp][:, qsl], True, False)
                            mm(bank(bk), KR[hp][:, c * 128:(c + 1) * 128], QR[hp][:, qsl], False, True)

                    def ex(g):
                        bk = (g % 2) * 2
                        ptb = g % 3
                        act(PT[:, ptb, :], PS[:, bk * 512:bk * 512 + 1024], AF.Exp, scale=scale)

                    def pv(g):
                        ptb = g % 3
                        for ci in range(2):
                            c = 2 * g + ci
                            mm(bank(OB), VH[hp][:, c, :], PT[:, ptb, ci * TB:(ci + 1) * TB], c == 0, c == 15)
                            mm(bank(DB), ones[:, :], PT[:, ptb, ci * TB:(ci + 1) * TB], c == 0, c == 15)

                    qk(0)
                    for g in range(8):
                        if g + 1 < 8:
                            qk(g + 1)
                        ex(g)
                        if gen is not None:
                            next(gen, None)
                        pv(g)
                    act(TMPF_B[:, :], bank(DB), AF.Ln)
                    act(TMPF_B[:, :], TMPF_B[:, :], AF.Exp, scale=-1.0)
                    tt(YA[:, h, qsl], bank(OB), TMPF_B[:, :], ALU.mult)
                if gen is not None:
                    for _ in gen:
                        pass
                if STOP == "attn0":
                    raise _Stop()
            if STOP == "p3":
                raise _Stop()

            P.dma("pool", WO[:, :, :], wo_d[l])
            def ya_sq(tb):
                tsl = slice(tb * TB, (tb + 1) * TB)
                for h in range(NH):
                    act(SQY[:, h, :], YA[:, h, tsl], AF.Square)

            def ya_fin(tb):
                tsl = slice(tb * TB, (tb + 1) * TB)
                ssb = nb()
                for h in range(NH):
                    mm(bank(ssb), ones[:, :], SQY[:, h, :], h == 0, h == NH - 1)
                rstd_from(bank(ssb), 768.0, RST_B[:, 0, :])
                for h in range(NH):
                    stt(YA[:, h, tsl], YA[:, h, tsl], g_col(l, 26 + h), RST_B[:, 0, :], ALU.mult, ALU.mult)

            def mlp_sq(tb):
                tsl = slice(tb * TB, (tb + 1) * TB)
                for c in range(8):
                    act(SQM[:, c, :], xT[:, c, tsl], AF.Square)

            def mlp_fin(tb):
                tsl = slice(tb * TB, (tb + 1) * TB)
                ssb = nb()
                for c in range(8):
                    mm(bank(ssb), ones[:, :], SQM[:, c, :], c == 0, c == 7)
                rstd_from(bank(ssb), 1024.0, RST_B[:, 1, :])
                for c in range(8):
                    stt(XNA[:, c, tsl], xT[:, c, tsl], g_col(l, 8 + c), RST_B[:, 1, :], ALU.mult, ALU.mult)

            def wout_group(tb, o):
                tsl = slice(tb * TB, (tb + 1) * TB)
                b = nb()
                for kc in range(8):
                    rhs = YFN[:, kc, tsl] if kc < 2 else YA[:, kc - 2, tsl]
                    mm(bank(b), WO[:, kc, o * 128:(o + 1) * 128], rhs, kc == 0, kc == 7)
                tt(xT[:, o, tsl], bank(b), xT[:, o, tsl], ALU.add)

            def load_w(e):
                if e == 0:
                    P.dma("pool", W1X[:, :, :], w1_d[l, :, :, 0:512])
                    P.dma("pool", W2X[:, :, :], w2_d[l, :, 0:4, :])
                else:
                    P.dma("pool", W1B[:, e % 2, :, :], w1_d[l, :, :, e * 512:(e + 1) * 512])
                    P.dma("pool", W2B[:, e % 2, :, :], w2_d[l, :, e * 4:(e + 1) * 4, :])

            load_w(0)

            ya_sq(0)
            ya_fin(0)
            for tb in range(NTB):
                if tb + 1 < NTB:
                    ya_sq(tb + 1)
                for o in range(4):
                    wout_group(tb, o)
                if tb > 0:
                    mlp_fin(tb - 1)
                if tb + 1 < NTB:
                    ya_fin(tb + 1)
                for o in range(4, 8):
                    wout_group(tb, o)
                mlp_sq(tb)
            mlp_fin(NTB - 1)

            ip = pairs.index((s, l))
            if ip + 1 < len(pairs):
                P.dma("pool", WIN[:, :, :], win_d[pairs[ip + 1][1]])

            def up(e, tb, hb):
                tsl = slice(tb * TB, (tb + 1) * TB)
                for jj in range(4):
                    b = nb()
                    for kc in range(8):
                        w1 = W1X[:, kc, jj * 128:(jj + 1) * 128] if e == 0 else W1B[:, e % 2, kc, jj * 128:(jj + 1) * 128]
                        mm(bank(b), w1, XNA[:, kc, tsl], kc == 0, kc == 7)
                    act(SQ_A[:, jj % 2, :], bank(b), AF.Square)
                    stt(HB[:, hb, jj, :], bank(b), 0.0, SQ_A[:, jj % 2, :], ALU.is_gt, ALU.mult)

            def down(e, tb, hb):
                tsl = slice(tb * TB, (tb + 1) * TB)
                for o in range(8):
                    b = nb()
                    for jj in range(4):
                        w2 = W2X[:, jj, o * 128:(o + 1) * 128] if e == 0 else W2B[:, e % 2, jj, o * 128:(o + 1) * 128]
                        mm(bank(b), w2, HB[:, hb, jj, :], jj == 0, jj == 3)
                    tt(xT[:, o, tsl], bank(b), xT[:, o, tsl], ALU.add)

            steps = [(e, tb) for e in range(8) for tb in range(NTB)]
            up(0, 0, 0)
            for i, (e, tb) in enumerate(steps):
                if tb == 0 and e + 1 < 8:
                    load_w(e + 1)
                if i + 1 < len(steps):
                    e2, tb2 = steps[i + 1]
                    up(e2, tb2, (i + 1) % 2)
                down(e, tb, i % 2)

          except _Stop:
            pass
        for c in range(8):
            P.dma("sp", yT_d[s, :, c, :], xT[:, c, :])

    P.emit()
    return nc


def _pmajor(w, kchunks):
    K, N = w.shape
    return np.ascontiguousarray(w.reshape(kchunks, 128, N).transpose(1, 0, 2))


def _consts():
    bf = ml_dtypes.bfloat16
    half = 32
    inv_freq = (10000.0 ** (-np.arange(half, dtype=np.float32) / half)).astype(np.float32)
    cst = np.zeros((128, 8), np.float32)
    p = np.arange(128)
    cst[:64, 0] = inv_freq[p[:64] % 32]
    cst[:, 1] = np.where(p % 2 == 0, 1.0, -1.0)
    cst[:, 2] = np.array([1.0, 0.0, -1.0, 0.0])[p % 4]
    cst[:, 3] = np.array([0.0, 1.0, 0.0, -1.0])[p % 4]
    cst[:, 4] = -cst[:, 3]
    cst[:, 5] = EPS
    cst[:, 6] = 1.5707963
    rm = np.zeros((64, 64), np.float32)
    for m in range(32):
        rm[m + 32, m] = -1.0
        rm[m, m + 32] = 1.0
    k = np.arange(64)
    a64 = 2 * np.pi * np.outer(k, k) / 64.0
    cc = np.cos(a64) / 8.0
    sc = np.sin(a64) / 8.0
    dft64 = np.zeros((128, 256), np.float64)
    dft64[0:64, 0:64] = cc
    dft64[64:128, 64:128] = cc
    dft64[0:64, 128:192] = sc
    dft64[64:128, 192:256] = sc
    sidx = np.arange(SEQ, dtype=np.int64)
    r = np.arange(512, dtype=np.int64)
    ph = (np.outer(sidx, r) % SEQ).astype(np.float64) * (2 * np.pi / SEQ)
    c0 = np.cos(ph) / math.sqrt(SEQ)
    ns0 = -np.sin(ph) / math.sqrt(SEQ)
    c0 = np.ascontiguousarray(c0.reshape(16, 128, 512).transpose(1, 0, 2))
    ns0 = np.ascontiguousarray(ns0.reshape(16, 128, 512).transpose(1, 0, 2))
    return dict(cst=cst, rm=rm.astype(bf), dft64=dft64.astype(np.float32).astype(bf),
                c0=c0.astype(np.float32).astype(bf), ns0=ns0.astype(np.float32).astype(bf))


def _gains(inp):
    g = np.zeros((128, 64), np.float32)

    def cols(v):
        return np.asarray(v, np.float32).reshape(-1, 128).T

    for l in range(DEPTH):
        o = l * 32
        g[:, o + 0:o + 8] = cols(inp["attn_norm_g"][l])
        g[:, o + 8:o + 16] = cols(inp["mlp_norm_g"][l])
        g[:, o + 16:o + 18] = cols(inp["q_a_g"][l])
        g[:, o + 18:o + 20] = cols(inp["kv_a_g"][l])
        g[:, o + 20] = inp["q_norm_g"][l][:128]
        g[:64, o + 21] = inp["q_norm_g"][l][128:]
        g[:, o + 22] = inp["k_norm_g"][l][:128]
        g[:64, o + 23] = inp["k_norm_g"][l][128:]
        g[:, o + 24:o + 26] = cols(inp["fourier_out_g"][l])
        g[:, o + 26:o + 32] = cols(inp["attn_out_g"][l])
    return g


def _shared_inputs(inp):
    f = lambda a: np.asarray(a, np.float32)
    sh = dict(_consts())
    sh["gains"] = _gains({k: np.asarray(v) for k, v in inp.items()})
    sh["w_in"] = np.stack([_pmajor(f(inp["w_in"][l]), 8) for l in range(DEPTH)])
    sh["w_fourier"] = np.ascontiguousarray(f(inp["w_fourier"]))
    sh["w_q_up"] = np.stack([_pmajor(f(inp["w_q_up"][l]), 2) for l in range(DEPTH)])
    sh["w_kv_up"] = np.stack([_pmajor(f(inp["w_kv_up"][l]), 2) for l in range(DEPTH)])
    sh["w_out"] = np.stack([_pmajor(f(inp["w_out"][l]), 8) for l in range(DEPTH)])
    sh["w_mlp_in"] = np.stack([_pmajor(f(inp["w_mlp_in"][l]), 8) for l in range(DEPTH)])
    sh["w_mlp_out"] = np.stack([_pmajor(f(inp["w_mlp_out"][l]), 32) for l in range(DEPTH)])
    return sh


def _x_to_dev(x):
    xt = np.asarray(x, np.float32).transpose(0, 2, 1).reshape(BATCH, 8, 128, SEQ).transpose(0, 2, 1, 3)
    return [np.ascontiguousarray(xt[i * NSEQ:(i + 1) * NSEQ]) for i in range(NCORES)]


def _y_from_dev(ys):
    yt = np.concatenate(ys, axis=0)
    return np.ascontiguousarray(yt.transpose(0, 2, 1, 3).reshape(BATCH, D, SEQ).transpose(0, 2, 1))


_NC_CACHE = {}


def kernel(**inputs):
    sh = _shared_inputs(inputs)
    pos = np.asarray(inputs["positions"], np.int32)
    posr = np.ascontiguousarray(np.broadcast_to(pos[:, None, :], (BATCH, 64, SEQ)))
    xs = _x_to_dev(inputs["x"])
    if "full" not in _NC_CACHE:
        _NC_CACHE["full"] = build_nc(layers=(0, 1))
    nc = _NC_CACHE["full"]
    in_maps = []
    for i in range(NCORES):
        m = dict(sh)
        m["xT"] = xs[i]
        m["pos"] = np.ascontiguousarray(posr[i * NSEQ:(i + 1) * NSEQ])
        in_maps.append(m)
    res = run_bass_kernel_spmd(nc, in_maps, core_ids=list(range(NCORES)))
    return _y_from_dev([np.asarray(r["yT"]) for r in res.results])
```

```python
import math
from bisect import bisect_left
from contextlib import ExitStack

import numpy as np
import ml_dtypes

import concourse.bass as bass
import concourse.mybir as mybir
from concourse.bass_utils import run_bass_kernel_spmd

F32 = mybir.dt.float32
BF16 = mybir.dt.bfloat16
I32 = mybir.dt.int32
ALU = mybir.AluOpType
AF = mybir.ActivationFunctionType

NCORES = 8
BATCH, SEQ, D = 16, 2048, 1024
DEPTH = 2
NSEQ = BATCH // NCORES
NH = 6
DFF = 4096
EPS = 1e-6
TB = 512
NTB = SEQ // TB
GRAN = 256
PIPE = True
P1PIPE = True
P4PIPE = True
HOIST = True
NBP = True
SKIPB = False
SPLIT_DMA_SEMS = True
VACT = False
NOV = False
STOP = None


class _Stop(Exception):
    pass
NBP_POOL = (6, 7)
_DS = {F32: 4, BF16: 2, I32: 4}


class Prog:
    ENG = ("pe", "act", "dve", "pool", "sp")

    def __init__(self, nc, ndma=20):
        self.nc = nc
        self.stream = {e: [] for e in self.ENG}
        self.npos = {e: 0 for e in self.ENG}
        self.ops = {e: [] for e in self.ENG}
        self.ndma = ndma
        self.dmacnt = [0] * ndma
        self.dmarr = {"sp": 0, "pool": 0}
        ctrs = list(self.ENG) + [("d", i) for i in range(ndma)]
        self.ctrs = ctrs
        self.seen = {e: {c: 0 for c in ctrs} for e in self.ENG}
        self.lastw = {}
        self.readers = {}

    @staticmethod
    def keys(ap):
        t = ap.tensor
        tn = type(t).__name__
        if tn.startswith("DRam"):
            return ()
        shape = list(t.shape)
        pstride = 1
        for s in shape[1:]:
            pstride *= int(s)
        off = int(ap.offset)
        p0 = off // pstride
        e0 = off % pstride
        dims = [(int(s), int(c)) for s, c in ap.ap]
        pc = dims[0][1]
        ext = 1
        for s, c in dims[1:]:
            ext += (c - 1) * abs(s)
        ds = _DS[ap.dtype]
        if tn.startswith("SB"):
            base = int(t.manual_sbuf_range[0])
            sp = 0
        else:
            base = 0
            sp = 1
        lo = base + e0 * ds
        hi = lo + ext * ds
        q0 = p0 // 32
        q1 = (p0 + pc - 1) // 32
        gr = 2048 if sp else GRAN
        return [(sp, g, q) for g in range(lo // gr, (hi - 1) // gr + 1)
                for q in range(q0, q1 + 1)]

    def _deps(self, rkeys, wkeys, eng=None):
        need = {}
        lastw, readers = self.lastw, self.readers
        for k in rkeys:
            w = lastw.get(k)
            if w is not None and need.get(w[0], 0) < w[1]:
                need[w[0]] = w[1]
            if k[0] == 1:
                r = readers.get(k)
                if r:
                    for c, p in r.items():
                        if c != eng and need.get(c, 0) < p:
                            need[c] = p
        for k in wkeys:
            w = lastw.get(k)
            if w is not None and need.get(w[0], 0) < w[1]:
                need[w[0]] = w[1]
            r = readers.get(k)
            if r:
                for c, p in r.items():
                    if need.get(c, 0) < p:
                        need[c] = p
        return need

    def _waits(self, eng, need):
        seen = self.seen[eng]
        for ctr, pos in need.items():
            if ctr == eng and eng == "pe":
                continue
            if seen[ctr] < pos:
                if not isinstance(ctr, tuple):
                    self.ops[ctr][pos - 1][2] = True
                self.stream[eng].append(["wait", ctr, pos])
                seen[ctr] = pos

    def _record(self, ctr, pos, rkeys, wkeys):
        lastw, readers = self.lastw, self.readers
        for k in rkeys:
            r = readers.get(k)
            if r is None:
                readers[k] = {ctr: pos}
            else:
                r[ctr] = pos
        for k in wkeys:
            lastw[k] = (ctr, pos)
            if k in readers:
                readers[k] = {}

    def op(self, eng, fn, reads, writes):
        rk = [k for a in reads for k in self.keys(a)]
        wk = [k for a in writes for k in self.keys(a)]
        self._waits(eng, self._deps(rk, wk, eng))
        ent = ["op", fn, False]
        self.stream[eng].append(ent)
        self.ops[eng].append(ent)
        self.npos[eng] += 1
        self._record(eng, self.npos[eng], rk, wk)

    def dma(self, queue, out, in_):
        rk = self.keys(in_)
        wk = self.keys(out)
        if SPLIT_DMA_SEMS:
            half = self.ndma // 2
            k = self.dmarr[queue]
            self.dmarr[queue] = (k + 1) % half
            j = k if queue == "sp" else half + k
        else:
            j = self.dmarr["sp"]
            self.dmarr["sp"] = (j + 1) % self.ndma
        n = self.dmacnt[j] + 1
        need = self._deps(rk, wk)
        if n > 1:
            need[("d", j)] = max(need.get(("d", j), 0), n - 1)
        self._waits(queue, need)
        self.stream[queue].append(["dma", (out, in_), j])
        self.dmacnt[j] = n
        self._record(("d", j), n, rk, wk)

    def emit(self):
        nc = self.nc
        for j in range(self.ndma):
            if self.dmacnt[j]:
                self.stream["sp"].append(["wait", ("d", j), self.dmacnt[j]])
        rank = {}
        for e in self.ENG:
            r, rk = 0, []
            for ent in self.ops[e]:
                if ent[2]:
                    r += 1
                rk.append(r)
            rank[e] = rk
        with ExitStack() as st:
            sem = {}
            for c in self.ctrs:
                nm = c if isinstance(c, str) else "dq%d" % c[1]
                sem[c] = st.enter_context(nc.semaphore("s_" + nm))
            block = st.enter_context(nc.Block())

            def mk(e):
                def body(eng):
                    for ent in self.stream[e]:
                        if ent[0] == "wait":
                            c, p = ent[1], ent[2]
                            v = 16 * p if isinstance(c, tuple) else rank[c][p - 1]
                            eng.wait_ge(sem[c], v)
                        elif ent[0] == "op":
                            ins = ent[1](eng)
                            if ent[2]:
                                ins.then_inc(sem[e], 1)
                        else:
                            o, i = ent[1]
                            eng.dma_start(out=o, in_=i).then_inc(sem[("d", ent[2])], 16)
                return body

            block.tensor(mk("pe"))
            block.scalar(mk("act"))
            block.vector(mk("dve"))
            block.gpsimd(mk("pool"))
            block.sync(mk("sp"))


def build_nc(layers=(0, 1), nseq=NSEQ, dbg=None):
    nc = bass.Bass("TRN2", target_bir_lowering=False)
    P = Prog(nc)

    def din(name, shape, dt):
        return nc.dram_tensor(name, list(shape), dt, kind="ExternalInput").ap()

    xT_d = din("xT", [NSEQ, 128, 8, SEQ], F32)
    pos_d = din("pos", [NSEQ, 64, SEQ], I32)
    gains_d = din("gains", [128, 64], F32)
    cst_d = din("cst", [128, 8], F32)
    rm_d = din("rm", [64, 64], BF16)
    dft64_d = din("dft64", [128, 256], BF16)
    c0_d = din("c0", [128, 16, 512], BF16)
    ns0_d = din("ns0", [128, 16, 512], BF16)
    win_d = din("w_in", [DEPTH, 128, 8, 832], F32)
    wf_d = din("w_fourier", [DEPTH, 4, 64, 64], F32)
    wq_d = din("w_q_up", [DEPTH, 128, 2, 1152], F32)
    wkv_d = din("w_kv_up", [DEPTH, 128, 2, 1536], F32)
    wo_d = din("w_out", [DEPTH, 128, 8, 1024], F32)
    w1_d = din("w_mlp_in", [DEPTH, 128, 8, DFF], F32)
    w2_d = din("w_mlp_out", [DEPTH, 128, 32, 1024], F32)
    yT_d = nc.dram_tensor("yT", [NSEQ, 128, 8, SEQ], F32, kind="ExternalOutput").ap()
    dbg_d = {}
    if dbg:
        for nm, shp in dbg.items():
            dbg_d[nm] = nc.dram_tensor("dbg_" + nm, list(shp), F32, kind="ExternalOutput").ap()

    cur = [16512]

    def sb(name, shape, dt, at=None):
        n = _DS[dt]
        for s in shape[1:]:
            n *= s
        if at is None:
            at = cur[0]
            cur[0] = (at + n + 31) // 32 * 32
        return nc.alloc_sbuf_tensor_at(name, list(shape), dt, offset=at)

    xT = sb("xT", [128, 8, SEQ], F32)
    cos2 = sb("cos2", [128, SEQ], BF16)
    sin2 = sb("sin2", [128, SEQ], BF16)
    ones = sb("ones", [128, 128], BF16)
    G = sb("gains", [128, 64], F32)
    CST = sb("cst", [128, 8], F32)
    RM = sb("rm", [128, 128], BF16)
    WFBD = sb("wfbd", [128, 2, 128], BF16)
    RHS1 = sb("rhs1", [128, 2, 256], BF16)
    DFT64 = sb("dft64", [128, 256], BF16)
    A0 = cur[0]
    R0, R1, R2, R3 = A0, A0 + 32768, A0 + 65536, A0 + 98304
    assert R3 + 32768 + 3072 <= 229344, R3

    WIN = sb("win", [128, 8, 832], BF16, at=R2 + 8192)
    XN = sb("xn", [128, 2, 8, TB], BF16, at=R0 + 13312)
    ZF = sb("zf", [128, 2, SEQ], BF16, at=R1)
    CQN = sb("cqn", [128, 2, SEQ], BF16, at=R1 + 8192)
    CKVN = sb("ckvn", [128, 2, SEQ], BF16, at=R1 + 16384)
    KPE = sb("kpe", [128, SEQ], F32, at=R1 + 24576)
    YFN = sb("yfn", [128, 2, SEQ], BF16, at=R2)
    SQ_A = sb("sqA", [128, 2, TB], BF16, at=R2 + 24576)
    RST_A = sb("rstA", [128, 2, TB], F32, at=R2 + 24576 + 2048)
    TMP_A = sb("tmpA", [128, 2, TB], BF16, at=R2 + 24576 + 2048 + 4096)
    SQ_B = sb("sqB", [128, 2, TB], BF16, at=R1 + 6144)
    RST_B = sb("rstB", [128, 2, TB], F32, at=R3 + 18944)
    TMP_B = sb("tmpB", [128, 3, TB], BF16, at=R3 + 18944 + 4096)
    TMPF_B = sb("tmpfB", [128, TB], F32, at=R3 + 18944 + 4096 + 3072)
    QRAW = sb("qraw", [128, 2, TB], F32, at=R3 + 18944 + 4096 + 3072 + 2048)
    SQX = sb("sqx", [128, 8, TB], BF16, at=R2)
    RAW4 = sb("raw4", [128, 4, TB], F32, at=R0)
    SQC = sb("sqc", [128, 4, TB], BF16, at=R0 + 8192)
    RSTC1 = sb("rstc1", [128, TB], F32, at=R2 + 24576 + 2048 + 4096)
    SQY = sb("sqy", [128, NH, TB], BF16, at=R1)
    SQM = sb("sqm", [128, 8, TB], BF16, at=R1 + 24576)
    W1X = sb("w1x", [128, 8, 512], BF16, at=R3)
    W2X = sb("w2x", [128, 4, 1024], BF16, at=R3 + 8192)
    PQ0 = sb("pq0", [128, 16, 512], BF16, at=R0)
    PQT = sb("pqt", [128, 16, 512], BF16, at=R0 + 16384)
    T1 = sb("t1", [128, 16, 512], BF16, at=R2 + 8192)
    C0 = sb("c0", [128, 16, 512], BF16, at=R3)
    NS0 = sb("ns0", [128, 16, 512], BF16, at=R3 + 16384)
    QN = [sb("qn%d" % i, [128, SEQ], BF16, at=R0 + 16384 * i) for i in range(2)]
    QR = [sb("qr%d" % i, [128, SEQ], BF16, at=R0 + 16384 * i + 4096) for i in range(2)]
    KN = [sb("kn%d" % i, [128, SEQ], BF16, at=R0 + 16384 * i + 8192) for i in range(2)]
    VH = [sb("vh%d" % i, [128, 16, 128], BF16, at=R0 + 16384 * i + 12288) for i in range(2)]
    KR = [sb("kr%d" % i, [128, SEQ], BF16, at=R1 + 24576 + 4096 * i) for i in range(2)]
    PT = sb("pt", [128, 3, 2 * TB], BF16, at=R1)
    YA = sb("ya", [128, NH, SEQ], BF16, at=R2 + 8192)
    WQ = sb("wq", [128, 2, 1152], BF16, at=R3)
    WKV = sb("wkv", [128, 2, 1536], BF16, at=R3 + 4608)
    KPR = sb("kpr", [128, SEQ], BF16, at=R3 + 10752)
    SQPE = sb("sqpe", [128, SEQ], BF16, at=R3 + 14848)
    TK = sb("tk", [128, SEQ], BF16, at=R2 + 8192 + 5 * 4096)
    QRAW2 = sb("qraw2", [128, TB], F32, at=R2 + 8192 + 5 * 4096)
    SQ2 = sb("sq2", [128, TB], BF16, at=R2 + 8192 + 5 * 4096 + 2048)
    WO = sb("wo", [128, 8, 1024], BF16, at=R1 + 8192)
    XNA = sb("xna", [128, 8, SEQ], BF16, at=R0)
    W1B = sb("w1b", [128, 2, 8, 512], BF16, at=R1)
    W2B = sb("w2b", [128, 2, 4, 1024], BF16, at=R1 + 16384)
    HB = sb("hb", [128, 2, 4, TB], BF16, at=R2)
    POSI = sb("posi", [128, SEQ], I32, at=R0)
    ANG = sb("ang", [128, SEQ], F32, at=R0 + 8192)
    KQI = sb("kqi", [128, SEQ], I32, at=R0 + 16384)
    KQF = sb("kqf", [128, SEQ], F32, at=R0 + 24576)
    RR = sb("rr", [128, SEQ], F32, at=R1)
    MM_ = sb("mmk", [128, SEQ], F32, at=R1 + 8192)

    WQR = sb("wqr", [128, 2, NH, 128], BF16, at=R3 + 32768)
    PS = nc.alloc_psum_tensor("ps", [128, 4096], F32)

    def bank(i, p=128, n=TB):
        return PS[0:p, i * 512:i * 512 + n]

    rr = [0]

    def nb():
        i = rr[0]
        rr[0] = (i + 1) % 8
        return i

    def mm(out, lhsT, rhs, start, stop):
        P.op("pe", lambda e: e.matmul(out, lhsT, rhs, start=start, stop=stop),
             [lhsT, rhs], [out])

    def act(out, in_, func, scale=1.0, bias=None):
        rd = [in_]
        if bias is not None and not isinstance(bias, float):
            rd.append(bias)
        if not isinstance(scale, float):
            rd.append(scale)
        if bias is None:
            P.op("act", lambda e: e.activation(out, in_, func, scale=scale), rd, [out])
        else:
            P.op("act", lambda e: e.activation(out, in_, func, bias=bias, scale=scale), rd, [out])

    def tt(out, in0, in1, op, eng="dve"):
        P.op(eng, lambda e: e.tensor_tensor(out, in0, in1, op), [in0, in1], [out])

    def ts(out, in0, s1, s2, op0, op1=None, eng="dve"):
        rd = [in0] + [s for s in (s1, s2) if s is not None and not isinstance(s, float)]
        if op1 is None:
            P.op(eng, lambda e: e.tensor_scalar(out, in0, s1, None, op0), rd, [out])
        else:
            P.op(eng, lambda e: e.tensor_scalar(out, in0, s1, s2, op0, op1), rd, [out])

    def stt(out, in0, scalar, in1, op0, op1):
        rd = [in0, in1] + ([] if isinstance(scalar, float) else [scalar])
        P.op("dve", lambda e: e.scalar_tensor_tensor(out, in0, scalar, in1, op0, op1), rd, [out])

    def cp(out, in_, eng="dve"):
        P.op(eng, lambda e: e.tensor_copy(out, in_), [in_], [out])

    def recip(out, in_):
        P.op("dve", lambda e: e.reciprocal(out, in_), [in_], [out])

    def memset(ap, v, eng="dve"):
        P.op(eng, lambda e: e.memset(ap, v), [], [ap])

    def dump(nm, ap_sb, dst):
        if nm in dbg_d:
            P.dma("sp", dst, ap_sb)


    def rstd_from(ssb, dd, rst):
        act(rst, ssb, AF.Ln, scale=1.0 / dd, bias=CST[:, 5:6])
        act(rst, rst, AF.Exp, scale=-0.5)

    memset(ones[:, :], 1.0)
    P.dma("sp", G[:, :], gains_d)
    P.dma("sp", CST[:, :], cst_d)
    memset(RM[:, :], 0.0)
    P.dma("sp", RM[0:64, 0:64], rm_d)
    memset(cos2[:, :], 0.0)
    memset(sin2[:, :], 0.0)
    memset(WQR[:, :, :, :], 0.0)
    P.dma("sp", DFT64[:, :], dft64_d)

    def g_col(l, c, p=128):
        return G[0:p, l * 32 + c:l * 32 + c + 1]

    pairs = [(s_, l_) for s_ in range(nseq) for l_ in layers]
    P.dma("pool", WIN[:, :, :], win_d[layers[0]])
    for s in range(nseq):
        for c in range(8):
            P.dma("sp", xT[:, c, :], xT_d[s, :, c, :])
        P.dma("sp", POSI[0:64, :], pos_d[s])
        cp(ANG[0:64, :], POSI[0:64, :])
        ts(ANG[0:64, :], ANG[0:64, :], CST[0:64, 0:1], None, ALU.mult)
        ts(KQI[0:64, :], ANG[0:64, :], 1.0 / (2 * math.pi), None, ALU.mult)
        cp(KQF[0:64, :], KQI[0:64, :])
        C1 = 6.28125
        C2 = 2 * math.pi - C1
        stt(RR[0:64, :], KQF[0:64, :], -C1, ANG[0:64, :], ALU.mult, ALU.add)
        stt(RR[0:64, :], KQF[0:64, :], -C2, RR[0:64, :], ALU.mult, ALU.add)
        ts(MM_[0:64, :], RR[0:64, :], math.pi, -2 * math.pi, ALU.is_gt, ALU.mult)
        tt(RR[0:64, :], RR[0:64, :], MM_[0:64, :], ALU.add)
        ts(MM_[0:64, :], RR[0:64, :], -math.pi, 2 * math.pi, ALU.is_lt, ALU.mult)
        tt(RR[0:64, :], RR[0:64, :], MM_[0:64, :], ALU.add)
        PI_S = 3.1415925
        ts(RR[0:64, :], RR[0:64, :], -PI_S, PI_S, ALU.max, ALU.min)
        act(sin2[0:64, :], RR[0:64, :], AF.Sin)
        act(MM_[0:64, :], RR[0:64, :], AF.Abs)
        act(cos2[0:64, :], MM_[0:64, :], AF.Sin, scale=-1.0, bias=CST[0:64, 6:7])
        if s == 0 and "cos" in dbg_d:
            cp(ANG[0:64, :], cos2[0:64, :])
            P.dma("sp", dbg_d["cos"], ANG[0:64, :])
            cp(KQF[0:64, :], sin2[0:64, :])
            P.dma("sp", dbg_d["sin"], KQF[0:64, :])

        for l in layers:
          try:
            memset(KPE[64:128, :], 0.0, eng="pool")
            memset(WFBD[:, :, :], 0.0, eng="pool")
            for g in range(4):
                pr, hf = g // 2, g % 2
                P.dma("pool", WFBD[64 * hf:64 * hf + 64, pr, 64 * hf:64 * hf + 64], wf_d[l, g])
            for pr in range(2):
                b = nb()
                mm(bank(b, 128, 128), DFT64[:, 0:128], WFBD[:, pr, :], True, True)
                mm(PS[:, b * 512 + 128:b * 512 + 256], DFT64[:, 128:256], WFBD[:, pr, :], True, True)
                act(RHS1[:, pr, :], PS[:, b * 512:b * 512 + 256], AF.Copy)

            def xn_sq(tb):
                tsl = slice(tb * TB, (tb + 1) * TB)
                for c in range(8):
                    act(SQX[:, c, :], xT[:, c, tsl], AF.Square)

            def xn_red(tb):
                ssb = nb()
                for c in range(8):
                    mm(bank(ssb), ones[:, :], SQX[:, c, :], c == 0, c == 7)
                rstd_from(bank(ssb), 1024.0, RST_A[:, 0, :])

            def xn_apply(tb, c0, c1):
                tsl = slice(tb * TB, (tb + 1) * TB)
                for c in range(c0, c1):
                    stt(XN[:, tb % 2, c, :], xT[:, c, tsl], g_col(l, c), RST_A[:, 0, :], ALU.mult, ALU.mult)

            def xn_fin(tb):
                xn_red(tb)
                xn_apply(tb, 0, 8)

            def chain_fin(tb):
                tsl = slice(tb * TB, (tb + 1) * TB)
                for (dst, j0, gc, rst) in ((CQN, 0, 16, RST_A[:, 1, :]), (CKVN, 2, 18, RSTC1[:, :])):
                    sb2 = nb()
                    for j in range(2):
                        mm(bank(sb2), ones[:, :], SQC[:, j0 + j, :], j == 0, j == 1)
                    rstd_from(bank(sb2), 256.0, rst)
                    for j in range(2):
                        stt(dst[:, j, tsl], RAW4[:, j0 + j, :], g_col(l, gc + j), rst,
                            ALU.mult, ALU.mult)

            def p1_group(tb, gi):
                tsl = slice(tb * TB, (tb + 1) * TB)
                xb = tb % 2
                b = nb()
                m = 128 if gi < 6 else 64
                for kc in range(8):
                    mm(bank(b, m), WIN[:, kc, gi * 128:gi * 128 + m], XN[:, xb, kc, :], kc == 0, kc == 7)
                if gi < 2:
                    act(ZF[:, gi, tsl], bank(b), AF.Copy)
                elif gi < 6:
                    cp(RAW4[:, gi - 2, :], bank(b))
                    act(SQC[:, gi - 2, :], RAW4[:, gi - 2, :], AF.Square)
                else:
                    act(KPE[0:64, tsl], bank(b, 64), AF.Copy)

            xn_sq(0)
            xn_fin(0)
            for tb in range(NTB):
                if tb + 1 < NTB:
                    xn_sq(tb + 1)
                p1_group(tb, 0)
                p1_group(tb, 1)
                p1_group(tb, 2)
                p1_group(tb, 3)
                if tb + 1 < NTB:
                    xn_red(tb + 1)
                p1_group(tb, 4)
                if tb + 1 < NTB:
                    xn_apply(tb + 1, 0, 4)
                p1_group(tb, 5)
                if tb + 1 < NTB:
                    xn_apply(tb + 1, 4, 8)
                p1_group(tb, 6)
                chain_fin(tb)
            if s == 0 and l == layers[0] and "cqn" in dbg_d:
                for j in range(2):
                    cp(ANG[:, :], CQN[:, j, :])
                    P.dma("sp", dbg_d["cqn"][j], ANG[:, :])
                    cp(ANG[:, :], ZF[:, j, :])
                    P.dma("sp", dbg_d["zf"][j], ANG[:, :])

            P.dma("sp", C0[:, :, :], c0_d)
            P.dma("sp", NS0[:, :, :], ns0_d)
            for c in range(16):
                b = nb()
                for pr in range(2):
                    mm(PS[:, b * 512 + 256 * pr:b * 512 + 256 * pr + 256],
                       ZF[:, pr, c * 128:(c + 1) * 128], RHS1[:, pr, :], True, True)
                if c % 2 == 0:
                    act(PQ0[:, c, :], bank(b), AF.Copy)
                else:
                    cp(PQ0[:, c, :], bank(b))

            def pq_view(t, half):
                return t[:, :, :].rearrange("p c (r h e) -> p (c r) h e", r=2, h=2)[:, :, half, :]

            sgn2, c1c, s1c, ns1c = CST[:, 1:2], CST[:, 2:3], CST[:, 3:4], CST[:, 4:5]
            for j in (0, 2, 1, 3):
                if j == 0:
                    src = PQ0
                elif j == 2:
                    ts(PQT[:, :, :], PQ0[:, :, :], sgn2, None, ALU.mult)
                    src = PQT
                else:
                    if j == 1:
                        ts(T1[:, :, :], PQ0[:, :, :], c1c, None, ALU.mult)
                    sa, sb_ = (ns1c, s1c) if j == 1 else (s1c, ns1c)
                    stt(pq_view(PQT, 0), pq_view(PQ0, 1), sa, pq_view(T1, 0), ALU.mult, ALU.add)
                    stt(pq_view(PQT, 1), pq_view(PQ0, 0), sb_, pq_view(T1, 1), ALU.mult, ALU.add)
                    src = PQT
                tsl = slice(j * TB, (j + 1) * TB)
                fb = []
                for fc in range(2):
                    b = nb()
                    fb.append(b)
                    for c in range(16):
                        mm(bank(b), src[:, c, 256 * fc:256 * fc + 128], C0[:, c, :], c == 0, False)
                        mm(bank(b), src[:, c, 256 * fc + 128:256 * fc + 256], NS0[:, c, :], False, c == 15)
                sb2 = nb()
                for fc in range(2):
                    act(SQ_A[:, fc, :], bank(fb[fc]), AF.Square)
                    mm(bank(sb2), ones[:, :], SQ_A[:, fc, :], fc == 0, fc == 1)
                rstd_from(bank(sb2), 256.0, RST_A[:, 0, :])
                for fc in range(2):
                    stt(YFN[:, fc, tsl], bank(fb[fc]), g_col(l, 24 + fc), RST_A[:, 0, :], ALU.mult, ALU.mult)

            P.dma("pool", WQ[:, :, :], wq_d[l])
            P.dma("pool", WKV[:, :, :], wkv_d[l])
            P.dma("pool", WQR[:, :, :, 0:64],
                  wq_d[l].rearrange("p k (h e) -> p k h e", h=NH)[:, :, :, 128:192])
            ts(TK[:, :], KPE[:, :], g_col(l, 23), None, ALU.mult)
            for tb in range(NTB):
                tsl = slice(tb * TB, (tb + 1) * TB)
                act(SQPE[:, tsl], KPE[:, tsl], AF.Square)
                b = nb()
                mm(bank(b), RM[:, :], TK[:, tsl], True, True)
                tt(TMP_B[:, 0, :], TK[:, tsl], cos2[:, tsl], ALU.mult)
                tt(TMP_B[:, 1, :], bank(b), sin2[:, tsl], ALU.mult)
                tt(KPR[:, tsl], TMP_B[:, 0, :], TMP_B[:, 1, :], ALU.add)

            scale = 1.0 / math.sqrt(192.0)
            pb = [0]

            def nbp():
                if not NBP:
                    return nb()
                pb[0] = (pb[0] + 1) % len(NBP_POOL)
                return NBP_POOL[pb[0]]

            def proj_gen(h):
                hp = h % 2
                for tb in range(NTB):
                    tsl = slice(tb * TB, (tb + 1) * TB)
                    qa, qb = nbp(), nbp()
                    for kc in range(2):
                        mm(bank(qa), WQ[:, kc, h * 192:h * 192 + 128], CQN[:, kc, tsl], kc == 0, kc == 1)
                    for kc in range(2):
                        mm(bank(qb), WQR[:, kc, h, :], CQN[:, kc, tsl], kc == 0, kc == 1)
                    act(SQ_B[:, 0, :], bank(qa), AF.Square)
                    act(SQ_B[:, 1, :], bank(qb), AF.Square)
                    cp(QRAW[:, 0, :], bank(qa))
                    cp(QRAW[:, 1, :], bank(qb))
                    yield
                    ka = nbp()
                    for kc in range(2):
                        mm(bank(ka), WKV[:, kc, h * 256:h * 256 + 128], CKVN[:, kc, tsl], kc == 0, kc == 1)
                    act(SQ2[:, :], bank(ka), AF.Square)
                    cp(QRAW2[:, :], bank(ka))
                    yield
                    ssb = nbp()
                    mm(bank(ssb), ones[:, :], SQ_B[:, 0, :], True, False)
                    mm(bank(ssb), ones[:, :], SQ_B[:, 1, :], False, True)
                    rstd_from(bank(ssb), 192.0, RST_B[:, 0, :])
                    stt(TMP_B[:, 0, :], QRAW[:, 1, :], g_col(l, 21), RST_B[:, 0, :], ALU.mult, ALU.mult)
                    stt(QN[hp][:, tsl], QRAW[:, 0, :], g_col(l, 20), RST_B[:, 0, :], ALU.mult, ALU.mult)
                    yield
                    ssk = nbp()
                    mm(bank(ssk), ones[:, :], SQ2[:, :], True, False)
                    mm(bank(ssk), ones[:, :], SQPE[:, tsl], False, True)
                    rstd_from(bank(ssk), 192.0, RST_B[:, 1, :])
                    stt(KN[hp][:, tsl], QRAW2[:, :], g_col(l, 22), RST_B[:, 1, :], ALU.mult, ALU.mult)
                    tt(KR[hp][:, tsl], KPR[:, tsl], RST_B[:, 1, :], ALU.mult)
                    yield
                    swb = nbp()
                    mm(bank(swb), RM[:, :], TMP_B[:, 0, :], True, True)
                    tt(TMP_B[:, 1, :], TMP_B[:, 0, :], cos2[:, tsl], ALU.mult)
                    tt(TMP_B[:, 2, :], bank(swb), sin2[:, tsl], ALU.mult)
                    tt(QR[hp][:, tsl], TMP_B[:, 1, :], TMP_B[:, 2, :], ALU.add)
                    yield
                    vb = nbp()
                    for ci in range(4):
                        c = tb * 4 + ci
                        for kc in range(2):
                            mm(PS[:, vb * 512 + ci * 128:vb * 512 + ci * 128 + 128],
                               CKVN[:, kc, c * 128:(c + 1) * 128],
                               WKV[:, kc, h * 256 + 128:h * 256 + 256], kc == 0, kc == 1)
                    cp(VH[hp][:, tb * 4:tb * 4 + 4, :], bank(vb))
                    yield

            if STOP == "p2":
                raise _Stop()
            if STOP and STOP.startswith("st"):
                for i_, _ in enumerate(proj_gen(0)):
                    if i_ + 1 >= int(STOP[2:]):
                        break
                raise _Stop()
            for _ in proj_gen(0):
                pass
            if STOP == "proj0":
                raise _Stop()
            for h in range(NH):
                hp = h % 2
                gen = proj_gen(h + 1) if h + 1 < NH else None
                if gen is not None and not PIPE:
                    for _ in gen:
                        pass
                    gen = None
                for qb_ in range(NTB):
                    qsl = slice(qb_ * TB, (qb_ + 1) * TB)
                    OB, DB = 4, 5

                    def qk(g):
                        for ci in range(2):
                            c = 2 * g + ci
                            bk = (g % 2) * 2 + ci
                            mm(bank(bk), KN[hp][:, c * 128:(c + 1) * 128], QN[hp][:, qsl], True, False)
                            mm(bank(bk), KR[hp][:, c * 128:(c + 1) * 128], QR[hp][:, qsl], False, True)

                    def ex(g):
                        bk = (g % 2) * 2
                        ptb = g % 3
                        act(PT[:, ptb, :], PS[:, bk * 512:bk * 512 + 1024], AF.Exp, scale=scale)

                    def pv(g):
                        ptb = g % 3
                        for ci in range(2):
                            c = 2 * g + ci
                            mm(bank(OB), VH[hp][:, c, :], PT[:, ptb, ci * TB:(ci + 1) * TB], c == 0, c == 15)
                            mm(bank(DB), ones[:, :], PT[:, ptb, ci * TB:(ci + 1) * TB], c == 0, c == 15)

                    qk(0)
                    for g in range(8):
                        if g + 1 < 8:
                            qk(g + 1)
                        ex(g)
                        if gen is not None:
                            next(gen, None)
                        pv(g)
                    act(TMPF_B[:, :], bank(DB), AF.Ln)
                    act(TMPF_B[:, :], TMPF_B[:, :], AF.Exp, scale=-1.0)
                    tt(YA[:, h, qsl], bank(OB), TMPF_B[:, :], ALU.mult)
                if gen is not None:
                    for _ in gen:
                        pass
                if STOP == "attn0":
                    raise _Stop()
            if STOP == "p3":
                raise _Stop()

            P.dma("pool", WO[:, :, :], wo_d[l])
            def ya_sq(tb):
                tsl = slice(tb * TB, (tb + 1) * TB)
                for h in range(NH):
                    act(SQY[:, h, :], YA[:, h, tsl], AF.Square)

            def ya_fin(tb):
                tsl = slice(tb * TB, (tb + 1) * TB)
                ssb = nb()
                for h in range(NH):
                    mm(bank(ssb), ones[:, :], SQY[:, h, :], h == 0, h == NH - 1)
                rstd_from(bank(ssb), 768.0, RST_B[:, 0, :])
                for h in range(NH):
                    stt(YA[:, h, tsl], YA[:, h, tsl], g_col(l, 26 + h), RST_B[:, 0, :], ALU.mult, ALU.mult)

            def mlp_sq(tb):
                tsl = slice(tb * TB, (tb + 1) * TB)
                for c in range(8):
                    act(SQM[:, c, :], xT[:, c, tsl], AF.Square)

            def mlp_fin(tb):
                tsl = slice(tb * TB, (tb + 1) * TB)
                ssb = nb()
                for c in range(8):
                    mm(bank(ssb), ones[:, :], SQM[:, c, :], c == 0, c == 7)
                rstd_from(bank(ssb), 1024.0, RST_B[:, 1, :])
                for c in range(8):
                    stt(XNA[:, c, tsl], xT[:, c, tsl], g_col(l, 8 + c), RST_B[:, 1, :], ALU.mult, ALU.mult)

            def wout_group(tb, o):
                tsl = slice(tb * TB, (tb + 1) * TB)
                b = nb()
                for kc in range(8):
                    rhs = YFN[:, kc, tsl] if kc < 2 else YA[:, kc - 2, tsl]
                    mm(bank(b), WO[:, kc, o * 128:(o + 1) * 128], rhs, kc == 0, kc == 7)
                tt(xT[:, o, tsl], bank(b), xT[:, o, tsl], ALU.add)

            def load_w(e):
                if e == 0:
                    P.dma("pool", W1X[:, :, :], w1_d[l, :, :, 0:512])
                    P.dma("pool", W2X[:, :, :], w2_d[l, :, 0:4, :])
                else:
                    P.dma("pool", W1B[:, e % 2, :, :], w1_d[l, :, :, e * 512:(e + 1) * 512])
                    P.dma("pool", W2B[:, e % 2, :, :], w2_d[l, :, e * 4:(e + 1) * 4, :])

            load_w(0)

            ya_sq(0)
            ya_fin(0)
            for tb in range(NTB):
                if tb + 1 < NTB:
                    ya_sq(tb + 1)
                for o in range(4):
                    wout_group(tb, o)
                if tb > 0:
                    mlp_fin(tb - 1)
                if tb + 1 < NTB:
                    ya_fin(tb + 1)
                for o in range(4, 8):
                    wout_group(tb, o)
                mlp_sq(tb)
            mlp_fin(NTB - 1)

            ip = pairs.index((s, l))
            if ip + 1 < len(pairs):
                P.dma("pool", WIN[:, :, :], win_d[pairs[ip + 1][1]])

            def up(e, tb, hb):
                tsl = slice(tb * TB, (tb + 1) * TB)
                for jj in range(4):
                    b = nb()
                    for kc in range(8):
                        w1 = W1X[:, kc, jj * 128:(jj + 1) * 128] if e == 0 else W1B[:, e % 2, kc, jj * 128:(jj + 1) * 128]
                        mm(bank(b), w1, XNA[:, kc, tsl], kc == 0, kc == 7)
                    act(SQ_A[:, jj % 2, :], bank(b), AF.Square)
                    stt(HB[:, hb, jj, :], bank(b), 0.0, SQ_A[:, jj % 2, :], ALU.is_gt, ALU.mult)

            def down(e, tb, hb):
                tsl = slice(tb * TB, (tb + 1) * TB)
                for o in range(8):
                    b = nb()
                    for jj in range(4):
                        w2 = W2X[:, jj, o * 128:(o + 1) * 128] if e == 0 else W2B[:, e % 2, jj, o * 128:(o + 1) * 128]
                        mm(bank(b), w2, HB[:, hb, jj, :], jj == 0, jj == 3)
                    tt(xT[:, o, tsl], bank(b), xT[:, o, tsl], ALU.add)

            steps = [(e, tb) for e in range(8) for tb in range(NTB)]
            up(0, 0, 0)
            for i, (e, tb) in enumerate(steps):
                if tb == 0 and e + 1 < 8:
                    load_w(e + 1)
                if i + 1 < len(steps):
                    e2, tb2 = steps[i + 1]
                    up(e2, tb2, (i + 1) % 2)
                down(e, tb, i % 2)

          except _Stop:
            pass
        for c in range(8):
            P.dma("sp", yT_d[s, :, c, :], xT[:, c, :])

    P.emit()
    return nc


def _pmajor(w, kchunks):
    K, N = w.shape
    return np.ascontiguousarray(w.reshape(kchunks, 128, N).transpose(1, 0, 2))


def _consts():
    bf = ml_dtypes.bfloat16
    half = 32
    inv_freq = (10000.0 ** (-np.arange(half, dtype=np.float32) / half)).astype(np.float32)
    cst = np.zeros((128, 8), np.float32)
    p = np.arange(128)
    cst[:64, 0] = inv_freq[p[:64] % 32]
    cst[:, 1] = np.where(p % 2 == 0, 1.0, -1.0)
    cst[:, 2] = np.array([1.0, 0.0, -1.0, 0.0])[p % 4]
    cst[:, 3] = np.array([0.0, 1.0, 0.0, -1.0])[p % 4]
    cst[:, 4] = -cst[:, 3]
    cst[:, 5] = EPS
    cst[:, 6] = 1.5707963
    rm = np.zeros((64, 64), np.float32)
    for m in range(32):
        rm[m + 32, m] = -1.0
        rm[m, m + 32] = 1.0
    k = np.arange(64)
    a64 = 2 * np.pi * np.outer(k, k) / 64.0
    cc = np.cos(a64) / 8.0
    sc = np.sin(a64) / 8.0
    dft64 = np.zeros((128, 256), np.float64)
    dft64[0:64, 0:64] = cc
    dft64[64:128, 64:128] = cc
    dft64[0:64, 128:192] = sc
    dft64[64:128, 192:256] = sc
    sidx = np.arange(SEQ, dtype=np.int64)
    r = np.arange(512, dtype=np.int64)
    ph = (np.outer(sidx, r) % SEQ).astype(np.float64) * (2 * np.pi / SEQ)
    c0 = np.cos(ph) / math.sqrt(SEQ)
    ns0 = -np.sin(ph) / math.sqrt(SEQ)
    c0 = np.ascontiguousarray(c0.reshape(16, 128, 512).transpose(1, 0, 2))
    ns0 = np.ascontiguousarray(ns0.reshape(16, 128, 512).transpose(1, 0, 2))
    return dict(cst=cst, rm=rm.astype(bf), dft64=dft64.astype(np.float32).astype(bf),
                c0=c0.astype(np.float32).astype(bf), ns0=ns0.astype(np.float32).astype(bf))


def _gains(inp):
    g = np.zeros((128, 64), np.float32)

    def cols(v):
        return np.asarray(v, np.float32).reshape(-1, 128).T

    for l in range(DEPTH):
        o = l * 32
        g[:, o + 0:o + 8] = cols(inp["attn_norm_g"][l])
        g[:, o + 8:o + 16] = cols(inp["mlp_norm_g"][l])
        g[:, o + 16:o + 18] = cols(inp["q_a_g"][l])
        g[:, o + 18:o + 20] = cols(inp["kv_a_g"][l])
        g[:, o + 20] = inp["q_norm_g"][l][:128]
        g[:64, o + 21] = inp["q_norm_g"][l][128:]
        g[:, o + 22] = inp["k_norm_g"][l][:128]
        g[:64, o + 23] = inp["k_norm_g"][l][128:]
        g[:, o + 24:o + 26] = cols(inp["fourier_out_g"][l])
        g[:, o + 26:o + 32] = cols(inp["attn_out_g"][l])
    return g


def _shared_inputs(inp):
    f = lambda a: np.asarray(a, np.float32)
    sh = dict(_consts())
    sh["gains"] = _gains({k: np.asarray(v) for k, v in inp.items()})
    sh["w_in"] = np.stack([_pmajor(f(inp["w_in"][l]), 8) for l in range(DEPTH)])
    sh["w_fourier"] = np.ascontiguousarray(f(inp["w_fourier"]))
    sh["w_q_up"] = np.stack([_pmajor(f(inp["w_q_up"][l]), 2) for l in range(DEPTH)])
    sh["w_kv_up"] = np.stack([_pmajor(f(inp["w_kv_up"][l]), 2) for l in range(DEPTH)])
    sh["w_out"] = np.stack([_pmajor(f(inp["w_out"][l]), 8) for l in range(DEPTH)])
    sh["w_mlp_in"] = np.stack([_pmajor(f(inp["w_mlp_in"][l]), 8) for l in range(DEPTH)])
    sh["w_mlp_out"] = np.stack([_pmajor(f(inp["w_mlp_out"][l]), 32) for l in range(DEPTH)])
    return sh


def _x_to_dev(x):
    xt = np.asarray(x, np.float32).transpose(0, 2, 1).reshape(BATCH, 8, 128, SEQ).transpose(0, 2, 1, 3)
    return [np.ascontiguousarray(xt[i * NSEQ:(i + 1) * NSEQ]) for i in range(NCORES)]


def _y_from_dev(ys):
    yt = np.concatenate(ys, axis=0)
    return np.ascontiguousarray(yt.transpose(0, 2, 1, 3).reshape(BATCH, D, SEQ).transpose(0, 2, 1))


_NC_CACHE = {}


def kernel(**inputs):
    sh = _shared_inputs(inputs)
    pos = np.asarray(inputs["positions"], np.int32)
    posr = np.ascontiguousarray(np.broadcast_to(pos[:, None, :], (BATCH, 64, SEQ)))
    xs = _x_to_dev(inputs["x"])
    if "full" not in _NC_CACHE:
        _NC_CACHE["full"] = build_nc(layers=(0, 1))
    nc = _NC_CACHE["full"]
    in_maps = []
    for i in range(NCORES):
        m = dict(sh)
        m["xT"] = xs[i]
        m["pos"] = np.ascontiguousarray(posr[i * NSEQ:(i + 1) * NSEQ])
        in_maps.append(m)
    res = run_bass_kernel_spmd(nc, in_maps, core_ids=list(range(NCORES)))
    return _y_from_dev([np.asarray(r["yT"]) for r in res.results])
```

```python
import math
from bisect import bisect_left
from contextlib import ExitStack

import numpy as np
import ml_dtypes

import concourse.bass as bass
import concourse.mybir as mybir
from concourse.bass_utils import run_bass_kernel_spmd

F32 = mybir.dt.float32
BF16 = mybir.dt.bfloat16
I32 = mybir.dt.int32
ALU = mybir.AluOpType
AF = mybir.ActivationFunctionType

NCORES = 8
BATCH, SEQ, D = 16, 2048, 1024
DEPTH = 2
NSEQ = BATCH // NCORES
NH = 6
DFF = 4096
EPS = 1e-6
TB = 512
NTB = SEQ // TB
GRAN = 256
PIPE = True
P1PIPE = True
P4PIPE = True
HOIST = True
NBP = True
SKIPB = False
SPLIT_DMA_SEMS = True
VACT = False
NOV = False
STOP = None


class _Stop(Exception):
    pass
NBP_POOL = (6, 7)
_DS = {F32: 4, BF16: 2, I32: 4}


class Prog:
    ENG = ("pe", "act", "dve", "pool", "sp")

    def __init__(self, nc, ndma=20):
        self.nc = nc
        self.stream = {e: [] for e in self.ENG}
        self.npos = {e: 0 for e in self.ENG}
        self.ops = {e: [] for e in self.ENG}
        self.ndma = ndma
        self.dmacnt = [0] * ndma
        self.dmarr = {"sp": 0, "pool": 0}
        ctrs = list(self.ENG) + [("d", i) for i in range(ndma)]
        self.ctrs = ctrs
        self.seen = {e: {c: 0 for c in ctrs} for e in self.ENG}
        self.lastw = {}
        self.readers = {}

    @staticmethod
    def keys(ap):
        t = ap.tensor
        tn = type(t).__name__
        if tn.startswith("DRam"):
            return ()
        shape = list(t.shape)
        pstride = 1
        for s in shape[1:]:
            pstride *= int(s)
        off = int(ap.offset)
        p0 = off // pstride
        e0 = off % pstride
        dims = [(int(s), int(c)) for s, c in ap.ap]
        pc = dims[0][1]
        ext = 1
        for s, c in dims[1:]:
            ext += (c - 1) * abs(s)
        ds = _DS[ap.dtype]
        if tn.startswith("SB"):
            base = int(t.manual_sbuf_range[0])
            sp = 0
        else:
            base = 0
            sp = 1
        lo = base + e0 * ds
        hi = lo + ext * ds
        q0 = p0 // 32
        q1 = (p0 + pc - 1) // 32
        gr = 2048 if sp else GRAN
        return [(sp, g, q) for g in range(lo // gr, (hi - 1) // gr + 1)
                for q in range(q0, q1 + 1)]

    def _deps(self, rkeys, wkeys, eng=None):
        need = {}
        lastw, readers = self.lastw, self.readers
        for k in rkeys:
            w = lastw.get(k)
            if w is not None and need.get(w[0], 0) < w[1]:
                need[w[0]] = w[1]
            if k[0] == 1:
                r = readers.get(k)
                if r:
                    for c, p in r.items():
                        if c != eng and need.get(c, 0) < p:
                            need[c] = p
        for k in wkeys:
            w = lastw.get(k)
            if w is not None and need.get(w[0], 0) < w[1]:
                need[w[0]] = w[1]
            r = readers.get(k)
            if r:
                for c, p in r.items():
                    if need.get(c, 0) < p:
                        need[c] = p
        return need

    def _waits(self, eng, need):
        seen = self.seen[eng]
        for ctr, pos in need.items():
            if ctr == eng and eng == "pe":
                continue
            if seen[ctr] < pos:
                if not isinstance(ctr, tuple):
                    self.ops[ctr][pos - 1][2] = True
                self.stream[eng].append(["wait", ctr, pos])
                seen[ctr] = pos

    def _record(self, ctr, pos, rkeys, wkeys):
        lastw, readers = self.lastw, self.readers
        for k in rkeys:
            r = readers.get(k)
            if r is None:
                readers[k] = {ctr: pos}
            else:
                r[ctr] = pos
        for k in wkeys:
            lastw[k] = (ctr, pos)
            if k in readers:
                readers[k] = {}

    def op(self, eng, fn, reads, writes):
        rk = [k for a in reads for k in self.keys(a)]
        wk = [k for a in writes for k in self.keys(a)]
        self._waits(eng, self._deps(rk, wk, eng))
        ent = ["op", fn, False]
        self.stream[eng].append(ent)
        self.ops[eng].append(ent)
        self.npos[eng] += 1
        self._record(eng, self.npos[eng], rk, wk)

    def dma(self, queue, out, in_):
        rk = self.keys(in_)
        wk = self.keys(out)
        if SPLIT_DMA_SEMS:
            half = self.ndma // 2
            k = self.dmarr[queue]
            self.dmarr[queue] = (k + 1) % half
            j = k if queue == "sp" else half + k
        else:
            j = self.dmarr["sp"]
            self.dmarr["sp"] = (j + 1) % self.ndma
        n = self.dmacnt[j] + 1
        need = self._deps(rk, wk)
        if n > 1:
            need[("d", j)] = max(need.get(("d", j), 0), n - 1)
        self._waits(queue, need)
        self.stream[queue].append(["dma", (out, in_), j])
        self.dmacnt[j] = n
        self._record(("d", j), n, rk, wk)

    def emit(self):
        nc = self.nc
        for j in range(self.ndma):
            if self.dmacnt[j]:
                self.stream["sp"].append(["wait", ("d", j), self.dmacnt[j]])
        rank = {}
        for e in self.ENG:
            r, rk = 0, []
            for ent in self.ops[e]:
                if ent[2]:
                    r += 1
                rk.append(r)
            rank[e] = rk
        with ExitStack() as st:
            sem = {}
            for c in self.ctrs:
                nm = c if isinstance(c, str) else "dq%d" % c[1]
                sem[c] = st.enter_context(nc.semaphore("s_" + nm))
            block = st.enter_context(nc.Block())

            def mk(e):
                def body(eng):
                    for ent in self.stream[e]:
                        if ent[0] == "wait":
                            c, p = ent[1], ent[2]
                            v = 16 * p if isinstance(c, tuple) else rank[c][p - 1]
                            eng.wait_ge(sem[c], v)
                        elif ent[0] == "op":
                            ins = ent[1](eng)
                            if ent[2]:
                                ins.then_inc(sem[e], 1)
                        else:
                            o, i = ent[1]
                            eng.dma_start(out=o, in_=i).then_inc(sem[("d", ent[2])], 16)
                return body

            block.tensor(mk("pe"))
            block.scalar(mk("act"))
            block.vector(mk("dve"))
            block.gpsimd(mk("pool"))
            block.sync(mk("sp"))


def build_nc(layers=(0, 1), nseq=NSEQ, dbg=None):
    nc = bass.Bass("TRN2", target_bir_lowering=False)
    P = Prog(nc)

    def din(name, shape, dt):
        return nc.dram_tensor(name, list(shape), dt, kind="ExternalInput").ap()

    xT_d = din("xT", [NSEQ, 128, 8, SEQ], F32)
    pos_d = din("pos", [NSEQ, 64, SEQ], I32)
    gains_d = din("gains", [128, 64], F32)
    cst_d = din("cst", [128, 8], F32)
    rm_d = din("rm", [64, 64], BF16)
    dft64_d = din("dft64", [128, 256], BF16)
    c0_d = din("c0", [128, 16, 512], BF16)
    ns0_d = din("ns0", [128, 16, 512], BF16)
    win_d = din("w_in", [DEPTH, 128, 8, 832], F32)
    wf_d = din("w_fourier", [DEPTH, 4, 64, 64], F32)
    wq_d = din("w_q_up", [DEPTH, 128, 2, 1152], F32)
    wkv_d = din("w_kv_up", [DEPTH, 128, 2, 1536], F32)
    wo_d = din("w_out", [DEPTH, 128, 8, 1024], F32)
    w1_d = din("w_mlp_in", [DEPTH, 128, 8, DFF], F32)
    w2_d = din("w_mlp_out", [DEPTH, 128, 32, 1024], F32)
    yT_d = nc.dram_tensor("yT", [NSEQ, 128, 8, SEQ], F32, kind="ExternalOutput").ap()
    dbg_d = {}
    if dbg:
        for nm, shp in dbg.items():
            dbg_d[nm] = nc.dram_tensor("dbg_" + nm, list(shp), F32, kind="ExternalOutput").ap()

    cur = [16512]

    def sb(name, shape, dt, at=None):
        n = _DS[dt]
        for s in shape[1:]:
            n *= s
        if at is None:
            at = cur[0]
            cur[0] = (at + n + 31) // 32 * 32
        return nc.alloc_sbuf_tensor_at(name, list(shape), dt, offset=at)

    xT = sb("xT", [128, 8, SEQ], F32)
    cos2 = sb("cos2", [128, SEQ], BF16)
    sin2 = sb("sin2", [128, SEQ], BF16)
    ones = sb("ones", [128, 128], BF16)
    G = sb("gains", [128, 64], F32)
    CST = sb("cst", [128, 8], F32)
    RM = sb("rm", [128, 128], BF16)
    WFBD = sb("wfbd", [128, 2, 128], BF16)
    RHS1 = sb("rhs1", [128, 2, 256], BF16)
    DFT64 = sb("dft64", [128, 256], BF16)
    A0 = cur[0]
    R0, R1, R2, R3 = A0, A0 + 32768, A0 + 65536, A0 + 98304
    assert R3 + 32768 + 3072 <= 229344, R3

    WIN = sb("win", [128, 8, 832], BF16, at=R2 + 8192)
    XN = sb("xn", [128, 2, 8, TB], BF16, at=R0 + 13312)
    ZF = sb("zf", [128, 2, SEQ], BF16, at=R1)
    CQN = sb("cqn", [128, 2, SEQ], BF16, at=R1 + 8192)
    CKVN = sb("ckvn", [128, 2, SEQ], BF16, at=R1 + 16384)
    KPE = sb("kpe", [128, SEQ], F32, at=R1 + 24576)
    YFN = sb("yfn", [128, 2, SEQ], BF16, at=R2)
    SQ_A = sb("sqA", [128, 2, TB], BF16, at=R2 + 24576)
    RST_A = sb("rstA", [128, 2, TB], F32, at=R2 + 24576 + 2048)
    TMP_A = sb("tmpA", [128, 2, TB], BF16, at=R2 + 24576 + 2048 + 4096)
    SQ_B = sb("sqB", [128, 2, TB], BF16, at=R1 + 6144)
    RST_B = sb("rstB", [128, 2, TB], F32, at=R3 + 18944)
    TMP_B = sb("tmpB", [128, 3, TB], BF16, at=R3 + 18944 + 4096)
    TMPF_B = sb("tmpfB", [128, TB], F32, at=R3 + 18944 + 4096 + 3072)
    QRAW = sb("qraw", [128, 2, TB], F32, at=R3 + 18944 + 4096 + 3072 + 2048)
    SQX = sb("sqx", [128, 8, TB], BF16, at=R2)
    RAW4 = sb("raw4", [128, 4, TB], F32, at=R0)
    SQC = sb("sqc", [128, 4, TB], BF16, at=R0 + 8192)
    RSTC1 = sb("rstc1", [128, TB], F32, at=R2 + 24576 + 2048 + 4096)
    SQY = sb("sqy", [128, NH, TB], BF16, at=R1)
    SQM = sb("sqm", [128, 8, TB], BF16, at=R1 + 24576)
    W1X = sb("w1x", [128, 8, 512], BF16, at=R3)
    W2X = sb("w2x", [128, 4, 1024], BF16, at=R3 + 8192)
    PQ0 = sb("pq0", [128, 16, 512], BF16, at=R0)
    PQT = sb("pqt", [128, 16, 512], BF16, at=R0 + 16384)
    T1 = sb("t1", [128, 16, 512], BF16, at=R2 + 8192)
    C0 = sb("c0", [128, 16, 512], BF16, at=R3)
    NS0 = sb("ns0", [128, 16, 512], BF16, at=R3 + 16384)
    QN = [sb("qn%d" % i, [128, SEQ], BF16, at=R0 + 16384 * i) for i in range(2)]
    QR = [sb("qr%d" % i, [128, SEQ], BF16, at=R0 + 16384 * i + 4096) for i in range(2)]
    KN = [sb("kn%d" % i, [128, SEQ], BF16, at=R0 + 16384 * i + 8192) for i in range(2)]
    VH = [sb("vh%d" % i, [128, 16, 128], BF16, at=R0 + 16384 * i + 12288) for i in range(2)]
    KR = [sb("kr%d" % i, [128, SEQ], BF16, at=R1 + 24576 + 4096 * i) for i in range(2)]
    PT = sb("pt", [128, 3, 2 * TB], BF16, at=R1)
    YA = sb("ya", [128, NH, SEQ], BF16, at=R2 + 8192)
    WQ = sb("wq", [128, 2, 1152], BF16, at=R3)
    WKV = sb("wkv", [128, 2, 1536], BF16, at=R3 + 4608)
    KPR = sb("kpr", [128, SEQ], BF16, at=R3 + 10752)
    SQPE = sb("sqpe", [128, SEQ], BF16, at=R3 + 14848)
    TK = sb("tk", [128, SEQ], BF16, at=R2 + 8192 + 5 * 4096)
    QRAW2 = sb("qraw2", [128, TB], F32, at=R2 + 8192 + 5 * 4096)
    SQ2 = sb("sq2", [128, TB], BF16, at=R2 + 8192 + 5 * 4096 + 2048)
    WO = sb("wo", [128, 8, 1024], BF16, at=R1 + 8192)
    XNA = sb("xna", [128, 8, SEQ], BF16, at=R0)
    W1B = sb("w1b", [128, 2, 8, 512], BF16, at=R1)
    W2B = sb("w2b", [128, 2, 4, 1024], BF16, at=R1 + 16384)
    HB = sb("hb", [128, 2, 4, TB], BF16, at=R2)
    POSI = sb("posi", [128, SEQ], I32, at=R0)
    ANG = sb("ang", [128, SEQ], F32, at=R0 + 8192)
    KQI = sb("kqi", [128, SEQ], I32, at=R0 + 16384)
    KQF = sb("kqf", [128, SEQ], F32, at=R0 + 24576)
    RR = sb("rr", [128, SEQ], F32, at=R1)
    MM_ = sb("mmk", [128, SEQ], F32, at=R1 + 8192)

    WQR = sb("wqr", [128, 2, NH, 128], BF16, at=R3 + 32768)
    PS = nc.alloc_psum_tensor("ps", [128, 4096], F32)

    def bank(i, p=128, n=TB):
        return PS[0:p, i * 512:i * 512 + n]

    rr = [0]

    def nb():
        i = rr[0]
        rr[0] = (i + 1) % 8
        return i

    def mm(out, lhsT, rhs, start, stop):
        P.op("pe", lambda e: e.matmul(out, lhsT, rhs, start=start, stop=stop),
             [lhsT, rhs], [out])

    def act(out, in_, func, scale=1.0, bias=None):
        rd = [in_]
        if bias is not None and not isinstance(bias, float):
            rd.append(bias)
        if not isinstance(scale, float):
            rd.append(scale)
        if bias is None:
            P.op("act", lambda e: e.activation(out, in_, func, scale=scale), rd, [out])
        else:
            P.op("act", lambda e: e.activation(out, in_, func, bias=bias, scale=scale), rd, [out])

    def tt(out, in0, in1, op, eng="dve"):
        P.op(eng, lambda e: e.tensor_tensor(out, in0, in1, op), [in0, in1], [out])

    def ts(out, in0, s1, s2, op0, op1=None, eng="dve"):
        rd = [in0] + [s for s in (s1, s2) if s is not None and not isinstance(s, float)]
        if op1 is None:
            P.op(eng, lambda e: e.tensor_scalar(out, in0, s1, None, op0), rd, [out])
        else:
            P.op(eng, lambda e: e.tensor_scalar(out, in0, s1, s2, op0, op1), rd, [out])

    def stt(out, in0, scalar, in1, op0, op1):
        rd = [in0, in1] + ([] if isinstance(scalar, float) else [scalar])
        P.op("dve", lambda e: e.scalar_tensor_tensor(out, in0, scalar, in1, op0, op1), rd, [out])

    def cp(out, in_, eng="dve"):
        P.op(eng, lambda e: e.tensor_copy(out, in_), [in_], [out])

    def recip(out, in_):
        P.op("dve", lambda e: e.reciprocal(out, in_), [in_], [out])

    def memset(ap, v, eng="dve"):
        P.op(eng, lambda e: e.memset(ap, v), [], [ap])

    def dump(nm, ap_sb, dst):
        if nm in dbg_d:
            P.dma("sp", dst, ap_sb)


    def rstd_from(ssb, dd, rst):
        act(rst, ssb, AF.Ln, scale=1.0 / dd, bias=CST[:, 5:6])
        act(rst, rst, AF.Exp, scale=-0.5)

    memset(ones[:, :], 1.0)
    P.dma("sp", G[:, :], gains_d)
    P.dma("sp", CST[:, :], cst_d)
    memset(RM[:, :], 0.0)
    P.dma("sp", RM[0:64, 0:64], rm_d)
    memset(cos2[:, :], 0.0)
    memset(sin2[:, :], 0.0)
    memset(WQR[:, :, :, :], 0.0)
    P.dma("sp", DFT64[:, :], dft64_d)

    def g_col(l, c, p=128):
        return G[0:p, l * 32 + c:l * 32 + c + 1]

    pairs = [(s_, l_) for s_ in range(nseq) for l_ in layers]
    P.dma("pool", WIN[:, :, :], win_d[layers[0]])
    for s in range(nseq):
        for c in range(8):
            P.dma("sp", xT[:, c, :], xT_d[s, :, c, :])
        P.dma("sp", POSI[0:64, :], pos_d[s])
        cp(ANG[0:64, :], POSI[0:64, :])
        ts(ANG[0:64, :], ANG[0:64, :], CST[0:64, 0:1], None, ALU.mult)
        ts(KQI[0:64, :], ANG[0:64, :], 1.0 / (2 * math.pi), None, ALU.mult)
        cp(KQF[0:64, :], KQI[0:64, :])
        C1 = 6.28125
        C2 = 2 * math.pi - C1
        stt(RR[0:64, :], KQF[0:64, :], -C1, ANG[0:64, :], ALU.mult, ALU.add)
        stt(RR[0:64, :], KQF[0:64, :], -C2, RR[0:64, :], ALU.mult, ALU.add)
        ts(MM_[0:64, :], RR[0:64, :], math.pi, -2 * math.pi, ALU.is_gt, ALU.mult)
        tt(RR[0:64, :], RR[0:64, :], MM_[0:64, :], ALU.add)
        ts(MM_[0:64, :], RR[0:64, :], -math.pi, 2 * math.pi, ALU.is_lt, ALU.mult)
        tt(RR[0:64, :], RR[0:64, :], MM_[0:64, :], ALU.add)
        PI_S = 3.1415925
        ts(RR[0:64, :], RR[0:64, :], -PI_S, PI_S, ALU.max, ALU.min)
        act(sin2[0:64, :], RR[0:64, :], AF.Sin)
        act(MM_[0:64, :], RR[0:64, :], AF.Abs)
        act(cos2[0:64, :], MM_[0:64, :], AF.Sin, scale=-1.0, bias=CST[0:64, 6:7])
        if s == 0 and "cos" in dbg_d:
            cp(ANG[0:64, :], cos2[0:64, :])
            P.dma("sp", dbg_d["cos"], ANG[0:64, :])
            cp(KQF[0:64, :], sin2[0:64, :])
            P.dma("sp", dbg_d["sin"], KQF[0:64, :])

        for l in layers:
          try:
            memset(KPE[64:128, :], 0.0, eng="pool")
            memset(WFBD[:, :, :], 0.0, eng="pool")
            for g in range(4):
                pr, hf = g // 2, g % 2
                P.dma("pool", WFBD[64 * hf:64 * hf + 64, pr, 64 * hf:64 * hf + 64], wf_d[l, g])
            for pr in range(2):
                b = nb()
                mm(bank(b, 128, 128), DFT64[:, 0:128], WFBD[:, pr, :], True, True)
                mm(PS[:, b * 512 + 128:b * 512 + 256], DFT64[:, 128:256], WFBD[:, pr, :], True, True)
                act(RHS1[:, pr, :], PS[:, b * 512:b * 512 + 256], AF.Copy)

            def xn_sq(tb):
                tsl = slice(tb * TB, (tb + 1) * TB)
                for c in range(8):
                    act(SQX[:, c, :], xT[:, c, tsl], AF.Square)

            def xn_red(tb):
                ssb = nb()
                for c in range(8):
                    mm(bank(ssb), ones[:, :], SQX[:, c, :], c == 0, c == 7)
                rstd_from(bank(ssb), 1024.0, RST_A[:, 0, :])

            def xn_apply(tb, c0, c1):
                tsl = slice(tb * TB, (tb + 1) * TB)
                for c in range(c0, c1):
                    stt(XN[:, tb % 2, c, :], xT[:, c, tsl], g_col(l, c), RST_A[:, 0, :], ALU.mult, ALU.mult)

            def xn_fin(tb):
                xn_red(tb)
                xn_apply(tb, 0, 8)

            def chain_fin(tb):
                tsl = slice(tb * TB, (tb + 1) * TB)
                for (dst, j0, gc, rst) in ((CQN, 0, 16, RST_A[:, 1, :]), (CKVN, 2, 18, RSTC1[:, :])):
                    sb2 = nb()
                    for j in range(2):
                        mm(bank(sb2), ones[:, :], SQC[:, j0 + j, :], j == 0, j == 1)
                    rstd_from(bank(sb2), 256.0, rst)
                    for j in range(2):
                        stt(dst[:, j, tsl], RAW4[:, j0 + j, :], g_col(l, gc + j), rst,
                            ALU.mult, ALU.mult)

            def p1_group(tb, gi):
                tsl = slice(tb * TB, (tb + 1) * TB)
                xb = tb % 2
                b = nb()
                m = 128 if gi < 6 else 64
                for kc in range(8):
                    mm(bank(b, m), WIN[:, kc, gi * 128:gi * 128 + m], XN[:, xb, kc, :], kc == 0, kc == 7)
                if gi < 2:
                    act(ZF[:, gi, tsl], bank(b), AF.Copy)
                elif gi < 6:
                    cp(RAW4[:, gi - 2, :], bank(b))
                    act(SQC[:, gi - 2, :], RAW4[:, gi - 2, :], AF.Square)
                else:
                    act(KPE[0:64, tsl], bank(b, 64), AF.Copy)

            xn_sq(0)
            xn_fin(0)
            for tb in range(NTB):
                if tb + 1 < NTB:
                    xn_sq(tb + 1)
                p1_group(tb, 0)
                p1_group(tb, 1)
                if tb > 0:
                    chain_fin(tb - 1)
                p1_group(tb, 2)
                p1_group(tb, 3)
                if tb + 1 < NTB:
                    xn_red(tb + 1)
                p1_group(tb, 4)
                if tb + 1 < NTB:
                    xn_apply(tb + 1, 0, 4)
                p1_group(tb, 5)
                if tb + 1 < NTB:
                    xn_apply(tb + 1, 4, 8)
                p1_group(tb, 6)
            chain_fin(NTB - 1)
            if s == 0 and l == layers[0] and "cqn" in dbg_d:
                for j in range(2):
                    cp(ANG[:, :], CQN[:, j, :])
                    P.dma("sp", dbg_d["cqn"][j], ANG[:, :])
                    cp(ANG[:, :], ZF[:, j, :])
                    P.dma("sp", dbg_d["zf"][j], ANG[:, :])

            P.dma("sp", C0[:, :, :], c0_d)
            P.dma("sp", NS0[:, :, :], ns0_d)
            for c in range(16):
                b = nb()
                for pr in range(2):
                    mm(PS[:, b * 512 + 256 * pr:b * 512 + 256 * pr + 256],
                       ZF[:, pr, c * 128:(c + 1) * 128], RHS1[:, pr, :], True, True)
                if c % 2 == 0:
                    act(PQ0[:, c, :], bank(b), AF.Copy)
                else:
                    cp(PQ0[:, c, :], bank(b))

            def pq_view(t, half):
                return t[:, :, :].rearrange("p c (r h e) -> p (c r) h e", r=2, h=2)[:, :, half, :]

            sgn2, c1c, s1c, ns1c = CST[:, 1:2], CST[:, 2:3], CST[:, 3:4], CST[:, 4:5]
            for j in (0, 2, 1, 3):
                if j == 0:
                    src = PQ0
                elif j == 2:
                    ts(PQT[:, :, :], PQ0[:, :, :], sgn2, None, ALU.mult)
                    src = PQT
                else:
                    if j == 1:
                        ts(T1[:, :, :], PQ0[:, :, :], c1c, None, ALU.mult)
                    sa, sb_ = (ns1c, s1c) if j == 1 else (s1c, ns1c)
                    stt(pq_view(PQT, 0), pq_view(PQ0, 1), sa, pq_view(T1, 0), ALU.mult, ALU.add)
                    stt(pq_view(PQT, 1), pq_view(PQ0, 0), sb_, pq_view(T1, 1), ALU.mult, ALU.add)
                    src = PQT
                tsl = slice(j * TB, (j + 1) * TB)
                fb = []
                for fc in range(2):
                    b = nb()
                    fb.append(b)
                    for c in range(16):
                        mm(bank(b), src[:, c, 256 * fc:256 * fc + 128], C0[:, c, :], c == 0, False)
                        mm(bank(b), src[:, c, 256 * fc + 128:256 * fc + 256], NS0[:, c, :], False, c == 15)
                sb2 = nb()
                for fc in range(2):
                    act(SQ_A[:, fc, :], bank(fb[fc]), AF.Square)
                    mm(bank(sb2), ones[:, :], SQ_A[:, fc, :], fc == 0, fc == 1)
                rstd_from(bank(sb2), 256.0, RST_A[:, 0, :])
                for fc in range(2):
                    stt(YFN[:, fc, tsl], bank(fb[fc]), g_col(l, 24 + fc), RST_A[:, 0, :], ALU.mult, ALU.mult)

            P.dma("pool", WQ[:, :, :], wq_d[l])
            P.dma("pool", WKV[:, :, :], wkv_d[l])
            P.dma("pool", WQR[:, :, :, 0:64],
                  wq_d[l].rearrange("p k (h e) -> p k h e", h=NH)[:, :, :, 128:192])
            ts(TK[:, :], KPE[:, :], g_col(l, 23), None, ALU.mult)
            for tb in range(NTB):
                tsl = slice(tb * TB, (tb + 1) * TB)
                act(SQPE[:, tsl], KPE[:, tsl], AF.Square)
                b = nb()
                mm(bank(b), RM[:, :], TK[:, tsl], True, True)
                tt(TMP_B[:, 0, :], TK[:, tsl], cos2[:, tsl], ALU.mult)
                tt(TMP_B[:, 1, :], bank(b), sin2[:, tsl], ALU.mult)
                tt(KPR[:, tsl], TMP_B[:, 0, :], TMP_B[:, 1, :], ALU.add)

            scale = 1.0 / math.sqrt(192.0)
            pb = [0]

            def nbp():
                if not NBP:
                    return nb()
                pb[0] = (pb[0] + 1) % len(NBP_POOL)
                return NBP_POOL[pb[0]]

            def proj_gen(h):
                hp = h % 2
                for tb in range(NTB):
                    tsl = slice(tb * TB, (tb + 1) * TB)
                    qa, qb = nbp(), nbp()
                    for kc in range(2):
                        mm(bank(qa), WQ[:, kc, h * 192:h * 192 + 128], CQN[:, kc, tsl], kc == 0, kc == 1)
                    for kc in range(2):
                        mm(bank(qb), WQR[:, kc, h, :], CQN[:, kc, tsl], kc == 0, kc == 1)
                    act(SQ_B[:, 0, :], bank(qa), AF.Square)
                    act(SQ_B[:, 1, :], bank(qb), AF.Square)
                    cp(QRAW[:, 0, :], bank(qa))
                    cp(QRAW[:, 1, :], bank(qb))
                    yield
                    ka = nbp()
                    for kc in range(2):
                        mm(bank(ka), WKV[:, kc, h * 256:h * 256 + 128], CKVN[:, kc, tsl], kc == 0, kc == 1)
                    act(SQ2[:, :], bank(ka), AF.Square)
                    cp(QRAW2[:, :], bank(ka))
                    yield
                    ssb = nbp()
                    mm(bank(ssb), ones[:, :], SQ_B[:, 0, :], True, False)
                    mm(bank(ssb), ones[:, :], SQ_B[:, 1, :], False, True)
                    rstd_from(bank(ssb), 192.0, RST_B[:, 0, :])
                    stt(TMP_B[:, 0, :], QRAW[:, 1, :], g_col(l, 21), RST_B[:, 0, :], ALU.mult, ALU.mult)
                    stt(QN[hp][:, tsl], QRAW[:, 0, :], g_col(l, 20), RST_B[:, 0, :], ALU.mult, ALU.mult)
                    yield
                    ssk = nbp()
                    mm(bank(ssk), ones[:, :], SQ2[:, :], True, False)
                    mm(bank(ssk), ones[:, :], SQPE[:, tsl], False, True)
                    rstd_from(bank(ssk), 192.0, RST_B[:, 1, :])
                    stt(KN[hp][:, tsl], QRAW2[:, :], g_col(l, 22), RST_B[:, 1, :], ALU.mult, ALU.mult)
                    tt(KR[hp][:, tsl], KPR[:, tsl], RST_B[:, 1, :], ALU.mult)
                    yield
                    swb = nbp()
                    mm(bank(swb), RM[:, :], TMP_B[:, 0, :], True, True)
                    tt(TMP_B[:, 1, :], TMP_B[:, 0, :], cos2[:, tsl], ALU.mult)
                    tt(TMP_B[:, 2, :], bank(swb), sin2[:, tsl], ALU.mult)
                    tt(QR[hp][:, tsl], TMP_B[:, 1, :], TMP_B[:, 2, :], ALU.add)
                    yield
                    vb = nbp()
                    for ci in range(4):
                        c = tb * 4 + ci
                        for kc in range(2):
                            mm(PS[:, vb * 512 + ci * 128:vb * 512 + ci * 128 + 128],
                               CKVN[:, kc, c * 128:(c + 1) * 128],
                               WKV[:, kc, h * 256 + 128:h * 256 + 256], kc == 0, kc == 1)
                    cp(VH[hp][:, tb * 4:tb * 4 + 4, :], bank(vb))
                    yield

            if STOP == "p2":
                raise _Stop()
            if STOP and STOP.startswith("st"):
                for i_, _ in enumerate(proj_gen(0)):
                    if i_ + 1 >= int(STOP[2:]):
                        break
                raise _Stop()
            for _ in proj_gen(0):
                pass
            if STOP == "proj0":
                raise _Stop()
            for h in range(NH):
                hp = h % 2
                gen = proj_gen(h + 1) if h + 1 < NH else None
                if gen is not None and not PIPE:
                    for _ in gen:
                        pass
                    gen = None
                for qb_ in range(NTB):
                    qsl = slice(qb_ * TB, (qb_ + 1) * TB)
                    OB, DB = 4, 5

                    def qk(g):
                        for ci in range(2):
                            c = 2 * g + ci
                            bk = (g % 2) * 2 + ci
                            mm(bank(bk), KN[hp][:, c * 128:(c + 1) * 128], QN[hp][:, qsl], True, False)
                            mm(bank(bk), KR[hp][:, c * 128:(c + 1) * 128], QR[hp][:, qsl], False, True)

                    def ex(g):
                        bk = (g % 2) * 2
                        ptb = g % 3
                        act(PT[:, ptb, :], PS[:, bk * 512:bk * 512 + 1024], AF.Exp, scale=scale)

                    def pv(g):
                        ptb = g % 3
                        for ci in range(2):
                            c = 2 * g + ci
                            mm(bank(OB), VH[hp][:, c, :], PT[:, ptb, ci * TB:(ci + 1) * TB], c == 0, c == 15)
                            mm(bank(DB), ones[:, :], PT[:, ptb, ci * TB:(ci + 1) * TB], c == 0, c == 15)

                    qk(0)
                    for g in range(8):
                        if g + 1 < 8:
                            qk(g + 1)
                        ex(g)
                        if gen is not None:
                            next(gen, None)
                        pv(g)
                    act(TMPF_B[:, :], bank(DB), AF.Ln)
                    act(TMPF_B[:, :], TMPF_B[:, :], AF.Exp, scale=-1.0)
                    tt(YA[:, h, qsl], bank(OB), TMPF_B[:, :], ALU.mult)
                if gen is not None:
                    for _ in gen:
                        pass
                if STOP == "attn0":
                    raise _Stop()
            if STOP == "p3":
                raise _Stop()

            P.dma("pool", WO[:, :, :], wo_d[l])
            def ya_sq(tb):
                tsl = slice(tb * TB, (tb + 1) * TB)
                for h in range(NH):
                    act(SQY[:, h, :], YA[:, h, tsl], AF.Square)

            def ya_red(tb):
                ssb = nb()
                for h in range(NH):
                    mm(bank(ssb), ones[:, :], SQY[:, h, :], h == 0, h == NH - 1)
                rstd_from(bank(ssb), 768.0, RST_B[:, 0, :])

            def ya_apply(tb, h0, h1):
                tsl = slice(tb * TB, (tb + 1) * TB)
                for h in range(h0, h1):
                    stt(YA[:, h, tsl], YA[:, h, tsl], g_col(l, 26 + h), RST_B[:, 0, :], ALU.mult, ALU.mult)

            def ya_fin(tb):
                ya_red(tb)
                ya_apply(tb, 0, NH)

            def mlp_sq(tb):
                tsl = slice(tb * TB, (tb + 1) * TB)
                for c in range(8):
                    act(SQM[:, c, :], xT[:, c, tsl], AF.Square)

            def mlp_red(tb):
                ssb = nb()
                for c in range(8):
                    mm(bank(ssb), ones[:, :], SQM[:, c, :], c == 0, c == 7)
                rstd_from(bank(ssb), 1024.0, RST_B[:, 1, :])

            def mlp_apply(tb, c0, c1):
                tsl = slice(tb * TB, (tb + 1) * TB)
                for c in range(c0, c1):
                    stt(XNA[:, c, tsl], xT[:, c, tsl], g_col(l, 8 + c), RST_B[:, 1, :], ALU.mult, ALU.mult)

            def mlp_fin(tb):
                mlp_red(tb)
                mlp_apply(tb, 0, 8)

            def wout_group(tb, o):
                tsl = slice(tb * TB, (tb + 1) * TB)
                b = nb()
                for kc in range(8):
                    rhs = YFN[:, kc, tsl] if kc < 2 else YA[:, kc - 2, tsl]
                    mm(bank(b), WO[:, kc, o * 128:(o + 1) * 128], rhs, kc == 0, kc == 7)
                tt(xT[:, o, tsl], bank(b), xT[:, o, tsl], ALU.add)

            def load_w(e):
                if e == 0:
                    P.dma("pool", W1X[:, :, :], w1_d[l, :, :, 0:512])
                    P.dma("pool", W2X[:, :, :], w2_d[l, :, 0:4, :])
                else:
                    P.dma("pool", W1B[:, e % 2, :, :], w1_d[l, :, :, e * 512:(e + 1) * 512])
                    P.dma("pool", W2B[:, e % 2, :, :], w2_d[l, :, e * 4:(e + 1) * 4, :])

            load_w(0)

            ya_sq(0)
            ya_fin(0)
            for tb in range(NTB):
                if tb + 1 < NTB:
                    ya_sq(tb + 1)
                for o in range(4):
                    wout_group(tb, o)
                if tb > 0:
                    mlp_red(tb - 1)
                if tb + 1 < NTB:
                    ya_red(tb + 1)
                wout_group(tb, 4)
                if tb + 1 < NTB:
                    ya_apply(tb + 1, 0, 3)
                wout_group(tb, 5)
                if tb + 1 < NTB:
                    ya_apply(tb + 1, 3, NH)
                wout_group(tb, 6)
                if tb > 0:
                    mlp_apply(tb - 1, 0, 4)
                wout_group(tb, 7)
                if tb > 0:
                    mlp_apply(tb - 1, 4, 8)
                mlp_sq(tb)
            mlp_fin(NTB - 1)

            ip = pairs.index((s, l))
            if ip + 1 < len(pairs):
                P.dma("pool", WIN[:, :, :], win_d[pairs[ip + 1][1]])

            def up(e, tb, hb):
                tsl = slice(tb * TB, (tb + 1) * TB)
                for jj in range(4):
                    b = nb()
                    for kc in range(8):
                        w1 = W1X[:, kc, jj * 128:(jj + 1) * 128] if e == 0 else W1B[:, e % 2, kc, jj * 128:(jj + 1) * 128]
                        mm(bank(b), w1, XNA[:, kc, tsl], kc == 0, kc == 7)
                    act(SQ_A[:, jj % 2, :], bank(b), AF.Square)
                    stt(HB[:, hb, jj, :], bank(b), 0.0, SQ_A[:, jj % 2, :], ALU.is_gt, ALU.mult)

            def down(e, tb, hb):
                tsl = slice(tb * TB, (tb + 1) * TB)
                for o in range(8):
                    b = nb()
                    for jj in range(4):
                        w2 = W2X[:, jj, o * 128:(o + 1) * 128] if e == 0 else W2B[:, e % 2, jj, o * 128:(o + 1) * 128]
                        mm(bank(b), w2, HB[:, hb, jj, :], jj == 0, jj == 3)
                    tt(xT[:, o, tsl], bank(b), xT[:, o, tsl], ALU.add)

            steps = [(e, tb) for e in range(8) for tb in range(NTB)]
            up(0, 0, 0)
            for i, (e, tb) in enumerate(steps):
                if tb == 0 and e + 1 < 8:
                    load_w(e + 1)
                if i + 1 < len(steps):
                    e2, tb2 = steps[i + 1]
                    up(e2, tb2, (i + 1) % 2)
                down(e, tb, i % 2)

          except _Stop:
            pass
        for c in range(8):
            P.dma("sp", yT_d[s, :, c, :], xT[:, c, :])

    P.emit()
    return nc


def _pmajor(w, kchunks):
    K, N = w.shape
    return np.ascontiguousarray(w.reshape(kchunks, 128, N).transpose(1, 0, 2))


def _consts():
    bf = ml_dtypes.bfloat16
    half = 32
    inv_freq = (10000.0 ** (-np.arange(half, dtype=np.float32) / half)).astype(np.float32)
    cst = np.zeros((128, 8), np.float32)
    p = np.arange(128)
    cst[:64, 0] = inv_freq[p[:64] % 32]
    cst[:, 1] = np.where(p % 2 == 0, 1.0, -1.0)
    cst[:, 2] = np.array([1.0, 0.0, -1.0, 0.0])[p % 4]
    cst[:, 3] = np.array([0.0, 1.0, 0.0, -1.0])[p % 4]
    cst[:, 4] = -cst[:, 3]
    cst[:, 5] = EPS
    cst[:, 6] = 1.5707963
    rm = np.zeros((64, 64), np.float32)
    for m in range(32):
        rm[m + 32, m] = -1.0
        rm[m, m + 32] = 1.0
    k = np.arange(64)
    a64 = 2 * np.pi * np.outer(k, k) / 64.0
    cc = np.cos(a64) / 8.0
    sc = np.sin(a64) / 8.0
    dft64 = np.zeros((128, 256), np.float64)
    dft64[0:64, 0:64] = cc
    dft64[64:128, 64:128] = cc
    dft64[0:64, 128:192] = sc
    dft64[64:128, 192:256] = sc
    sidx = np.arange(SEQ, dtype=np.int64)
    r = np.arange(512, dtype=np.int64)
    ph = (np.outer(sidx, r) % SEQ).astype(np.float64) * (2 * np.pi / SEQ)
    c0 = np.cos(ph) / math.sqrt(SEQ)
    ns0 = -np.sin(ph) / math.sqrt(SEQ)
    c0 = np.ascontiguousarray(c0.reshape(16, 128, 512).transpose(1, 0, 2))
    ns0 = np.ascontiguousarray(ns0.reshape(16, 128, 512).transpose(1, 0, 2))
    return dict(cst=cst, rm=rm.astype(bf), dft64=dft64.astype(np.float32).astype(bf),
                c0=c0.astype(np.float32).astype(bf), ns0=ns0.astype(np.float32).astype(bf))


def _gains(inp):
    g = np.zeros((128, 64), np.float32)

    def cols(v):
        return np.asarray(v, np.float32).reshape(-1, 128).T

    for l in range(DEPTH):
        o = l * 32
        g[:, o + 0:o + 8] = cols(inp["attn_norm_g"][l])
        g[:, o + 8:o + 16] = cols(inp["mlp_norm_g"][l])
        g[:, o + 16:o + 18] = cols(inp["q_a_g"][l])
        g[:, o + 18:o + 20] = cols(inp["kv_a_g"][l])
        g[:, o + 20] = inp["q_norm_g"][l][:128]
        g[:64, o + 21] = inp["q_norm_g"][l][128:]
        g[:, o + 22] = inp["k_norm_g"][l][:128]
        g[:64, o + 23] = inp["k_norm_g"][l][128:]
        g[:, o + 24:o + 26] = cols(inp["fourier_out_g"][l])
        g[:, o + 26:o + 32] = cols(inp["attn_out_g"][l])
    return g


def _shared_inputs(inp):
    f = lambda a: np.asarray(a, np.float32)
    sh = dict(_consts())
    sh["gains"] = _gains({k: np.asarray(v) for k, v in inp.items()})
    sh["w_in"] = np.stack([_pmajor(f(inp["w_in"][l]), 8) for l in range(DEPTH)])
    sh["w_fourier"] = np.ascontiguousarray(f(inp["w_fourier"]))
    sh["w_q_up"] = np.stack([_pmajor(f(inp["w_q_up"][l]), 2) for l in range(DEPTH)])
    sh["w_kv_up"] = np.stack([_pmajor(f(inp["w_kv_up"][l]), 2) for l in range(DEPTH)])
    sh["w_out"] = np.stack([_pmajor(f(inp["w_out"][l]), 8) for l in range(DEPTH)])
    sh["w_mlp_in"] = np.stack([_pmajor(f(inp["w_mlp_in"][l]), 8) for l in range(DEPTH)])
    sh["w_mlp_out"] = np.stack([_pmajor(f(inp["w_mlp_out"][l]), 32) for l in range(DEPTH)])
    return sh


def _x_to_dev(x):
    xt = np.asarray(x, np.float32).transpose(0, 2, 1).reshape(BATCH, 8, 128, SEQ).transpose(0, 2, 1, 3)
    return [np.ascontiguousarray(xt[i * NSEQ:(i + 1) * NSEQ]) for i in range(NCORES)]


def _y_from_dev(ys):
    yt = np.concatenate(ys, axis=0)
    return np.ascontiguousarray(yt.transpose(0, 2, 1, 3).reshape(BATCH, D, SEQ).transpose(0, 2, 1))


_NC_CACHE = {}


def kernel(**inputs):
    sh = _shared_inputs(inputs)
    pos = np.asarray(inputs["positions"], np.int32)
    posr = np.ascontiguousarray(np.broadcast_to(pos[:, None, :], (BATCH, 64, SEQ)))
    xs = _x_to_dev(inputs["x"])
    if "full" not in _NC_CACHE:
        _NC_CACHE["full"] = build_nc(layers=(0, 1))
    nc = _NC_CACHE["full"]
    in_maps = []
    for i in range(NCORES):
        m = dict(sh)
        m["xT"] = xs[i]
        m["pos"] = np.ascontiguousarray(posr[i * NSEQ:(i + 1) * NSEQ])
        in_maps.append(m)
    res = run_bass_kernel_spmd(nc, in_maps, core_ids=list(range(NCORES)))
    return _y_from_dev([np.asarray(r["yT"]) for r in res.results])
```

```python
import math
from bisect import bisect_left
from contextlib import ExitStack

import numpy as np
import ml_dtypes

import concourse.bass as bass
import concourse.mybir as mybir
from concourse.bass_utils import run_bass_kernel_spmd

F32 = mybir.dt.float32
BF16 = mybir.dt.bfloat16
I32 = mybir.dt.int32
ALU = mybir.AluOpType
AF = mybir.ActivationFunctionType

NCORES = 8
BATCH, SEQ, D = 16, 2048, 1024
DEPTH = 2
NSEQ = BATCH // NCORES
NH = 6
DFF = 4096
EPS = 1e-6
TB = 512
NTB = SEQ // TB
GRAN = 256
PIPE = True
P1PIPE = True
P4PIPE = True
HOIST = True
NBP = True
SKIPB = False
SPLIT_DMA_SEMS = True
VACT = False
NOV = False
STOP = None


class _Stop(Exception):
    pass
NBP_POOL = (6, 7)
_DS = {F32: 4, BF16: 2, I32: 4}


class Prog:
    ENG = ("pe", "act", "dve", "pool", "sp")

    def __init__(self, nc, ndma=20):
        self.nc = nc
        self.stream = {e: [] for e in self.ENG}
        self.npos = {e: 0 for e in self.ENG}
        self.ops = {e: [] for e in self.ENG}
        self.ndma = ndma
        self.dmacnt = [0] * ndma
        self.dmarr = {"sp": 0, "pool": 0}
        ctrs = list(self.ENG) + [("d", i) for i in range(ndma)]
        self.ctrs = ctrs
        self.seen = {e: {c: 0 for c in ctrs} for e in self.ENG}
        self.lastw = {}
        self.readers = {}

    @staticmethod
    def keys(ap):
        t = ap.tensor
        tn = type(t).__name__
        if tn.startswith("DRam"):
            return ()
        shape = list(t.shape)
        pstride = 1
        for s in shape[1:]:
            pstride *= int(s)
        off = int(ap.offset)
        p0 = off // pstride
        e0 = off % pstride
        dims = [(int(s), int(c)) for s, c in ap.ap]
        pc = dims[0][1]
        ext = 1
        for s, c in dims[1:]:
            ext += (c - 1) * abs(s)
        ds = _DS[ap.dtype]
        if tn.startswith("SB"):
            base = int(t.manual_sbuf_range[0])
            sp = 0
        else:
            base = 0
            sp = 1
        lo = base + e0 * ds
        hi = lo + ext * ds
        q0 = p0 // 32
        q1 = (p0 + pc - 1) // 32
        gr = 2048 if sp else GRAN
        return [(sp, g, q) for g in range(lo // gr, (hi - 1) // gr + 1)
                for q in range(q0, q1 + 1)]

    def _deps(self, rkeys, wkeys, eng=None):
        need = {}
        lastw, readers = self.lastw, self.readers
        for k in rkeys:
            w = lastw.get(k)
            if w is not None and need.get(w[0], 0) < w[1]:
                need[w[0]] = w[1]
            if k[0] == 1:
                r = readers.get(k)
                if r:
                    for c, p in r.items():
                        if c != eng and need.get(c, 0) < p:
                            need[c] = p
        for k in wkeys:
            w = lastw.get(k)
            if w is not None and need.get(w[0], 0) < w[1]:
                need[w[0]] = w[1]
            r = readers.get(k)
            if r:
                for c, p in r.items():
                    if need.get(c, 0) < p:
                        need[c] = p
        return need

    def _waits(self, eng, need):
        seen = self.seen[eng]
        for ctr, pos in need.items():
            if ctr == eng and eng == "pe":
                continue
            if seen[ctr] < pos:
                if not isinstance(ctr, tuple):
                    self.ops[ctr][pos - 1][2] = True
                self.stream[eng].append(["wait", ctr, pos])
                seen[ctr] = pos

    def _record(self, ctr, pos, rkeys, wkeys):
        lastw, readers = self.lastw, self.readers
        for k in rkeys:
            r = readers.get(k)
            if r is None:
                readers[k] = {ctr: pos}
            else:
                r[ctr] = pos
        for k in wkeys:
            lastw[k] = (ctr, pos)
            if k in readers:
                readers[k] = {}

    def op(self, eng, fn, reads, writes):
        rk = [k for a in reads for k in self.keys(a)]
        wk = [k for a in writes for k in self.keys(a)]
        self._waits(eng, self._deps(rk, wk, eng))
        ent = ["op", fn, False]
        self.stream[eng].append(ent)
        self.ops[eng].append(ent)
        self.npos[eng] += 1
        self._record(eng, self.npos[eng], rk, wk)

    def dma(self, queue, out, in_):
        rk = self.keys(in_)
        wk = self.keys(out)
        if SPLIT_DMA_SEMS:
            half = self.ndma // 2
            k = self.dmarr[queue]
            self.dmarr[queue] = (k + 1) % half
            j = k if queue == "sp" else half + k
        else:
            j = self.dmarr["sp"]
            self.dmarr["sp"] = (j + 1) % self.ndma
        n = self.dmacnt[j] + 1
        need = self._deps(rk, wk)
        if n > 1:
            need[("d", j)] = max(need.get(("d", j), 0), n - 1)
        self._waits(queue, need)
        self.stream[queue].append(["dma", (out, in_), j])
        self.dmacnt[j] = n
        self._record(("d", j), n, rk, wk)

    def emit(self):
        nc = self.nc
        for j in range(self.ndma):
            if self.dmacnt[j]:
                self.stream["sp"].append(["wait", ("d", j), self.dmacnt[j]])
        rank = {}
        for e in self.ENG:
            r, rk = 0, []
            for ent in self.ops[e]:
                if ent[2]:
                    r += 1
                rk.append(r)
            rank[e] = rk
        with ExitStack() as st:
            sem = {}
            for c in self.ctrs:
                nm = c if isinstance(c, str) else "dq%d" % c[1]
                sem[c] = st.enter_context(nc.semaphore("s_" + nm))
            block = st.enter_context(nc.Block())

            def mk(e):
                def body(eng):
                    for ent in self.stream[e]:
                        if ent[0] == "wait":
                            c, p = ent[1], ent[2]
                            v = 16 * p if isinstance(c, tuple) else rank[c][p - 1]
                            eng.wait_ge(sem[c], v)
                        elif ent[0] == "op":
                            ins = ent[1](eng)
                            if ent[2]:
                                ins.then_inc(sem[e], 1)
                        else:
                            o, i = ent[1]
                            eng.dma_start(out=o, in_=i).then_inc(sem[("d", ent[2])], 16)
                return body

            block.tensor(mk("pe"))
            block.scalar(mk("act"))
            block.vector(mk("dve"))
            block.gpsimd(mk("pool"))
            block.sync(mk("sp"))


def build_nc(layers=(0, 1), nseq=NSEQ, dbg=None):
    nc = bass.Bass("TRN2", target_bir_lowering=False)
    P = Prog(nc)

    def din(name, shape, dt):
        return nc.dram_tensor(name, list(shape), dt, kind="ExternalInput").ap()

    xT_d = din("xT", [NSEQ, 128, 8, SEQ], F32)
    pos_d = din("pos", [NSEQ, 64, SEQ], I32)
    gains_d = din("gains", [128, 64], F32)
    cst_d = din("cst", [128, 8], F32)
    rm_d = din("rm", [64, 64], BF16)
    dft64_d = din("dft64", [128, 256], BF16)
    c0_d = din("c0", [128, 16, 512], BF16)
    ns0_d = din("ns0", [128, 16, 512], BF16)
    win_d = din("w_in", [DEPTH, 128, 8, 832], F32)
    wf_d = din("w_fourier", [DEPTH, 4, 64, 64], F32)
    wq_d = din("w_q_up", [DEPTH, 128, 2, 1152], F32)
    wkv_d = din("w_kv_up", [DEPTH, 128, 2, 1536], F32)
    wo_d = din("w_out", [DEPTH, 128, 8, 1024], F32)
    w1_d = din("w_mlp_in", [DEPTH, 128, 8, DFF], F32)
    w2_d = din("w_mlp_out", [DEPTH, 128, 32, 1024], F32)
    yT_d = nc.dram_tensor("yT", [NSEQ, 128, 8, SEQ], F32, kind="ExternalOutput").ap()
    dbg_d = {}
    if dbg:
        for nm, shp in dbg.items():
            dbg_d[nm] = nc.dram_tensor("dbg_" + nm, list(shp), F32, kind="ExternalOutput").ap()

    cur = [16512]

    def sb(name, shape, dt, at=None):
        n = _DS[dt]
        for s in shape[1:]:
            n *= s
        if at is None:
            at = cur[0]
            cur[0] = (at + n + 31) // 32 * 32
        return nc.alloc_sbuf_tensor_at(name, list(shape), dt, offset=at)

    xT = sb("xT", [128, 8, SEQ], F32)
    cos2 = sb("cos2", [128, SEQ], BF16)
    sin2 = sb("sin2", [128, SEQ], BF16)
    ones = sb("ones", [128, 128], BF16)
    G = sb("gains", [128, 64], F32)
    CST = sb("cst", [128, 8], F32)
    RM = sb("rm", [128, 128], BF16)
    WFBD = sb("wfbd", [128, 2, 128], BF16)
    RHS1 = sb("rhs1", [128, 2, 256], BF16)
    DFT64 = sb("dft64", [128, 256], BF16)
    A0 = cur[0]
    R0, R1, R2, R3 = A0, A0 + 32768, A0 + 65536, A0 + 98304
    assert R3 + 32768 + 3072 <= 229344, R3

    WIN = sb("win", [128, 8, 832], BF16, at=R2 + 8192)
    XN = sb("xn", [128, 2, 8, TB], BF16, at=R0 + 13312)
    ZF = sb("zf", [128, 2, SEQ], BF16, at=R1)
    CQN = sb("cqn", [128, 2, SEQ], BF16, at=R1 + 8192)
    CKVN = sb("ckvn", [128, 2, SEQ], BF16, at=R1 + 16384)
    KPE = sb("kpe", [128, SEQ], F32, at=R1 + 24576)
    YFN = sb("yfn", [128, 2, SEQ], BF16, at=R2)
    SQ_A = sb("sqA", [128, 2, TB], BF16, at=R2 + 24576)
    RST_A = sb("rstA", [128, 2, TB], F32, at=R2 + 24576 + 2048)
    TMP_A = sb("tmpA", [128, 2, TB], BF16, at=R2 + 24576 + 2048 + 4096)
    SQ_B = sb("sqB", [128, 2, TB], BF16, at=R1 + 6144)
    RST_B = sb("rstB", [128, 2, TB], F32, at=R3 + 18944)
    TMP_B = sb("tmpB", [128, 3, TB], BF16, at=R3 + 18944 + 4096)
    TMPF_B = sb("tmpfB", [128, TB], F32, at=R3 + 18944 + 4096 + 3072)
    QRAW = sb("qraw", [128, 2, TB], F32, at=R3 + 18944 + 4096 + 3072 + 2048)
    SQX = sb("sqx", [128, 8, TB], BF16, at=R2)
    RAW4 = sb("raw4", [128, 4, TB], F32, at=R0)
    SQC = sb("sqc", [128, 4, TB], BF16, at=R0 + 8192)
    RSTC1 = sb("rstc1", [128, TB], F32, at=R2 + 24576 + 2048 + 4096)
    SQY = sb("sqy", [128, NH, TB], BF16, at=R1)
    SQM = sb("sqm", [128, 8, TB], BF16, at=R1 + 24576)
    W1X = sb("w1x", [128, 8, 512], BF16, at=R3)
    W2X = sb("w2x", [128, 4, 1024], BF16, at=R3 + 8192)
    PQ0 = sb("pq0", [128, 16, 512], BF16, at=R0)
    PQT = sb("pqt", [128, 16, 512], BF16, at=R0 + 16384)
    T1 = sb("t1", [128, 16, 512], BF16, at=R2 + 8192)
    C0 = sb("c0", [128, 16, 512], BF16, at=R3)
    NS0 = sb("ns0", [128, 16, 512], BF16, at=R3 + 16384)
    QN = [sb("qn%d" % i, [128, SEQ], BF16, at=R0 + 16384 * i) for i in range(2)]
    QR = [sb("qr%d" % i, [128, SEQ], BF16, at=R0 + 16384 * i + 4096) for i in range(2)]
    KN = [sb("kn%d" % i, [128, SEQ], BF16, at=R0 + 16384 * i + 8192) for i in range(2)]
    VH = [sb("vh%d" % i, [128, 16, 128], BF16, at=R0 + 16384 * i + 12288) for i in range(2)]
    KR = [sb("kr%d" % i, [128, SEQ], BF16, at=R1 + 24576 + 4096 * i) for i in range(2)]
    PT = sb("pt", [128, 3, 2 * TB], BF16, at=R1)
    YA = sb("ya", [128, NH, SEQ], BF16, at=R2 + 8192)
    WQ = sb("wq", [128, 2, 1152], BF16, at=R3)
    WKV = sb("wkv", [128, 2, 1536], BF16, at=R3 + 4608)
    KPR = sb("kpr", [128, SEQ], BF16, at=R3 + 10752)
    SQPE = sb("sqpe", [128, SEQ], BF16, at=R3 + 14848)
    TK = sb("tk", [128, SEQ], BF16, at=R2 + 8192 + 5 * 4096)
    QRAW2 = sb("qraw2", [128, TB], F32, at=R2 + 8192 + 5 * 4096)
    SQ2 = sb("sq2", [128, TB], BF16, at=R2 + 8192 + 5 * 4096 + 2048)
    WO = sb("wo", [128, 8, 1024], BF16, at=R1 + 8192)
    XNA = sb("xna", [128, 8, SEQ], BF16, at=R0)
    W1B = sb("w1b", [128, 2, 8, 512], BF16, at=R1)
    W2B = sb("w2b", [128, 2, 4, 1024], BF16, at=R1 + 16384)
    HB = sb("hb", [128, 2, 4, TB], BF16, at=R2)
    POSI = sb("posi", [128, SEQ], I32, at=R0)
    ANG = sb("ang", [128, SEQ], F32, at=R0 + 8192)
    KQI = sb("kqi", [128, SEQ], I32, at=R0 + 16384)
    KQF = sb("kqf", [128, SEQ], F32, at=R0 + 24576)
    RR = sb("rr", [128, SEQ], F32, at=R1)
    MM_ = sb("mmk", [128, SEQ], F32, at=R1 + 8192)

    WQR = sb("wqr", [128, 2, NH, 128], BF16, at=R3 + 32768)
    PS = nc.alloc_psum_tensor("ps", [128, 4096], F32)

    def bank(i, p=128, n=TB):
        return PS[0:p, i * 512:i * 512 + n]

    rr = [0]

    def nb():
        i = rr[0]
        rr[0] = (i + 1) % 8
        return i

    def mm(out, lhsT, rhs, start, stop):
        P.op("pe", lambda e: e.matmul(out, lhsT, rhs, start=start, stop=stop),
             [lhsT, rhs], [out])

    def act(out, in_, func, scale=1.0, bias=None):
        rd = [in_]
        if bias is not None and not isinstance(bias, float):
            rd.append(bias)
        if not isinstance(scale, float):
            rd.append(scale)
        if bias is None:
            P.op("act", lambda e: e.activation(out, in_, func, scale=scale), rd, [out])
        else:
            P.op("act", lambda e: e.activation(out, in_, func, bias=bias, scale=scale), rd, [out])

    def tt(out, in0, in1, op, eng="dve"):
        P.op(eng, lambda e: e.tensor_tensor(out, in0, in1, op), [in0, in1], [out])

    def ts(out, in0, s1, s2, op0, op1=None, eng="dve"):
        rd = [in0] + [s for s in (s1, s2) if s is not None and not isinstance(s, float)]
        if op1 is None:
            P.op(eng, lambda e: e.tensor_scalar(out, in0, s1, None, op0), rd, [out])
        else:
            P.op(eng, lambda e: e.tensor_scalar(out, in0, s1, s2, op0, op1), rd, [out])

    def stt(out, in0, scalar, in1, op0, op1):
        rd = [in0, in1] + ([] if isinstance(scalar, float) else [scalar])
        P.op("dve", lambda e: e.scalar_tensor_tensor(out, in0, scalar, in1, op0, op1), rd, [out])

    def cp(out, in_, eng="dve"):
        P.op(eng, lambda e: e.tensor_copy(out, in_), [in_], [out])

    def recip(out, in_):
        P.op("dve", lambda e: e.reciprocal(out, in_), [in_], [out])

    def memset(ap, v, eng="dve"):
        P.op(eng, lambda e: e.memset(ap, v), [], [ap])

    def dump(nm, ap_sb, dst):
        if nm in dbg_d:
            P.dma("sp", dst, ap_sb)


    def rstd_from(ssb, dd, rst):
        act(rst, ssb, AF.Ln, scale=1.0 / dd, bias=CST[:, 5:6])
        act(rst, rst, AF.Exp, scale=-0.5)

    memset(ones[:, :], 1.0)
    P.dma("sp", G[:, :], gains_d)
    P.dma("sp", CST[:, :], cst_d)
    memset(RM[:, :], 0.0)
    P.dma("sp", RM[0:64, 0:64], rm_d)
    memset(cos2[:, :], 0.0)
    memset(sin2[:, :], 0.0)
    memset(WQR[:, :, :, :], 0.0)
    P.dma("sp", DFT64[:, :], dft64_d)

    def g_col(l, c, p=128):
        return G[0:p, l * 32 + c:l * 32 + c + 1]

    pairs = [(s_, l_) for s_ in range(nseq) for l_ in layers]
    P.dma("pool", WIN[:, :, :], win_d[layers[0]])
    stored = [False]
    for s in range(nseq):
        stored[0] = False
        for tb in range(NTB):
            for c in range(8):
                P.dma("sp", xT[:, c, tb * TB:(tb + 1) * TB], xT_d[s, :, c, tb * TB:(tb + 1) * TB])
        P.dma("sp", POSI[0:64, :], pos_d[s])
        cp(ANG[0:64, :], POSI[0:64, :])
        ts(ANG[0:64, :], ANG[0:64, :], CST[0:64, 0:1], None, ALU.mult)
        ts(KQI[0:64, :], ANG[0:64, :], 1.0 / (2 * math.pi), None, ALU.mult)
        cp(KQF[0:64, :], KQI[0:64, :])
        C1 = 6.28125
        C2 = 2 * math.pi - C1
        stt(RR[0:64, :], KQF[0:64, :], -C1, ANG[0:64, :], ALU.mult, ALU.add)
        stt(RR[0:64, :], KQF[0:64, :], -C2, RR[0:64, :], ALU.mult, ALU.add)
        ts(MM_[0:64, :], RR[0:64, :], math.pi, -2 * math.pi, ALU.is_gt, ALU.mult)
        tt(RR[0:64, :], RR[0:64, :], MM_[0:64, :], ALU.add)
        ts(MM_[0:64, :], RR[0:64, :], -math.pi, 2 * math.pi, ALU.is_lt, ALU.mult)
        tt(RR[0:64, :], RR[0:64, :], MM_[0:64, :], ALU.add)
        PI_S = 3.1415925
        ts(RR[0:64, :], RR[0:64, :], -PI_S, PI_S, ALU.max, ALU.min)
        act(sin2[0:64, :], RR[0:64, :], AF.Sin)
        act(MM_[0:64, :], RR[0:64, :], AF.Abs)
        act(cos2[0:64, :], MM_[0:64, :], AF.Sin, scale=-1.0, bias=CST[0:64, 6:7])
        if s == 0 and "cos" in dbg_d:
            cp(ANG[0:64, :], cos2[0:64, :])
            P.dma("sp", dbg_d["cos"], ANG[0:64, :])
            cp(KQF[0:64, :], sin2[0:64, :])
            P.dma("sp", dbg_d["sin"], KQF[0:64, :])

        for l in layers:
          try:
            memset(KPE[64:128, :], 0.0, eng="pool")
            memset(WFBD[:, :, :], 0.0, eng="pool")
            for g in range(4):
                pr, hf = g // 2, g % 2
                P.dma("pool", WFBD[64 * hf:64 * hf + 64, pr, 64 * hf:64 * hf + 64], wf_d[l, g])
            for pr in range(2):
                b = nb()
                mm(bank(b, 128, 128), DFT64[:, 0:128], WFBD[:, pr, :], True, True)
                mm(PS[:, b * 512 + 128:b * 512 + 256], DFT64[:, 128:256], WFBD[:, pr, :], True, True)
                act(RHS1[:, pr, :], PS[:, b * 512:b * 512 + 256], AF.Copy)

            def xn_sq(tb):
                tsl = slice(tb * TB, (tb + 1) * TB)
                for c in range(8):
                    act(SQX[:, c, :], xT[:, c, tsl], AF.Square)

            def xn_red(tb):
                ssb = nb()
                for c in range(8):
                    mm(bank(ssb), ones[:, :], SQX[:, c, :], c == 0, c == 7)
                rstd_from(bank(ssb), 1024.0, RST_A[:, 0, :])

            def xn_apply(tb, c0, c1):
                tsl = slice(tb * TB, (tb + 1) * TB)
                for c in range(c0, c1):
                    stt(XN[:, tb % 2, c, :], xT[:, c, tsl], g_col(l, c), RST_A[:, 0, :], ALU.mult, ALU.mult)

            def xn_fin(tb):
                xn_red(tb)
                xn_apply(tb, 0, 8)

            def chain_fin(tb):
                tsl = slice(tb * TB, (tb + 1) * TB)
                for (dst, j0, gc, rst) in ((CQN, 0, 16, RST_A[:, 1, :]), (CKVN, 2, 18, RSTC1[:, :])):
                    sb2 = nb()
                    for j in range(2):
                        mm(bank(sb2), ones[:, :], SQC[:, j0 + j, :], j == 0, j == 1)
                    rstd_from(bank(sb2), 256.0, rst)
                    for j in range(2):
                        stt(dst[:, j, tsl], RAW4[:, j0 + j, :], g_col(l, gc + j), rst,
                            ALU.mult, ALU.mult)

            def p1_group(tb, gi):
                tsl = slice(tb * TB, (tb + 1) * TB)
                xb = tb % 2
                b = nb()
                m = 128 if gi < 6 else 64
                for kc in range(8):
                    mm(bank(b, m), WIN[:, kc, gi * 128:gi * 128 + m], XN[:, xb, kc, :], kc == 0, kc == 7)
                if gi < 2:
                    act(ZF[:, gi, tsl], bank(b), AF.Copy)
                elif gi < 6:
                    cp(RAW4[:, gi - 2, :], bank(b))
                    act(SQC[:, gi - 2, :], RAW4[:, gi - 2, :], AF.Square)
                else:
                    act(KPE[0:64, tsl], bank(b, 64), AF.Copy)

            xn_sq(0)
            xn_fin(0)
            for tb in range(NTB):
                if tb + 1 < NTB:
                    xn_sq(tb + 1)
                p1_group(tb, 0)
                p1_group(tb, 1)
                if tb > 0:
                    chain_fin(tb - 1)
                p1_group(tb, 2)
                p1_group(tb, 3)
                if tb + 1 < NTB:
                    xn_red(tb + 1)
                p1_group(tb, 4)
                if tb + 1 < NTB:
                    xn_apply(tb + 1, 0, 4)
                p1_group(tb, 5)
                if tb + 1 < NTB:
                    xn_apply(tb + 1, 4, 8)
                p1_group(tb, 6)
            chain_fin(NTB - 1)
            if s == 0 and l == layers[0] and "cqn" in dbg_d:
                for j in range(2):
                    cp(ANG[:, :], CQN[:, j, :])
                    P.dma("sp", dbg_d["cqn"][j], ANG[:, :])
                    cp(ANG[:, :], ZF[:, j, :])
                    P.dma("sp", dbg_d["zf"][j], ANG[:, :])

            P.dma("sp", C0[:, :, :], c0_d)
            P.dma("sp", NS0[:, :, :], ns0_d)
            for c in range(16):
                b = nb()
                for pr in range(2):
                    mm(PS[:, b * 512 + 256 * pr:b * 512 + 256 * pr + 256],
                       ZF[:, pr, c * 128:(c + 1) * 128], RHS1[:, pr, :], True, True)
                if c % 2 == 0:
                    act(PQ0[:, c, :], bank(b), AF.Copy)
                else:
                    cp(PQ0[:, c, :], bank(b))

            def pq_view(t, half):
                return t[:, :, :].rearrange("p c (r h e) -> p (c r) h e", r=2, h=2)[:, :, half, :]

            sgn2, c1c, s1c, ns1c = CST[:, 1:2], CST[:, 2:3], CST[:, 3:4], CST[:, 4:5]
            for j in (0, 2, 1, 3):
                if j == 0:
                    src = PQ0
                elif j == 2:
                    ts(PQT[:, :, :], PQ0[:, :, :], sgn2, None, ALU.mult)
                    src = PQT
                else:
                    if j == 1:
                        ts(T1[:, :, :], PQ0[:, :, :], c1c, None, ALU.mult)
                    sa, sb_ = (ns1c, s1c) if j == 1 else (s1c, ns1c)
                    stt(pq_view(PQT, 0), pq_view(PQ0, 1), sa, pq_view(T1, 0), ALU.mult, ALU.add)
                    stt(pq_view(PQT, 1), pq_view(PQ0, 0), sb_, pq_view(T1, 1), ALU.mult, ALU.add)
                    src = PQT
                tsl = slice(j * TB, (j + 1) * TB)
                fb = []
                for fc in range(2):
                    b = nb()
                    fb.append(b)
                    for c in range(16):
                        mm(bank(b), src[:, c, 256 * fc:256 * fc + 128], C0[:, c, :], c == 0, False)
                        mm(bank(b), src[:, c, 256 * fc + 128:256 * fc + 256], NS0[:, c, :], False, c == 15)
                sb2 = nb()
                for fc in range(2):
                    act(SQ_A[:, fc, :], bank(fb[fc]), AF.Square)
                    mm(bank(sb2), ones[:, :], SQ_A[:, fc, :], fc == 0, fc == 1)
                rstd_from(bank(sb2), 256.0, RST_A[:, 0, :])
                for fc in range(2):
                    stt(YFN[:, fc, tsl], bank(fb[fc]), g_col(l, 24 + fc), RST_A[:, 0, :], ALU.mult, ALU.mult)

            P.dma("pool", WQ[:, :, :], wq_d[l])
            P.dma("pool", WKV[:, :, :], wkv_d[l])
            P.dma("pool", WQR[:, :, :, 0:64],
                  wq_d[l].rearrange("p k (h e) -> p k h e", h=NH)[:, :, :, 128:192])
            ts(TK[:, :], KPE[:, :], g_col(l, 23), None, ALU.mult)
            for tb in range(NTB):
                tsl = slice(tb * TB, (tb + 1) * TB)
                act(SQPE[:, tsl], KPE[:, tsl], AF.Square)
                b = nb()
                mm(bank(b), RM[:, :], TK[:, tsl], True, True)
                tt(TMP_B[:, 0, :], TK[:, tsl], cos2[:, tsl], ALU.mult)
                tt(TMP_B[:, 1, :], bank(b), sin2[:, tsl], ALU.mult)
                tt(KPR[:, tsl], TMP_B[:, 0, :], TMP_B[:, 1, :], ALU.add)

            scale = 1.0 / math.sqrt(192.0)
            pb = [0]

            def nbp():
                if not NBP:
                    return nb()
                pb[0] = (pb[0] + 1) % len(NBP_POOL)
                return NBP_POOL[pb[0]]

            def proj_gen(h):
                hp = h % 2
                for tb in range(NTB):
                    tsl = slice(tb * TB, (tb + 1) * TB)
                    qa, qb = nbp(), nbp()
                    for kc in range(2):
                        mm(bank(qa), WQ[:, kc, h * 192:h * 192 + 128], CQN[:, kc, tsl], kc == 0, kc == 1)
                    for kc in range(2):
                        mm(bank(qb), WQR[:, kc, h, :], CQN[:, kc, tsl], kc == 0, kc == 1)
                    act(SQ_B[:, 0, :], bank(qa), AF.Square)
                    act(SQ_B[:, 1, :], bank(qb), AF.Square)
                    cp(QRAW[:, 0, :], bank(qa))
                    cp(QRAW[:, 1, :], bank(qb))
                    yield
                    ka = nbp()
                    for kc in range(2):
                        mm(bank(ka), WKV[:, kc, h * 256:h * 256 + 128], CKVN[:, kc, tsl], kc == 0, kc == 1)
                    act(SQ2[:, :], bank(ka), AF.Square)
                    cp(QRAW2[:, :], bank(ka))
                    yield
                    ssb = nbp()
                    mm(bank(ssb), ones[:, :], SQ_B[:, 0, :], True, False)
                    mm(bank(ssb), ones[:, :], SQ_B[:, 1, :], False, True)
                    rstd_from(bank(ssb), 192.0, RST_B[:, 0, :])
                    stt(TMP_B[:, 0, :], QRAW[:, 1, :], g_col(l, 21), RST_B[:, 0, :], ALU.mult, ALU.mult)
                    stt(QN[hp][:, tsl], QRAW[:, 0, :], g_col(l, 20), RST_B[:, 0, :], ALU.mult, ALU.mult)
                    yield
                    ssk = nbp()
                    mm(bank(ssk), ones[:, :], SQ2[:, :], True, False)
                    mm(bank(ssk), ones[:, :], SQPE[:, tsl], False, True)
                    rstd_from(bank(ssk), 192.0, RST_B[:, 1, :])
                    stt(KN[hp][:, tsl], QRAW2[:, :], g_col(l, 22), RST_B[:, 1, :], ALU.mult, ALU.mult)
                    tt(KR[hp][:, tsl], KPR[:, tsl], RST_B[:, 1, :], ALU.mult)
                    yield
                    swb = nbp()
                    mm(bank(swb), RM[:, :], TMP_B[:, 0, :], True, True)
                    tt(TMP_B[:, 1, :], TMP_B[:, 0, :], cos2[:, tsl], ALU.mult)
                    tt(TMP_B[:, 2, :], bank(swb), sin2[:, tsl], ALU.mult)
                    tt(QR[hp][:, tsl], TMP_B[:, 1, :], TMP_B[:, 2, :], ALU.add)
                    yield
                    vb = nbp()
                    for ci in range(4):
                        c = tb * 4 + ci
                        for kc in range(2):
                            mm(PS[:, vb * 512 + ci * 128:vb * 512 + ci * 128 + 128],
                               CKVN[:, kc, c * 128:(c + 1) * 128],
                               WKV[:, kc, h * 256 + 128:h * 256 + 256], kc == 0, kc == 1)
                    cp(VH[hp][:, tb * 4:tb * 4 + 4, :], bank(vb))
                    yield

            if STOP == "p2":
                raise _Stop()
            if STOP and STOP.startswith("st"):
                for i_, _ in enumerate(proj_gen(0)):
                    if i_ + 1 >= int(STOP[2:]):
                        break
                raise _Stop()
            for _ in proj_gen(0):
                pass
            if STOP == "proj0":
                raise _Stop()
            for h in range(NH):
                hp = h % 2
                gen = proj_gen(h + 1) if h + 1 < NH else None
                if gen is not None and not PIPE:
                    for _ in gen:
                        pass
                    gen = None
                for qb_ in range(NTB):
                    qsl = slice(qb_ * TB, (qb_ + 1) * TB)
                    OB, DB = 4, 5

                    def qk(g):
                        for ci in range(2):
                            c = 2 * g + ci
                            bk = (g % 2) * 2 + ci
                            mm(bank(bk), KN[hp][:, c * 128:(c + 1) * 128], QN[hp][:, qsl], True, False)
                            mm(bank(bk), KR[hp][:, c * 128:(c + 1) * 128], QR[hp][:, qsl], False, True)

                    def ex(g):
                        bk = (g % 2) * 2
                        ptb = g % 3
                        act(PT[:, ptb, :], PS[:, bk * 512:bk * 512 + 1024], AF.Exp, scale=scale)

                    def pv(g):
                        ptb = g % 3
                        for ci in range(2):
                            c = 2 * g + ci
                            mm(bank(OB), VH[hp][:, c, :], PT[:, ptb, ci * TB:(ci + 1) * TB], c == 0, c == 15)
                            mm(bank(DB), ones[:, :], PT[:, ptb, ci * TB:(ci + 1) * TB], c == 0, c == 15)

                    qk(0)
                    for g in range(8):
                        if g + 1 < 8:
                            qk(g + 1)
                        ex(g)
                        if gen is not None:
                            next(gen, None)
                        pv(g)
                    act(TMPF_B[:, :], bank(DB), AF.Ln)
                    act(TMPF_B[:, :], TMPF_B[:, :], AF.Exp, scale=-1.0)
                    tt(YA[:, h, qsl], bank(OB), TMPF_B[:, :], ALU.mult)
                if gen is not None:
                    for _ in gen:
                        pass
                if STOP == "attn0":
                    raise _Stop()
            if STOP == "p3":
                raise _Stop()

            P.dma("pool", WO[:, :, :], wo_d[l])
            def ya_sq(tb):
                tsl = slice(tb * TB, (tb + 1) * TB)
                for h in range(NH):
                    act(SQY[:, h, :], YA[:, h, tsl], AF.Square)

            def ya_red(tb):
                ssb = nb()
                for h in range(NH):
                    mm(bank(ssb), ones[:, :], SQY[:, h, :], h == 0, h == NH - 1)
                rstd_from(bank(ssb), 768.0, RST_B[:, 0, :])

            def ya_apply(tb, h0, h1):
                tsl = slice(tb * TB, (tb + 1) * TB)
                for h in range(h0, h1):
                    stt(YA[:, h, tsl], YA[:, h, tsl], g_col(l, 26 + h), RST_B[:, 0, :], ALU.mult, ALU.mult)

            def ya_fin(tb):
                ya_red(tb)
                ya_apply(tb, 0, NH)

            def mlp_sq(tb):
                tsl = slice(tb * TB, (tb + 1) * TB)
                for c in range(8):
                    act(SQM[:, c, :], xT[:, c, tsl], AF.Square)

            def mlp_red(tb):
                ssb = nb()
                for c in range(8):
                    mm(bank(ssb), ones[:, :], SQM[:, c, :], c == 0, c == 7)
                rstd_from(bank(ssb), 1024.0, RST_B[:, 1, :])

            def mlp_apply(tb, c0, c1):
                tsl = slice(tb * TB, (tb + 1) * TB)
                for c in range(c0, c1):
                    stt(XNA[:, c, tsl], xT[:, c, tsl], g_col(l, 8 + c), RST_B[:, 1, :], ALU.mult, ALU.mult)

            def mlp_fin(tb):
                mlp_red(tb)
                mlp_apply(tb, 0, 8)

            def wout_group(tb, o):
                tsl = slice(tb * TB, (tb + 1) * TB)
                b = nb()
                for kc in range(8):
                    rhs = YFN[:, kc, tsl] if kc < 2 else YA[:, kc - 2, tsl]
                    mm(bank(b), WO[:, kc, o * 128:(o + 1) * 128], rhs, kc == 0, kc == 7)
                tt(xT[:, o, tsl], bank(b), xT[:, o, tsl], ALU.add)

            def load_w(e):
                if e == 0:
                    P.dma("pool", W1X[:, :, :], w1_d[l, :, :, 0:512])
                    P.dma("pool", W2X[:, :, :], w2_d[l, :, 0:4, :])
                else:
                    P.dma("pool", W1B[:, e % 2, :, :], w1_d[l, :, :, e * 512:(e + 1) * 512])
                    P.dma("pool", W2B[:, e % 2, :, :], w2_d[l, :, e * 4:(e + 1) * 4, :])

            load_w(0)

            ya_sq(0)
            ya_fin(0)
            for tb in range(NTB):
                if tb + 1 < NTB:
                    ya_sq(tb + 1)
                for o in range(4):
                    wout_group(tb, o)
                if tb > 0:
                    mlp_red(tb - 1)
                if tb + 1 < NTB:
                    ya_red(tb + 1)
                wout_group(tb, 4)
                if tb + 1 < NTB:
                    ya_apply(tb + 1, 0, 3)
                wout_group(tb, 5)
                if tb + 1 < NTB:
                    ya_apply(tb + 1, 3, NH)
                wout_group(tb, 6)
                if tb > 0:
                    mlp_apply(tb - 1, 0, 4)
                wout_group(tb, 7)
                if tb > 0:
                    mlp_apply(tb - 1, 4, 8)
                mlp_sq(tb)
            mlp_fin(NTB - 1)

            ip = pairs.index((s, l))
            if ip + 1 < len(pairs):
                P.dma("pool", WIN[:, :, :], win_d[pairs[ip + 1][1]])

            def up(e, tb, hb):
                tsl = slice(tb * TB, (tb + 1) * TB)
                for jj in range(4):
                    b = nb()
                    for kc in range(8):
                        w1 = W1X[:, kc, jj * 128:(jj + 1) * 128] if e == 0 else W1B[:, e % 2, kc, jj * 128:(jj + 1) * 128]
                        mm(bank(b), w1, XNA[:, kc, tsl], kc == 0, kc == 7)
                    act(SQ_A[:, jj % 2, :], bank(b), AF.Square)
                    stt(HB[:, hb, jj, :], bank(b), 0.0, SQ_A[:, jj % 2, :], ALU.is_gt, ALU.mult)

            def down(e, tb, hb):
                tsl = slice(tb * TB, (tb + 1) * TB)
                for o in range(8):
                    b = nb()
                    for jj in range(4):
                        w2 = W2X[:, jj, o * 128:(o + 1) * 128] if e == 0 else W2B[:, e % 2, jj, o * 128:(o + 1) * 128]
                        mm(bank(b), w2, HB[:, hb, jj, :], jj == 0, jj == 3)
                    tt(xT[:, o, tsl], bank(b), xT[:, o, tsl], ALU.add)

            steps = [(e, tb) for e in range(8) for tb in range(NTB)]
            up(0, 0, 0)
            for i, (e, tb) in enumerate(steps):
                if tb == 0 and e + 1 < 8:
                    load_w(e + 1)
                if i + 1 < len(steps):
                    e2, tb2 = steps[i + 1]
                    up(e2, tb2, (i + 1) % 2)
                down(e, tb, i % 2)
                if e == 7 and l == layers[-1] and STOP is None:
                    for c in range(8):
                        P.dma("sp", yT_d[s, :, c, tb * TB:(tb + 1) * TB], xT[:, c, tb * TB:(tb + 1) * TB])
                    stored[0] = True

          except _Stop:
            pass
        if not stored[0]:
            for c in range(8):
                P.dma("sp", yT_d[s, :, c, :], xT[:, c, :])

    P.emit()
    return nc


def _pmajor(w, kchunks):
    K, N = w.shape
    return np.ascontiguousarray(w.reshape(kchunks, 128, N).transpose(1, 0, 2))


def _consts():
    bf = ml_dtypes.bfloat16
    half = 32
    inv_freq = (10000.0 ** (-np.arange(half, dtype=np.float32) / half)).astype(np.float32)
    cst = np.zeros((128, 8), np.float32)
    p = np.arange(128)
    cst[:64, 0] = inv_freq[p[:64] % 32]
    cst[:, 1] = np.where(p % 2 == 0, 1.0, -1.0)
    cst[:, 2] = np.array([1.0, 0.0, -1.0, 0.0])[p % 4]
    cst[:, 3] = np.array([0.0, 1.0, 0.0, -1.0])[p % 4]
    cst[:, 4] = -cst[:, 3]
    cst[:, 5] = EPS
    cst[:, 6] = 1.5707963
    rm = np.zeros((64, 64), np.float32)
    for m in range(32):
        rm[m + 32, m] = -1.0
        rm[m, m + 32] = 1.0
    k = np.arange(64)
    a64 = 2 * np.pi * np.outer(k, k) / 64.0
    cc = np.cos(a64) / 8.0
    sc = np.sin(a64) / 8.0
    dft64 = np.zeros((128, 256), np.float64)
    dft64[0:64, 0:64] = cc
    dft64[64:128, 64:128] = cc
    dft64[0:64, 128:192] = sc
    dft64[64:128, 192:256] = sc
    sidx = np.arange(SEQ, dtype=np.int64)
    r = np.arange(512, dtype=np.int64)
    ph = (np.outer(sidx, r) % SEQ).astype(np.float64) * (2 * np.pi / SEQ)
    c0 = np.cos(ph) / math.sqrt(SEQ)
    ns0 = -np.sin(ph) / math.sqrt(SEQ)
    c0 = np.ascontiguousarray(c0.reshape(16, 128, 512).transpose(1, 0, 2))
    ns0 = np.ascontiguousarray(ns0.reshape(16, 128, 512).transpose(1, 0, 2))
    return dict(cst=cst, rm=rm.astype(bf), dft64=dft64.astype(np.float32).astype(bf),
                c0=c0.astype(np.float32).astype(bf), ns0=ns0.astype(np.float32).astype(bf))


def _gains(inp):
    g = np.zeros((128, 64), np.float32)

    def cols(v):
        return np.asarray(v, np.float32).reshape(-1, 128).T

    for l in range(DEPTH):
        o = l * 32
        g[:, o + 0:o + 8] = cols(inp["attn_norm_g"][l])
        g[:, o + 8:o + 16] = cols(inp["mlp_norm_g"][l])
        g[:, o + 16:o + 18] = cols(inp["q_a_g"][l])
        g[:, o + 18:o + 20] = cols(inp["kv_a_g"][l])
        g[:, o + 20] = inp["q_norm_g"][l][:128]
        g[:64, o + 21] = inp["q_norm_g"][l][128:]
        g[:, o + 22] = inp["k_norm_g"][l][:128]
        g[:64, o + 23] = inp["k_norm_g"][l][128:]
        g[:, o + 24:o + 26] = cols(inp["fourier_out_g"][l])
        g[:, o + 26:o + 32] = cols(inp["attn_out_g"][l])
    return g


def _shared_inputs(inp):
    f = lambda a: np.asarray(a, np.float32)
    sh = dict(_consts())
    sh["gains"] = _gains({k: np.asarray(v) for k, v in inp.items()})
    sh["w_in"] = np.stack([_pmajor(f(inp["w_in"][l]), 8) for l in range(DEPTH)])
    sh["w_fourier"] = np.ascontiguousarray(f(inp["w_fourier"]))
    sh["w_q_up"] = np.stack([_pmajor(f(inp["w_q_up"][l]), 2) for l in range(DEPTH)])
    sh["w_kv_up"] = np.stack([_pmajor(f(inp["w_kv_up"][l]), 2) for l in range(DEPTH)])
    sh["w_out"] = np.stack([_pmajor(f(inp["w_out"][l]), 8) for l in range(DEPTH)])
    sh["w_mlp_in"] = np.stack([_pmajor(f(inp["w_mlp_in"][l]), 8) for l in range(DEPTH)])
    sh["w_mlp_out"] = np.stack([_pmajor(f(inp["w_mlp_out"][l]), 32) for l in range(DEPTH)])
    return sh


def _x_to_dev(x):
    xt = np.asarray(x, np.float32).transpose(0, 2, 1).reshape(BATCH, 8, 128, SEQ).transpose(0, 2, 1, 3)
    return [np.ascontiguousarray(xt[i * NSEQ:(i + 1) * NSEQ]) for i in range(NCORES)]


def _y_from_dev(ys):
    yt = np.concatenate(ys, axis=0)
    return np.ascontiguousarray(yt.transpose(0, 2, 1, 3).reshape(BATCH, D, SEQ).transpose(0, 2, 1))


_NC_CACHE = {}


def kernel(**inputs):
    sh = _shared_inputs(inputs)
    pos = np.asarray(inputs["positions"], np.int32)
    posr = np.ascontiguousarray(np.broadcast_to(pos[:, None, :], (BATCH, 64, SEQ)))
    xs = _x_to_dev(inputs["x"])
    if "full" not in _NC_CACHE:
        _NC_CACHE["full"] = build_nc(layers=(0, 1))
    nc = _NC_CACHE["full"]
    in_maps = []
    for i in range(NCORES):
        m = dict(sh)
        m["xT"] = xs[i]
        m["pos"] = np.ascontiguousarray(posr[i * NSEQ:(i + 1) * NSEQ])
        in_maps.append(m)
    res = run_bass_kernel_spmd(nc, in_maps, core_ids=list(range(NCORES)))
    return _y_from_dev([np.asarray(r["yT"]) for r in res.results])
```

```python
import math
from bisect import bisect_left
from contextlib import ExitStack

import numpy as np
import ml_dtypes

import concourse.bass as bass
import concourse.mybir as mybir
from concourse.bass_utils import run_bass_kernel_spmd

F32 = mybir.dt.float32
BF16 = mybir.dt.bfloat16
I32 = mybir.dt.int32
ALU = mybir.AluOpType
AF = mybir.ActivationFunctionType

NCORES = 8
BATCH, SEQ, D = 16, 2048, 1024
DEPTH = 2
NSEQ = BATCH // NCORES
NH = 6
DFF = 4096
EPS = 1e-6
TB = 512
NTB = SEQ // TB
GRAN = 256
PIPE = True
P1PIPE = True
P4PIPE = True
HOIST = True
NBP = True
SKIPB = False
SPLIT_DMA_SEMS = True
VACT = False
NOV = False
STOP = None


class _Stop(Exception):
    pass
NBP_POOL = (6, 7)
_DS = {F32: 4, BF16: 2, I32: 4}


class Prog:
    ENG = ("pe", "act", "dve", "pool", "sp")

    def __init__(self, nc, ndma=20):
        self.nc = nc
        self.stream = {e: [] for e in self.ENG}
        self.npos = {e: 0 for e in self.ENG}
        self.ops = {e: [] for e in self.ENG}
        self.ndma = ndma
        self.dmacnt = [0] * ndma
        self.dmarr = {"sp": 0, "pool": 0}
        ctrs = list(self.ENG) + [("d", i) for i in range(ndma)]
        self.ctrs = ctrs
        self.seen = {e: {c: 0 for c in ctrs} for e in self.ENG}
        self.lastw = {}
        self.readers = {}

    @staticmethod
    def keys(ap):
        t = ap.tensor
        tn = type(t).__name__
        if tn.startswith("DRam"):
            return ()
        shape = list(t.shape)
        pstride = 1
        for s in shape[1:]:
            pstride *= int(s)
        off = int(ap.offset)
        p0 = off // pstride
        e0 = off % pstride
        dims = [(int(s), int(c)) for s, c in ap.ap]
        pc = dims[0][1]
        ext = 1
        for s, c in dims[1:]:
            ext += (c - 1) * abs(s)
        ds = _DS[ap.dtype]
        if tn.startswith("SB"):
            base = int(t.manual_sbuf_range[0])
            sp = 0
        else:
            base = 0
            sp = 1
        lo = base + e0 * ds
        hi = lo + ext * ds
        q0 = p0 // 32
        q1 = (p0 + pc - 1) // 32
        gr = 2048 if sp else GRAN
        return [(sp, g, q) for g in range(lo // gr, (hi - 1) // gr + 1)
                for q in range(q0, q1 + 1)]

    def _deps(self, rkeys, wkeys, eng=None):
        need = {}
        lastw, readers = self.lastw, self.readers
        for k in rkeys:
            w = lastw.get(k)
            if w is not None and need.get(w[0], 0) < w[1]:
                need[w[0]] = w[1]
            if k[0] == 1:
                r = readers.get(k)
                if r:
                    for c, p in r.items():
                        if c != eng and need.get(c, 0) < p:
                            need[c] = p
        for k in wkeys:
            w = lastw.get(k)
            if w is not None and need.get(w[0], 0) < w[1]:
                need[w[0]] = w[1]
            r = readers.get(k)
            if r:
                for c, p in r.items():
                    if need.get(c, 0) < p:
                        need[c] = p
        return need

    def _waits(self, eng, need):
        seen = self.seen[eng]
        for ctr, pos in need.items():
            if ctr == eng and eng == "pe":
                continue
            if seen[ctr] < pos:
                if not isinstance(ctr, tuple):
                    self.ops[ctr][pos - 1][2] = True
                self.stream[eng].append(["wait", ctr, pos])
                seen[ctr] = pos

    def _record(self, ctr, pos, rkeys, wkeys):
        lastw, readers = self.lastw, self.readers
        for k in rkeys:
            r = readers.get(k)
            if r is None:
                readers[k] = {ctr: pos}
            else:
                r[ctr] = pos
        for k in wkeys:
            lastw[k] = (ctr, pos)
            if k in readers:
                readers[k] = {}

    def op(self, eng, fn, reads, writes):
        rk = [k for a in reads for k in self.keys(a)]
        wk = [k for a in writes for k in self.keys(a)]
        self._waits(eng, self._deps(rk, wk, eng))
        ent = ["op", fn, False]
        self.stream[eng].append(ent)
        self.ops[eng].append(ent)
        self.npos[eng] += 1
        self._record(eng, self.npos[eng], rk, wk)

    def dma(self, queue, out, in_):
        rk = self.keys(in_)
        wk = self.keys(out)
        if SPLIT_DMA_SEMS:
            half = self.ndma // 2
            k = self.dmarr[queue]
            self.dmarr[queue] = (k + 1) % half
            j = k if queue == "sp" else half + k
        else:
            j = self.dmarr["sp"]
            self.dmarr["sp"] = (j + 1) % self.ndma
        n = self.dmacnt[j] + 1
        need = self._deps(rk, wk)
        if n > 1:
            need[("d", j)] = max(need.get(("d", j), 0), n - 1)
        self._waits(queue, need)
        self.stream[queue].append(["dma", (out, in_), j])
        self.dmacnt[j] = n
        self._record(("d", j), n, rk, wk)

    def emit(self):
        nc = self.nc
        for j in range(self.ndma):
            if self.dmacnt[j]:
                self.stream["sp"].append(["wait", ("d", j), self.dmacnt[j]])
        rank = {}
        for e in self.ENG:
            r, rk = 0, []
            for ent in self.ops[e]:
                if ent[2]:
                    r += 1
                rk.append(r)
            rank[e] = rk
        with ExitStack() as st:
            sem = {}
            for c in self.ctrs:
                nm = c if isinstance(c, str) else "dq%d" % c[1]
                sem[c] = st.enter_context(nc.semaphore("s_" + nm))
            block = st.enter_context(nc.Block())

            def mk(e):
                def body(eng):
                    for ent in self.stream[e]:
                        if ent[0] == "wait":
                            c, p = ent[1], ent[2]
                            v = 16 * p if isinstance(c, tuple) else rank[c][p - 1]
                            eng.wait_ge(sem[c], v)
                        elif ent[0] == "op":
                            ins = ent[1](eng)
                            if ent[2]:
                                ins.then_inc(sem[e], 1)
                        else:
                            o, i = ent[1]
                            eng.dma_start(out=o, in_=i).then_inc(sem[("d", ent[2])], 16)
                return body

            block.tensor(mk("pe"))
            block.scalar(mk("act"))
            block.vector(mk("dve"))
            block.gpsimd(mk("pool"))
            block.sync(mk("sp"))


def build_nc(layers=(0, 1), nseq=NSEQ, dbg=None):
    nc = bass.Bass("TRN2", target_bir_lowering=False)
    P = Prog(nc)

    def din(name, shape, dt):
        return nc.dram_tensor(name, list(shape), dt, kind="ExternalInput").ap()

    xT_d = din("xT", [NSEQ, 128, 8, SEQ], F32)
    pos_d = din("pos", [NSEQ, 64, SEQ], I32)
    gains_d = din("gains", [128, 64], F32)
    cst_d = din("cst", [128, 8], F32)
    rm_d = din("rm", [64, 64], BF16)
    dft64_d = din("dft64", [128, 256], BF16)
    c0_d = din("c0", [128, 16, 512], BF16)
    ns0_d = din("ns0", [128, 16, 512], BF16)
    win_d = din("w_in", [DEPTH, 128, 8, 832], F32)
    wf_d = din("w_fourier", [DEPTH, 4, 64, 64], F32)
    wq_d = din("w_q_up", [DEPTH, 128, 2, 1152], F32)
    wkv_d = din("w_kv_up", [DEPTH, 128, 2, 1536], F32)
    wo_d = din("w_out", [DEPTH, 128, 8, 1024], F32)
    w1_d = din("w_mlp_in", [DEPTH, 128, 8, DFF], F32)
    w2_d = din("w_mlp_out", [DEPTH, 128, 32, 1024], F32)
    yT_d = nc.dram_tensor("yT", [NSEQ, 128, 8, SEQ], F32, kind="ExternalOutput").ap()
    dbg_d = {}
    if dbg:
        for nm, shp in dbg.items():
            dbg_d[nm] = nc.dram_tensor("dbg_" + nm, list(shp), F32, kind="ExternalOutput").ap()

    cur = [16512]

    def sb(name, shape, dt, at=None):
        n = _DS[dt]
        for s in shape[1:]:
            n *= s
        if at is None:
            at = cur[0]
            cur[0] = (at + n + 31) // 32 * 32
        return nc.alloc_sbuf_tensor_at(name, list(shape), dt, offset=at)

    xT = sb("xT", [128, 8, SEQ], F32)
    cos2 = sb("cos2", [128, SEQ], BF16)
    sin2 = sb("sin2", [128, SEQ], BF16)
    ones = sb("ones", [128, 128], BF16)
    G = sb("gains", [128, 64], F32)
    CST = sb("cst", [128, 8], F32)
    RM = sb("rm", [128, 128], BF16)
    WFBD = sb("wfbd", [128, 2, 128], BF16)
    RHS1 = sb("rhs1", [128, 2, 256], BF16)
    DFT64 = sb("dft64", [128, 256], BF16)
    A0 = cur[0]
    R0, R1, R2, R3 = A0, A0 + 32768, A0 + 65536, A0 + 98304
    assert R3 + 32768 + 3072 <= 229344, R3

    WIN = sb("win", [128, 8, 832], BF16, at=R2 + 8192)
    XN = sb("xn", [128, 2, 8, TB], BF16, at=R0 + 13312)
    ZF = sb("zf", [128, 2, SEQ], BF16, at=R1)
    CQN = sb("cqn", [128, 2, SEQ], BF16, at=R1 + 8192)
    CKVN = sb("ckvn", [128, 2, SEQ], BF16, at=R1 + 16384)
    KPE = sb("kpe", [128, SEQ], F32, at=R1 + 24576)
    YFN = sb("yfn", [128, 2, SEQ], BF16, at=R2)
    SQ_A = sb("sqA", [128, 2, TB], BF16, at=R2 + 24576)
    RST_A = sb("rstA", [128, 2, TB], F32, at=R2 + 24576 + 2048)
    TMP_A = sb("tmpA", [128, 2, TB], BF16, at=R2 + 24576 + 2048 + 4096)
    SQ_B = sb("sqB", [128, 2, TB], BF16, at=R1 + 6144)
    RST_B = sb("rstB", [128, 2, TB], F32, at=R3 + 18944)
    TMP_B = sb("tmpB", [128, 3, TB], BF16, at=R3 + 18944 + 4096)
    TMPF_B = sb("tmpfB", [128, TB], F32, at=R3 + 18944 + 4096 + 3072)
    QRAW = sb("qraw", [128, 2, TB], F32, at=R3 + 18944 + 4096 + 3072 + 2048)
    SQX = sb("sqx", [128, 8, TB], BF16, at=R2)
    RAW4 = sb("raw4", [128, 4, TB], F32, at=R0)
    SQC = sb("sqc", [128, 4, TB], BF16, at=R0 + 8192)
    RSTC1 = sb("rstc1", [128, TB], F32, at=R2 + 24576 + 2048 + 4096)
    SQY = sb("sqy", [128, NH, TB], BF16, at=R1)
    SQM = sb("sqm", [128, 8, TB], BF16, at=R1 + 24576)
    W1X = sb("w1x", [128, 8, 512], BF16, at=R3)
    W2X = sb("w2x", [128, 4, 1024], BF16, at=R3 + 8192)
    PQ0 = sb("pq0", [128, 16, 512], BF16, at=R0)
    PQT = sb("pqt", [128, 16, 512], BF16, at=R0 + 16384)
    T1 = sb("t1", [128, 16, 512], BF16, at=R2 + 8192)
    C0 = sb("c0", [128, 16, 512], BF16, at=R3)
    NS0 = sb("ns0", [128, 16, 512], BF16, at=R3 + 16384)
    QN = [sb("qn%d" % i, [128, SEQ], BF16, at=R0 + 16384 * i) for i in range(2)]
    QR = [sb("qr%d" % i, [128, SEQ], BF16, at=R0 + 16384 * i + 4096) for i in range(2)]
    KN = [sb("kn%d" % i, [128, SEQ], BF16, at=R0 + 16384 * i + 8192) for i in range(2)]
    VH = [sb("vh%d" % i, [128, 16, 128], BF16, at=R0 + 16384 * i + 12288) for i in range(2)]
    KR = [sb("kr%d" % i, [128, SEQ], BF16, at=R1 + 24576 + 4096 * i) for i in range(2)]
    PT = sb("pt", [128, 3, 2 * TB], BF16, at=R1)
    YA = sb("ya", [128, NH, SEQ], BF16, at=R2 + 8192)
    WQ = sb("wq", [128, 2, 1152], BF16, at=R3)
    WKV = sb("wkv", [128, 2, 1536], BF16, at=R3 + 4608)
    KPR = sb("kpr", [128, SEQ], BF16, at=R3 + 10752)
    SQPE = sb("sqpe", [128, SEQ], BF16, at=R3 + 14848)
    TK = sb("tk", [128, SEQ], BF16, at=R2 + 8192 + 5 * 4096)
    QRAW2 = sb("qraw2", [128, TB], F32, at=R2 + 8192 + 5 * 4096)
    SQ2 = sb("sq2", [128, TB], BF16, at=R2 + 8192 + 5 * 4096 + 2048)
    WO = sb("wo", [128, 8, 1024], BF16, at=R1 + 8192)
    XNA = sb("xna", [128, 8, SEQ], BF16, at=R0)
    W1B = sb("w1b", [128, 2, 8, 512], BF16, at=R1)
    W2B = sb("w2b", [128, 2, 4, 1024], BF16, at=R1 + 16384)
    HB = sb("hb", [128, 2, 4, TB], BF16, at=R2)
    POSI = sb("posi", [128, SEQ], I32, at=R0)
    ANG = sb("ang", [128, SEQ], F32, at=R0 + 8192)
    KQI = sb("kqi", [128, SEQ], I32, at=R0 + 16384)
    KQF = sb("kqf", [128, SEQ], F32, at=R0 + 24576)
    RR = sb("rr", [128, SEQ], F32, at=R1)
    MM_ = sb("mmk", [128, SEQ], F32, at=R1 + 8192)

    WQR = sb("wqr", [128, 2, NH, 128], BF16, at=R3 + 32768)
    PS = nc.alloc_psum_tensor("ps", [128, 4096], F32)

    def bank(i, p=128, n=TB):
        return PS[0:p, i * 512:i * 512 + n]

    rr = [0]

    def nb():
        i = rr[0]
        rr[0] = (i + 1) % 8
        return i

    def mm(out, lhsT, rhs, start, stop):
        P.op("pe", lambda e: e.matmul(out, lhsT, rhs, start=start, stop=stop),
             [lhsT, rhs], [out])

    def act(out, in_, func, scale=1.0, bias=None):
        rd = [in_]
        if bias is not None and not isinstance(bias, float):
            rd.append(bias)
        if not isinstance(scale, float):
            rd.append(scale)
        if bias is None:
            P.op("act", lambda e: e.activation(out, in_, func, scale=scale), rd, [out])
        else:
            P.op("act", lambda e: e.activation(out, in_, func, bias=bias, scale=scale), rd, [out])

    def tt(out, in0, in1, op, eng="dve"):
        P.op(eng, lambda e: e.tensor_tensor(out, in0, in1, op), [in0, in1], [out])

    def ts(out, in0, s1, s2, op0, op1=None, eng="dve"):
        rd = [in0] + [s for s in (s1, s2) if s is not None and not isinstance(s, float)]
        if op1 is None:
            P.op(eng, lambda e: e.tensor_scalar(out, in0, s1, None, op0), rd, [out])
        else:
            P.op(eng, lambda e: e.tensor_scalar(out, in0, s1, s2, op0, op1), rd, [out])

    def stt(out, in0, scalar, in1, op0, op1):
        rd = [in0, in1] + ([] if isinstance(scalar, float) else [scalar])
        P.op("dve", lambda e: e.scalar_tensor_tensor(out, in0, scalar, in1, op0, op1), rd, [out])

    def cp(out, in_, eng="dve"):
        P.op(eng, lambda e: e.tensor_copy(out, in_), [in_], [out])

    def recip(out, in_):
        P.op("dve", lambda e: e.reciprocal(out, in_), [in_], [out])

    def memset(ap, v, eng="dve"):
        P.op(eng, lambda e: e.memset(ap, v), [], [ap])

    def dump(nm, ap_sb, dst):
        if nm in dbg_d:
            P.dma("sp", dst, ap_sb)


    def rstd_from(ssb, dd, rst):
        act(rst, ssb, AF.Ln, scale=1.0 / dd, bias=CST[:, 5:6])
        act(rst, rst, AF.Exp, scale=-0.5)

    memset(ones[:, :], 1.0)
    P.dma("sp", G[:, :], gains_d)
    P.dma("sp", CST[:, :], cst_d)
    memset(RM[:, :], 0.0)
    P.dma("sp", RM[0:64, 0:64], rm_d)
    memset(cos2[:, :], 0.0)
    memset(sin2[:, :], 0.0)
    memset(WQR[:, :, :, :], 0.0)
    P.dma("sp", DFT64[:, :], dft64_d)

    def g_col(l, c, p=128):
        return G[0:p, l * 32 + c:l * 32 + c + 1]

    pairs = [(s_, l_) for s_ in range(nseq) for l_ in layers]
    P.dma("pool", WIN[:, :, :], win_d[layers[0]])
    stored = [False]
    for s in range(nseq):
        stored[0] = False
        P.dma("sp", POSI[0:64, :], pos_d[s])
        for tb in range(NTB):
            for c in range(8):
                P.dma("sp", xT[:, c, tb * TB:(tb + 1) * TB], xT_d[s, :, c, tb * TB:(tb + 1) * TB])
        cp(ANG[0:64, :], POSI[0:64, :])
        ts(ANG[0:64, :], ANG[0:64, :], CST[0:64, 0:1], None, ALU.mult)
        ts(KQI[0:64, :], ANG[0:64, :], 1.0 / (2 * math.pi), None, ALU.mult)
        cp(KQF[0:64, :], KQI[0:64, :])
        C1 = 6.28125
        C2 = 2 * math.pi - C1
        stt(RR[0:64, :], KQF[0:64, :], -C1, ANG[0:64, :], ALU.mult, ALU.add)
        stt(RR[0:64, :], KQF[0:64, :], -C2, RR[0:64, :], ALU.mult, ALU.add)
        ts(MM_[0:64, :], RR[0:64, :], math.pi, -2 * math.pi, ALU.is_gt, ALU.mult)
        tt(RR[0:64, :], RR[0:64, :], MM_[0:64, :], ALU.add)
        ts(MM_[0:64, :], RR[0:64, :], -math.pi, 2 * math.pi, ALU.is_lt, ALU.mult)
        tt(RR[0:64, :], RR[0:64, :], MM_[0:64, :], ALU.add)
        PI_S = 3.1415925
        ts(RR[0:64, :], RR[0:64, :], -PI_S, PI_S, ALU.max, ALU.min)
        act(sin2[0:64, :], RR[0:64, :], AF.Sin)
        act(MM_[0:64, :], RR[0:64, :], AF.Abs)
        act(cos2[0:64, :], MM_[0:64, :], AF.Sin, scale=-1.0, bias=CST[0:64, 6:7])
        if s == 0 and "cos" in dbg_d:
            cp(ANG[0:64, :], cos2[0:64, :])
            P.dma("sp", dbg_d["cos"], ANG[0:64, :])
            cp(KQF[0:64, :], sin2[0:64, :])
            P.dma("sp", dbg_d["sin"], KQF[0:64, :])

        for l in layers:
          try:
            memset(KPE[64:128, :], 0.0, eng="pool")
            memset(WFBD[:, :, :], 0.0, eng="pool")
            for g in range(4):
                pr, hf = g // 2, g % 2
                P.dma("pool", WFBD[64 * hf:64 * hf + 64, pr, 64 * hf:64 * hf + 64], wf_d[l, g])
            for pr in range(2):
                b = nb()
                mm(bank(b, 128, 128), DFT64[:, 0:128], WFBD[:, pr, :], True, True)
                mm(PS[:, b * 512 + 128:b * 512 + 256], DFT64[:, 128:256], WFBD[:, pr, :], True, True)
                act(RHS1[:, pr, :], PS[:, b * 512:b * 512 + 256], AF.Copy)

            def xn_sq(tb):
                tsl = slice(tb * TB, (tb + 1) * TB)
                for c in range(8):
                    act(SQX[:, c, :], xT[:, c, tsl], AF.Square)

            def xn_red(tb):
                ssb = nb()
                for c in range(8):
                    mm(bank(ssb), ones[:, :], SQX[:, c, :], c == 0, c == 7)
                rstd_from(bank(ssb), 1024.0, RST_A[:, 0, :])

            def xn_apply(tb, c0, c1):
                tsl = slice(tb * TB, (tb + 1) * TB)
                for c in range(c0, c1):
                    stt(XN[:, tb % 2, c, :], xT[:, c, tsl], g_col(l, c), RST_A[:, 0, :], ALU.mult, ALU.mult)

            def xn_fin(tb):
                xn_red(tb)
                xn_apply(tb, 0, 8)

            def chain_fin(tb):
                tsl = slice(tb * TB, (tb + 1) * TB)
                for (dst, j0, gc, rst) in ((CQN, 0, 16, RST_A[:, 1, :]), (CKVN, 2, 18, RSTC1[:, :])):
                    sb2 = nb()
                    for j in range(2):
                        mm(bank(sb2), ones[:, :], SQC[:, j0 + j, :], j == 0, j == 1)
                    rstd_from(bank(sb2), 256.0, rst)
                    for j in range(2):
                        stt(dst[:, j, tsl], RAW4[:, j0 + j, :], g_col(l, gc + j), rst,
                            ALU.mult, ALU.mult)

            def p1_group(tb, gi):
                tsl = slice(tb * TB, (tb + 1) * TB)
                xb = tb % 2
                b = nb()
                m = 128 if gi < 6 else 64
                for kc in range(8):
                    mm(bank(b, m), WIN[:, kc, gi * 128:gi * 128 + m], XN[:, xb, kc, :], kc == 0, kc == 7)
                if gi < 2:
                    act(ZF[:, gi, tsl], bank(b), AF.Copy)
                elif gi < 6:
                    cp(RAW4[:, gi - 2, :], bank(b))
                    act(SQC[:, gi - 2, :], RAW4[:, gi - 2, :], AF.Square)
                else:
                    act(KPE[0:64, tsl], bank(b, 64), AF.Copy)

            xn_sq(0)
            xn_fin(0)
            for tb in range(NTB):
                if tb + 1 < NTB:
                    xn_sq(tb + 1)
                p1_group(tb, 0)
                p1_group(tb, 1)
                if tb > 0:
                    chain_fin(tb - 1)
                p1_group(tb, 2)
                p1_group(tb, 3)
                if tb + 1 < NTB:
                    xn_red(tb + 1)
                p1_group(tb, 4)
                if tb + 1 < NTB:
                    xn_apply(tb + 1, 0, 4)
                p1_group(tb, 5)
                if tb + 1 < NTB:
                    xn_apply(tb + 1, 4, 8)
                p1_group(tb, 6)
            chain_fin(NTB - 1)
            if s == 0 and l == layers[0] and "cqn" in dbg_d:
                for j in range(2):
                    cp(ANG[:, :], CQN[:, j, :])
                    P.dma("sp", dbg_d["cqn"][j], ANG[:, :])
                    cp(ANG[:, :], ZF[:, j, :])
                    P.dma("sp", dbg_d["zf"][j], ANG[:, :])

            P.dma("sp", C0[:, :, :], c0_d)
            P.dma("sp", NS0[:, :, :], ns0_d)
            for c in range(16):
                b = nb()
                for pr in range(2):
                    mm(PS[:, b * 512 + 256 * pr:b * 512 + 256 * pr + 256],
                       ZF[:, pr, c * 128:(c + 1) * 128], RHS1[:, pr, :], True, True)
                if c % 2 == 0:
                    act(PQ0[:, c, :], bank(b), AF.Copy)
                else:
                    cp(PQ0[:, c, :], bank(b))

            def pq_view(t, half):
                return t[:, :, :].rearrange("p c (r h e) -> p (c r) h e", r=2, h=2)[:, :, half, :]

            sgn2, c1c, s1c, ns1c = CST[:, 1:2], CST[:, 2:3], CST[:, 3:4], CST[:, 4:5]
            for j in (0, 2, 1, 3):
                if j == 0:
                    src = PQ0
                elif j == 2:
                    ts(PQT[:, :, :], PQ0[:, :, :], sgn2, None, ALU.mult)
                    src = PQT
                else:
                    if j == 1:
                        ts(T1[:, :, :], PQ0[:, :, :], c1c, None, ALU.mult)
                    sa, sb_ = (ns1c, s1c) if j == 1 else (s1c, ns1c)
                    stt(pq_view(PQT, 0), pq_view(PQ0, 1), sa, pq_view(T1, 0), ALU.mult, ALU.add)
                    stt(pq_view(PQT, 1), pq_view(PQ0, 0), sb_, pq_view(T1, 1), ALU.mult, ALU.add)
                    src = PQT
                tsl = slice(j * TB, (j + 1) * TB)
                fb = []
                for fc in range(2):
                    b = nb()
                    fb.append(b)
                    for c in range(16):
                        mm(bank(b), src[:, c, 256 * fc:256 * fc + 128], C0[:, c, :], c == 0, False)
                        mm(bank(b), src[:, c, 256 * fc + 128:256 * fc + 256], NS0[:, c, :], False, c == 15)
                sb2 = nb()
                for fc in range(2):
                    act(SQ_A[:, fc, :], bank(fb[fc]), AF.Square)
                    mm(bank(sb2), ones[:, :], SQ_A[:, fc, :], fc == 0, fc == 1)
                rstd_from(bank(sb2), 256.0, RST_A[:, 0, :])
                for fc in range(2):
                    stt(YFN[:, fc, tsl], bank(fb[fc]), g_col(l, 24 + fc), RST_A[:, 0, :], ALU.mult, ALU.mult)

            P.dma("pool", WQ[:, :, :], wq_d[l])
            P.dma("pool", WKV[:, :, :], wkv_d[l])
            P.dma("pool", WQR[:, :, :, 0:64],
                  wq_d[l].rearrange("p k (h e) -> p k h e", h=NH)[:, :, :, 128:192])
            ts(TK[:, :], KPE[:, :], g_col(l, 23), None, ALU.mult)
            for tb in range(NTB):
                tsl = slice(tb * TB, (tb + 1) * TB)
                act(SQPE[:, tsl], KPE[:, tsl], AF.Square)
                b = nb()
                mm(bank(b), RM[:, :], TK[:, tsl], True, True)
                tt(TMP_B[:, 0, :], TK[:, tsl], cos2[:, tsl], ALU.mult)
                tt(TMP_B[:, 1, :], bank(b), sin2[:, tsl], ALU.mult)
                tt(KPR[:, tsl], TMP_B[:, 0, :], TMP_B[:, 1, :], ALU.add)

            scale = 1.0 / math.sqrt(192.0)
            pb = [0]

            def nbp():
                if not NBP:
                    return nb()
                pb[0] = (pb[0] + 1) % len(NBP_POOL)
                return NBP_POOL[pb[0]]

            def proj_gen(h):
                hp = h % 2
                for tb in range(NTB):
                    tsl = slice(tb * TB, (tb + 1) * TB)
                    qa, qb = nbp(), nbp()
                    for kc in range(2):
                        mm(bank(qa), WQ[:, kc, h * 192:h * 192 + 128], CQN[:, kc, tsl], kc == 0, kc == 1)
                    for kc in range(2):
                        mm(bank(qb), WQR[:, kc, h, :], CQN[:, kc, tsl], kc == 0, kc == 1)
                    act(SQ_B[:, 0, :], bank(qa), AF.Square)
                    act(SQ_B[:, 1, :], bank(qb), AF.Square)
                    cp(QRAW[:, 0, :], bank(qa))
                    cp(QRAW[:, 1, :], bank(qb))
                    yield
                    ka = nbp()
                    for kc in range(2):
                        mm(bank(ka), WKV[:, kc, h * 256:h * 256 + 128], CKVN[:, kc, tsl], kc == 0, kc == 1)
                    act(SQ2[:, :], bank(ka), AF.Square)
                    cp(QRAW2[:, :], bank(ka))
                    yield
                    ssb = nbp()
                    mm(bank(ssb), ones[:, :], SQ_B[:, 0, :], True, False)
                    mm(bank(ssb), ones[:, :], SQ_B[:, 1, :], False, True)
                    rstd_from(bank(ssb), 192.0, RST_B[:, 0, :])
                    stt(TMP_B[:, 0, :], QRAW[:, 1, :], g_col(l, 21), RST_B[:, 0, :], ALU.mult, ALU.mult)
                    stt(QN[hp][:, tsl], QRAW[:, 0, :], g_col(l, 20), RST_B[:, 0, :], ALU.mult, ALU.mult)
                    yield
                    ssk = nbp()
                    mm(bank(ssk), ones[:, :], SQ2[:, :], True, False)
                    mm(bank(ssk), ones[:, :], SQPE[:, tsl], False, True)
                    rstd_from(bank(ssk), 192.0, RST_B[:, 1, :])
                    stt(KN[hp][:, tsl], QRAW2[:, :], g_col(l, 22), RST_B[:, 1, :], ALU.mult, ALU.mult)
                    tt(KR[hp][:, tsl], KPR[:, tsl], RST_B[:, 1, :], ALU.mult)
                    yield
                    swb = nbp()
                    mm(bank(swb), RM[:, :], TMP_B[:, 0, :], True, True)
                    tt(TMP_B[:, 1, :], TMP_B[:, 0, :], cos2[:, tsl], ALU.mult)
                    tt(TMP_B[:, 2, :], bank(swb), sin2[:, tsl], ALU.mult)
                    tt(QR[hp][:, tsl], TMP_B[:, 1, :], TMP_B[:, 2, :], ALU.add)
                    yield
                    vb = nbp()
                    for ci in range(4):
                        c = tb * 4 + ci
                        for kc in range(2):
                            mm(PS[:, vb * 512 + ci * 128:vb * 512 + ci * 128 + 128],
                               CKVN[:, kc, c * 128:(c + 1) * 128],
                               WKV[:, kc, h * 256 + 128:h * 256 + 256], kc == 0, kc == 1)
                    cp(VH[hp][:, tb * 4:tb * 4 + 4, :], bank(vb))
                    yield

            if STOP == "p2":
                raise _Stop()
            if STOP and STOP.startswith("st"):
                for i_, _ in enumerate(proj_gen(0)):
                    if i_ + 1 >= int(STOP[2:]):
                        break
                raise _Stop()
            for _ in proj_gen(0):
                pass
            if STOP == "proj0":
                raise _Stop()
            for h in range(NH):
                hp = h % 2
                gen = proj_gen(h + 1) if h + 1 < NH else None
                if gen is not None and not PIPE:
                    for _ in gen:
                        pass
                    gen = None
                for qb_ in range(NTB):
                    qsl = slice(qb_ * TB, (qb_ + 1) * TB)
                    OB, DB = 4, 5

                    def qk(g):
                        for ci in range(2):
                            c = 2 * g + ci
                            bk = (g % 2) * 2 + ci
                            mm(bank(bk), KN[hp][:, c * 128:(c + 1) * 128], QN[hp][:, qsl], True, False)
                            mm(bank(bk), KR[hp][:, c * 128:(c + 1) * 128], QR[hp][:, qsl], False, True)

                    def ex(g):
                        bk = (g % 2) * 2
                        ptb = g % 3
                        act(PT[:, ptb, :], PS[:, bk * 512:bk * 512 + 1024], AF.Exp, scale=scale)

                    def pv(g):
                        ptb = g % 3
                        for ci in range(2):
                            c = 2 * g + ci
                            mm(bank(OB), VH[hp][:, c, :], PT[:, ptb, ci * TB:(ci + 1) * TB], c == 0, c == 15)
                            mm(bank(DB), ones[:, :], PT[:, ptb, ci * TB:(ci + 1) * TB], c == 0, c == 15)

                    qk(0)
                    for g in range(8):
                        if g + 1 < 8:
                            qk(g + 1)
                        ex(g)
                        if gen is not None:
                            next(gen, None)
                        pv(g)
                    act(TMPF_B[:, :], bank(DB), AF.Ln)
                    act(TMPF_B[:, :], TMPF_B[:, :], AF.Exp, scale=-1.0)
                    tt(YA[:, h, qsl], bank(OB), TMPF_B[:, :], ALU.mult)
                if gen is not None:
                    for _ in gen:
                        pass
                if STOP == "attn0":
                    raise _Stop()
            if STOP == "p3":
                raise _Stop()

            P.dma("pool", WO[:, :, :], wo_d[l])
            def ya_sq(tb):
                tsl = slice(tb * TB, (tb + 1) * TB)
                for h in range(NH):
                    act(SQY[:, h, :], YA[:, h, tsl], AF.Square)

            def ya_red(tb):
                ssb = nb()
                for h in range(NH):
                    mm(bank(ssb), ones[:, :], SQY[:, h, :], h == 0, h == NH - 1)
                rstd_from(bank(ssb), 768.0, RST_B[:, 0, :])

            def ya_apply(tb, h0, h1):
                tsl = slice(tb * TB, (tb + 1) * TB)
                for h in range(h0, h1):
                    stt(YA[:, h, tsl], YA[:, h, tsl], g_col(l, 26 + h), RST_B[:, 0, :], ALU.mult, ALU.mult)

            def ya_fin(tb):
                ya_red(tb)
                ya_apply(tb, 0, NH)

            def mlp_sq(tb):
                tsl = slice(tb * TB, (tb + 1) * TB)
                for c in range(8):
                    act(SQM[:, c, :], xT[:, c, tsl], AF.Square)

            def mlp_red(tb):
                ssb = nb()
                for c in range(8):
                    mm(bank(ssb), ones[:, :], SQM[:, c, :], c == 0, c == 7)
                rstd_from(bank(ssb), 1024.0, RST_B[:, 1, :])

            def mlp_apply(tb, c0, c1):
                tsl = slice(tb * TB, (tb + 1) * TB)
                for c in range(c0, c1):
                    stt(XNA[:, c, tsl], xT[:, c, tsl], g_col(l, 8 + c), RST_B[:, 1, :], ALU.mult, ALU.mult)

            def mlp_fin(tb):
                mlp_red(tb)
                mlp_apply(tb, 0, 8)

            def wout_group(tb, o):
                tsl = slice(tb * TB, (tb + 1) * TB)
                b = nb()
                for kc in range(8):
                    rhs = YFN[:, kc, tsl] if kc < 2 else YA[:, kc - 2, tsl]
                    mm(bank(b), WO[:, kc, o * 128:(o + 1) * 128], rhs, kc == 0, kc == 7)
                tt(xT[:, o, tsl], bank(b), xT[:, o, tsl], ALU.add)

            def load_w(e):
                if e == 0:
                    P.dma("pool", W1X[:, :, :], w1_d[l, :, :, 0:512])
                    P.dma("pool", W2X[:, :, :], w2_d[l, :, 0:4, :])
                else:
                    P.dma("pool", W1B[:, e % 2, :, :], w1_d[l, :, :, e * 512:(e + 1) * 512])
                    P.dma("pool", W2B[:, e % 2, :, :], w2_d[l, :, e * 4:(e + 1) * 4, :])

            load_w(0)

            ya_sq(0)
            ya_fin(0)
            for tb in range(NTB):
                if tb + 1 < NTB:
                    ya_sq(tb + 1)
                for o in range(4):
                    wout_group(tb, o)
                if tb > 0:
                    mlp_red(tb - 1)
                if tb + 1 < NTB:
                    ya_red(tb + 1)
                wout_group(tb, 4)
                if tb + 1 < NTB:
                    ya_apply(tb + 1, 0, 3)
                wout_group(tb, 5)
                if tb + 1 < NTB:
                    ya_apply(tb + 1, 3, NH)
                wout_group(tb, 6)
                if tb > 0:
                    mlp_apply(tb - 1, 0, 4)
                wout_group(tb, 7)
                if tb > 0:
                    mlp_apply(tb - 1, 4, 8)
                mlp_sq(tb)
            mlp_fin(NTB - 1)

            ip = pairs.index((s, l))
            if ip + 1 < len(pairs):
                P.dma("pool", WIN[:, :, :], win_d[pairs[ip + 1][1]])

            def up(e, tb, hb):
                tsl = slice(tb * TB, (tb + 1) * TB)
                for jj in range(4):
                    b = nb()
                    for kc in range(8):
                        w1 = W1X[:, kc, jj * 128:(jj + 1) * 128] if e == 0 else W1B[:, e % 2, kc, jj * 128:(jj + 1) * 128]
                        mm(bank(b), w1, XNA[:, kc, tsl], kc == 0, kc == 7)
                    act(SQ_A[:, jj % 2, :], bank(b), AF.Square)
                    stt(HB[:, hb, jj, :], bank(b), 0.0, SQ_A[:, jj % 2, :], ALU.is_gt, ALU.mult)

            def down(e, tb, hb):
                tsl = slice(tb * TB, (tb + 1) * TB)
                for o in range(8):
                    b = nb()
                    for jj in range(4):
                        w2 = W2X[:, jj, o * 128:(o + 1) * 128] if e == 0 else W2B[:, e % 2, jj, o * 128:(o + 1) * 128]
                        mm(bank(b), w2, HB[:, hb, jj, :], jj == 0, jj == 3)
                    tt(xT[:, o, tsl], bank(b), xT[:, o, tsl], ALU.add)

            steps = [(e, tb) for e in range(8) for tb in range(NTB)]
            up(0, 0, 0)
            for i, (e, tb) in enumerate(steps):
                if tb == 0 and e + 1 < 8:
                    load_w(e + 1)
                if i + 1 < len(steps):
                    e2, tb2 = steps[i + 1]
                    up(e2, tb2, (i + 1) % 2)
                down(e, tb, i % 2)
                if e == 7 and l == layers[-1] and STOP is None:
                    for c in range(8):
                        P.dma("sp", yT_d[s, :, c, tb * TB:(tb + 1) * TB], xT[:, c, tb * TB:(tb + 1) * TB])
                    stored[0] = True

          except _Stop:
            pass
        if not stored[0]:
            for c in range(8):
                P.dma("sp", yT_d[s, :, c, :], xT[:, c, :])

    P.emit()
    return nc


def _pmajor(w, kchunks):
    K, N = w.shape
    return np.ascontiguousarray(w.reshape(kchunks, 128, N).transpose(1, 0, 2))


def _consts():
    bf = ml_dtypes.bfloat16
    half = 32
    inv_freq = (10000.0 ** (-np.arange(half, dtype=np.float32) / half)).astype(np.float32)
    cst = np.zeros((128, 8), np.float32)
    p = np.arange(128)
    cst[:64, 0] = inv_freq[p[:64] % 32]
    cst[:, 1] = np.where(p % 2 == 0, 1.0, -1.0)
    cst[:, 2] = np.array([1.0, 0.0, -1.0, 0.0])[p % 4]
    cst[:, 3] = np.array([0.0, 1.0, 0.0, -1.0])[p % 4]
    cst[:, 4] = -cst[:, 3]
    cst[:, 5] = EPS
    cst[:, 6] = 1.5707963
    rm = np.zeros((64, 64), np.float32)
    for m in range(32):
        rm[m + 32, m] = -1.0
        rm[m, m + 32] = 1.0
    k = np.arange(64)
    a64 = 2 * np.pi * np.outer(k, k) / 64.0
    cc = np.cos(a64) / 8.0
    sc = np.sin(a64) / 8.0
    dft64 = np.zeros((128, 256), np.float64)
    dft64[0:64, 0:64] = cc
    dft64[64:128, 64:128] = cc
    dft64[0:64, 128:192] = sc
    dft64[64:128, 192:256] = sc
    sidx = np.arange(SEQ, dtype=np.int64)
    r = np.arange(512, dtype=np.int64)
    ph = (np.outer(sidx, r) % SEQ).astype(np.float64) * (2 * np.pi / SEQ)
    c0 = np.cos(ph) / math.sqrt(SEQ)
    ns0 = -np.sin(ph) / math.sqrt(SEQ)
    c0 = np.ascontiguousarray(c0.reshape(16, 128, 512).transpose(1, 0, 2))
    ns0 = np.ascontiguousarray(ns0.reshape(16, 128, 512).transpose(1, 0, 2))
    return dict(cst=cst, rm=rm.astype(bf), dft64=dft64.astype(np.float32).astype(bf),
                c0=c0.astype(np.float32).astype(bf), ns0=ns0.astype(np.float32).astype(bf))


def _gains(inp):
    g = np.zeros((128, 64), np.float32)

    def cols(v):
        return np.asarray(v, np.float32).reshape(-1, 128).T

    for l in range(DEPTH):
        o = l * 32
        g[:, o + 0:o + 8] = cols(inp["attn_norm_g"][l])
        g[:, o + 8:o + 16] = cols(inp["mlp_norm_g"][l])
        g[:, o + 16:o + 18] = cols(inp["q_a_g"][l])
        g[:, o + 18:o + 20] = cols(inp["kv_a_g"][l])
        g[:, o + 20] = inp["q_norm_g"][l][:128]
        g[:64, o + 21] = inp["q_norm_g"][l][128:]
        g[:, o + 22] = inp["k_norm_g"][l][:128]
        g[:64, o + 23] = inp["k_norm_g"][l][128:]
        g[:, o + 24:o + 26] = cols(inp["fourier_out_g"][l])
        g[:, o + 26:o + 32] = cols(inp["attn_out_g"][l])
    return g


def _shared_inputs(inp):
    f = lambda a: np.asarray(a, np.float32)
    sh = dict(_consts())
    sh["gains"] = _gains({k: np.asarray(v) for k, v in inp.items()})
    sh["w_in"] = np.stack([_pmajor(f(inp["w_in"][l]), 8) for l in range(DEPTH)])
    sh["w_fourier"] = np.ascontiguousarray(f(inp["w_fourier"]))
    sh["w_q_up"] = np.stack([_pmajor(f(inp["w_q_up"][l]), 2) for l in range(DEPTH)])
    sh["w_kv_up"] = np.stack([_pmajor(f(inp["w_kv_up"][l]), 2) for l in range(DEPTH)])
    sh["w_out"] = np.stack([_pmajor(f(inp["w_out"][l]), 8) for l in range(DEPTH)])
    sh["w_mlp_in"] = np.stack([_pmajor(f(inp["w_mlp_in"][l]), 8) for l in range(DEPTH)])
    sh["w_mlp_out"] = np.stack([_pmajor(f(inp["w_mlp_out"][l]), 32) for l in range(DEPTH)])
    return sh


def _x_to_dev(x):
    xt = np.asarray(x, np.float32).transpose(0, 2, 1).reshape(BATCH, 8, 128, SEQ).transpose(0, 2, 1, 3)
    return [np.ascontiguousarray(xt[i * NSEQ:(i + 1) * NSEQ]) for i in range(NCORES)]


def _y_from_dev(ys):
    yt = np.concatenate(ys, axis=0)
    return np.ascontiguousarray(yt.transpose(0, 2, 1, 3).reshape(BATCH, D, SEQ).transpose(0, 2, 1))


_NC_CACHE = {}


def kernel(**inputs):
    sh = _shared_inputs(inputs)
    pos = np.asarray(inputs["positions"], np.int32)
    posr = np.ascontiguousarray(np.broadcast_to(pos[:, None, :], (BATCH, 64, SEQ)))
    xs = _x_to_dev(inputs["x"])
    if "full" not in _NC_CACHE:
        _NC_CACHE["full"] = build_nc(layers=(0, 1))
    nc = _NC_CACHE["full"]
    in_maps = []
    for i in range(NCORES):
        m = dict(sh)
        m["xT"] = xs[i]
        m["pos"] = np.ascontiguousarray(posr[i * NSEQ:(i + 1) * NSEQ])
        in_maps.append(m)
    res = run_bass_kernel_spmd(nc, in_maps, core_ids=list(range(NCORES)))
    return _y_from_dev([np.asarray(r["yT"]) for r in res.results])
```
